# Optimizing a Trainium2 kernel written in Bass

```python
import jax, jax.numpy as jnp
from jax import lax
import numpy as np

D_MODEL = 1024
BATCH = 2
SEQ = 8192
DEPTH = 4

D_MIX = D_MODEL
HEAD_DIM = 64
A_HEADS = 4
A_WIDTH = A_HEADS * HEAD_DIM
A_CHUNK = 128
B_HEADS = 8
B_DK = 64
B_DV = 64
B_WIDTH = B_HEADS * B_DV
B_CONV = 4
B_CHUNK = 64
C_HEADS = 4
C_DK = 32
C_DV = 64
C_KEY = C_HEADS * C_DK
C_WIDTH = C_HEADS * C_DV
C_RANK = 16
C_TAU = 16.0
C_CHUNK = 64
D_FF = 2816
ALPHA = (2.0 * DEPTH) ** 0.25
BETA_INIT = (8.0 * DEPTH) ** -0.25
EPS = 1e-5
SPLITS = (A_WIDTH, A_WIDTH,
          3 * B_WIDTH, B_WIDTH, B_HEADS, B_HEADS,
          C_KEY, C_KEY, C_WIDTH, C_WIDTH, C_RANK)
D_IN = sum(SPLITS)

kernel_name = "hymba_style_sgu_gdn_gla_macaron_deepnorm"


def layer_norm(x, g, b):
    xf = x.astype(jnp.float32)
    mu = jnp.mean(xf, -1, keepdims=True)
    var = jnp.mean(jnp.square(xf - mu), -1, keepdims=True)
    return ((xf - mu) * lax.rsqrt(var + EPS) * g + b).astype(x.dtype)


def rms_norm(x, g):
    xf = x.astype(jnp.float32)
    return (xf * lax.rsqrt(jnp.mean(xf * xf, -1, keepdims=True) + EPS) * g).astype(x.dtype)


def l2_norm(x):
    xf = x.astype(jnp.float32)
    return (xf * lax.rsqrt(jnp.sum(xf * xf, -1, keepdims=True) + 1e-6)).astype(x.dtype)


def swiglu_half(x, w1, w3, w2):
    return 0.5 * ((jax.nn.silu(x @ w1) * (x @ w3)) @ w2)


def to_chunks(x, c):
    b_, t, h, d = x.shape
    return x.reshape(b_, t // c, c, h, d).transpose(0, 3, 1, 2, 4)


def gates_to_chunks(x, c):
    b_, t, h = x.shape
    return x.reshape(b_, t // c, c, h).transpose(0, 3, 1, 2)


def from_chunks(x):
    b_, h, n, c, d = x.shape
    return x.transpose(0, 2, 3, 1, 4).reshape(b_, n * c, h, d)


def causal_short_conv(x, w):
    k = w.shape[0]
    t = x.shape[1]
    xp = jnp.pad(x, ((0, 0), (k - 1, 0), (0, 0)))
    y = w[0] * xp[:, 0:t]
    for i in range(1, k):
        y = y + w[i] * xp[:, i:i + t]
    return y


def spatial_gating(u, v, ln_g, ln_b, w_s, b_s):
    b_, t, _ = v.shape
    n = t // A_CHUNK
    v = layer_norm(v, ln_g, ln_b).reshape(b_, n, A_CHUNK, A_HEADS, HEAD_DIM)
    mask = jnp.tril(jnp.ones((A_CHUNK, A_CHUNK), dtype=bool))
    w = jnp.where(mask, w_s, 0.0)
    z = jnp.einsum('hij,bnjhd->bnihd', w, v) + b_s.T[:, :, None]
    return u * z.reshape(b_, t, A_WIDTH)


def gated_delta_rule(q, k, v, log_g, beta):
    out_dtype = v.dtype
    q, k, v, log_g, beta = (t_.astype(jnp.float32) for t_ in (q, k, v, log_g, beta))
    c = q.shape[3]
    b_, h, _, _, dk = q.shape
    dv = v.shape[-1]
    incl = jnp.tril(jnp.ones((c, c), dtype=bool))
    strict = jnp.tril(jnp.ones((c, c), dtype=bool), -1)
    gam = jnp.cumsum(log_g, axis=-1)
    decay = jnp.exp(jnp.where(incl, gam[..., :, None] - gam[..., None, :], -jnp.inf))
    kk = jnp.einsum('bhnid,bhnjd->bhnij', k, k)
    lower = jnp.where(strict, beta[..., :, None] * kk * decay, 0.0)
    eye = jnp.eye(c, dtype=jnp.float32)
    rhs = jnp.concatenate([beta[..., None] * v, (beta * jnp.exp(gam))[..., None] * k], axis=-1)
    sol = lax.linalg.triangular_solve(eye + lower, rhs, left_side=True, lower=True, unit_diagonal=True)
    u_base, w_state = sol[..., :dv], sol[..., dv:]
    p_intra = jnp.einsum('bhnid,bhnjd->bhnij', q, k) * decay
    q_dec = q * jnp.exp(gam)[..., None]
    k_dec = k * jnp.exp(gam[..., -1:] - gam)[..., None]
    g_tot = jnp.exp(gam[..., -1])

    def step(s, inp):
        u_b, w_c, p_c, qd, kd, gt = inp
        u = u_b - jnp.einsum('bhck,bhkv->bhcv', w_c, s)
        o = jnp.einsum('bhck,bhkv->bhcv', qd, s) + jnp.einsum('bhij,bhjv->bhiv', p_c, u)
        s = gt[..., None, None] * s + jnp.einsum('bhck,bhcv->bhkv', kd, u)
        return s, o

    s0 = jnp.zeros((b_, h, dk, dv), jnp.float32)
    xs = tuple(jnp.moveaxis(t_, 2, 0) for t_ in (u_base, w_state, p_intra, q_dec, k_dec, g_tot))
    _, o = lax.scan(step, s0, xs)
    return jnp.moveaxis(o, 0, 2).astype(out_dtype)


def gated_deltanet(qkv, z, beta_logit, decay_logit, conv_w, a_log, dt_bias, norm_g):
    b_, t, _ = qkv.shape
    qkv = jax.nn.silu(causal_short_conv(qkv, conv_w))
    q, k, v = jnp.split(qkv, 3, axis=-1)
    q = l2_norm(q.reshape(b_, t, B_HEADS, B_DK)) * (B_DK ** -0.5)
    k = l2_norm(k.reshape(b_, t, B_HEADS, B_DK))
    v = v.reshape(b_, t, B_HEADS, B_DV)
    beta = jax.nn.sigmoid(beta_logit.astype(jnp.float32))
    log_g = -jnp.exp(a_log) * jax.nn.softplus(decay_logit.astype(jnp.float32) + dt_bias)
    o = gated_delta_rule(to_chunks(q, B_CHUNK), to_chunks(k, B_CHUNK), to_chunks(v, B_CHUNK),
                         gates_to_chunks(log_g, B_CHUNK), gates_to_chunks(beta, B_CHUNK))
    o = from_chunks(o)
    o = rms_norm(o, norm_g) * jax.nn.silu(z.reshape(b_, t, B_HEADS, B_DV))
    return o.reshape(b_, t, B_WIDTH)


def gla_chunked(q, k, v, log_a):
    out_dtype = v.dtype
    q, k, v, log_a = (t_.astype(jnp.float32) for t_ in (q, k, v, log_a))
    c = q.shape[3]
    b_, h, _, _, dk = q.shape
    dv = v.shape[-1]
    incl = jnp.tril(jnp.ones((c, c), dtype=bool))
    bcum = jnp.cumsum(log_a, axis=3)
    q_dec = q * jnp.exp(bcum)
    p_intra = jnp.where(incl, jnp.einsum('bhnid,bhnjd->bhnij', q_dec, k * jnp.exp(-bcum)), 0.0)
    o_intra = jnp.einsum('bhnij,bhnjv->bhniv', p_intra, v)
    k_dec = k * jnp.exp(bcum[..., -1:, :] - bcum)
    g_tot = jnp.exp(bcum[..., -1, :])

    def step(s, inp):
        o_i, qd, kd, vc, gt = inp
        o = o_i + jnp.einsum('bhck,bhkv->bhcv', qd, s)
        s = gt[..., :, None] * s + jnp.einsum('bhck,bhcv->bhkv', kd, vc)
        return s, o

    s0 = jnp.zeros((b_, h, dk, dv), jnp.float32)
    xs = tuple(jnp.moveaxis(t_, 2, 0) for t_ in (o_intra, q_dec, k_dec, v, g_tot))
    _, o = lax.scan(step, s0, xs)
    return jnp.moveaxis(o, 0, 2).astype(out_dtype)


def gla(q, k, v, r, g_low, gate_up, gate_b, norm_g):
    b_, t, _ = q.shape
    log_a = jax.nn.log_sigmoid((g_low @ gate_up + gate_b).astype(jnp.float32)) / C_TAU
    q = q.reshape(b_, t, C_HEADS, C_DK) * (C_DK ** -0.5)
    k = k.reshape(b_, t, C_HEADS, C_DK)
    v = v.reshape(b_, t, C_HEADS, C_DV)
    log_a = log_a.reshape(b_, t, C_HEADS, C_DK)
    o = gla_chunked(to_chunks(q, C_CHUNK), to_chunks(k, C_CHUNK), to_chunks(v, C_CHUNK),
                    to_chunks(log_a, C_CHUNK))
    o = rms_norm(from_chunks(o), norm_g) * jax.nn.silu(r.reshape(b_, t, C_HEADS, C_DV))
    return o.reshape(b_, t, C_WIDTH)


def setup_inputs(seed: int = 0) -> dict:
    key = jax.random.key(seed)
    ks = jax.random.split(key, 40)
    L = DEPTH
    f32 = jnp.float32

    def nrm(i, shape, scale):
        return jax.random.normal(ks[i], shape, f32) * scale

    def gain(i, shape):
        return 1.0 + 0.1 * jax.random.normal(ks[i], shape, f32)

    a_val = jax.random.uniform(ks[17], (L, B_HEADS), f32, 1.0, 16.0)
    dt = jnp.exp(jax.random.uniform(ks[18], (L, B_HEADS), f32, np.log(1e-3), np.log(1e-1)))
    return {
        "x": jax.random.normal(ks[0], (BATCH, SEQ, D_MODEL), f32),
        "ffn1_w1": nrm(1, (L, D_MODEL, D_FF), D_MODEL ** -0.5),
        "ffn1_w3": nrm(2, (L, D_MODEL, D_FF), D_MODEL ** -0.5),
        "ffn1_w2": nrm(3, (L, D_FF, D_MODEL), BETA_INIT * D_FF ** -0.5),
        "ln1_g": gain(4, (L, D_MODEL)),
        "ln1_b": nrm(5, (L, D_MODEL), 0.02),
        "w_in": nrm(6, (L, D_MODEL, D_IN), D_MODEL ** -0.5),
        "sgu_ln_g": gain(7, (L, A_WIDTH)),
        "sgu_ln_b": nrm(8, (L, A_WIDTH), 0.02),
        "sgu_w": nrm(9, (L, A_HEADS, A_CHUNK, A_CHUNK), 0.5 * A_CHUNK ** -0.5),
        "sgu_b": gain(10, (L, A_HEADS, A_CHUNK)),
        "gdn_conv_w": nrm(11, (L, B_CONV, 3 * B_WIDTH), 0.5),
        "gdn_a_log": jnp.log(a_val),
        "gdn_dt_bias": dt + jnp.log(-jnp.expm1(-dt)),
        "gdn_norm_g": gain(12, (L, B_DV)),
        "gla_gate_up": nrm(13, (L, C_RANK, C_KEY), C_RANK ** -0.5),
        "gla_gate_b": nrm(14, (L, C_KEY), 0.1),
        "gla_norm_g": gain(15, (L, C_DV)),
        "w_out": nrm(16, (L, D_MIX, D_MODEL), BETA_INIT * D_MIX ** -0.5),
        "ln2_g": gain(19, (L, D_MODEL)),
        "ln2_b": nrm(20, (L, D_MODEL), 0.02),
        "ffn2_w1": nrm(21, (L, D_MODEL, D_FF), D_MODEL ** -0.5),
        "ffn2_w3": nrm(22, (L, D_MODEL, D_FF), D_MODEL ** -0.5),
        "ffn2_w2": nrm(23, (L, D_FF, D_MODEL), BETA_INIT * D_FF ** -0.5),
        "ln3_g": gain(24, (L, D_MODEL)),
        "ln3_b": nrm(25, (L, D_MODEL), 0.02),
    }


def reference(x, ffn1_w1, ffn1_w3, ffn1_w2, ln1_g, ln1_b, w_in, sgu_ln_g, sgu_ln_b, sgu_w, sgu_b,
              gdn_conv_w, gdn_a_log, gdn_dt_bias, gdn_norm_g, gla_gate_up, gla_gate_b, gla_norm_g,
              w_out, ln2_g, ln2_b, ffn2_w1, ffn2_w3, ffn2_w2, ln3_g, ln3_b):
    split_at = [int(s) for s in np.cumsum(SPLITS)[:-1]]
    for l in range(DEPTH):
        x = layer_norm(ALPHA * x + swiglu_half(x, ffn1_w1[l], ffn1_w3[l], ffn1_w2[l]), ln1_g[l], ln1_b[l])
        h = x @ w_in[l]
        (a_u, a_v, b_qkv, b_z, b_beta, b_decay,
         c_q, c_k, c_v, c_r, c_g) = jnp.split(h, split_at, axis=-1)
        out_a = spatial_gating(jax.nn.gelu(a_u), jax.nn.gelu(a_v), sgu_ln_g[l], sgu_ln_b[l], sgu_w[l], sgu_b[l])
        out_b = gated_deltanet(b_qkv, b_z, b_beta, b_decay, gdn_conv_w[l], gdn_a_log[l], gdn_dt_bias[l], gdn_norm_g[l])
        out_c = gla(c_q, c_k, c_v, c_r, c_g, gla_gate_up[l], gla_gate_b[l], gla_norm_g[l])
        mix = jnp.concatenate([out_a, out_b, out_c], axis=-1) @ w_out[l]
        x = layer_norm(ALPHA * x + mix, ln2_g[l], ln2_b[l])
        x = layer_norm(ALPHA * x + swiglu_half(x, ffn2_w1[l], ffn2_w3[l], ffn2_w2[l]), ln3_g[l], ln3_b[l])
    return x
```

```python
import numpy as np
from contextlib import ExitStack
import concourse.bass as bass
import concourse.mybir as mybir
from concourse.bass_utils import run_bass_kernel_spmd

F32 = mybir.dt.float32
BF16 = mybir.dt.bfloat16
AF = mybir.ActivationFunctionType
ALU = mybir.AluOpType
AX = mybir.AxisListType

D = 1024
DFF = 2816
NF = DFF // 128
DEPTH = 4
TOK = 2048
NT = TOK // 128
ALPHA = (2.0 * DEPTH) ** 0.25
EPS = 1e-5
DIN = 3360


import os as _os0
SKIP_SAME = tuple(_os0.environ.get("SKIP_SAME", "pe").split(","))


class Buf:
    __slots__ = ("name", "wev", "revs", "dsem")

    def __init__(self, name):
        self.name = name
        self.wev = None
        self.revs = {}
        self.dsem = None


class Sched:
    ENG = ("pe", "act", "dve", "pool", "sp")

    def __init__(self, nc, stack):
        self.nc = nc
        self.stack = stack
        self.prog = {e: [] for e in self.ENG}
        self.esem = {e: stack.enter_context(nc.semaphore("s_" + e)) for e in self.ENG}
        self.cnt = {}
        self.seen = {e: {} for e in self.ENG}
        self.nsem = len(self.ENG)
        self.nbuf = 0
        self.dbg = []

    def buf(self, name=None):
        self.nbuf += 1
        return Buf(name or "b%d" % self.nbuf)

    def bufs(self, n, name="b"):
        return [self.buf("%s%d" % (name, i)) for i in range(n)]

    def sb(self, name, shape, dtype):
        return self.stack.enter_context(self.nc.sbuf_tensor(name, list(shape), dtype))

    def ps(self, name, shape, dtype):
        return self.stack.enter_context(self.nc.psum_tensor(name, list(shape), dtype))

    def op(self, eng, fn, reads=(), writes=(), dma=None, ndma=1):
        waits = {}

        def need(ev):
            if ev is None:
                return
            s, v = ev
            if v > waits.get(s, 0):
                waits[s] = v

        for b in reads:
            need(b.wev)
        for b in writes:
            need(b.wev)
            for ev in b.revs.values():
                need(ev)
        if dma is not None:
            if dma.dsem is None:
                self.nsem += 1
                dma.dsem = self.stack.enter_context(self.nc.semaphore("d%d" % self.nsem))
            sem = dma.dsem
            amt = 16
            total = 16 * ndma
        else:
            sem = self.esem[eng]
            amt = 1
            total = 1
        own = self.esem[eng]
        wl = []
        for s, v in waits.items():
            if s is own and eng in SKIP_SAME:
                continue
            if self.seen[eng].get(s, 0) >= v:
                continue
            self.seen[eng][s] = v
            wl.append((s, v))
        self.cnt[sem] = self.cnt.get(sem, 0) + total
        ev = (sem, self.cnt[sem])
        self.prog[eng].append((wl, fn, sem, amt))
        for b in reads:
            b.revs[sem] = ev
        for b in writes:
            b.wev = ev
            b.revs = {}
        return ev

    def dump(self, name, ap, buf, shape, dtype=F32):
        d = self.nc.dram_tensor("dbg_" + name, list(shape), dtype, kind="ExternalOutput").ap()
        db = self.buf("dbg_" + name)
        self.dbg.append(db)
        self.op("sp", lambda e: e.dma_start(out=d, in_=ap), reads=[buf], writes=[db], dma=buf)

    def finish(self, bufs):
        bufs = list(bufs) + self.dbg
        waits = {}
        for b in bufs:
            for ev in [b.wev] + list(b.revs.values()):
                if ev is not None and ev[1] > waits.get(ev[0], 0):
                    waits[ev[0]] = ev[1]
        self.prog["sp"].append((list(waits.items()), None, None, 0))

    def emit(self):
        prog = self.prog

        def mk(name):
            def f(e):
                for wl, fn, sem, amt in prog[name]:
                    for s, v in wl:
                        e.wait_ge(s, v)
                    if fn is None:
                        continue
                    r = fn(e)
                    if not isinstance(r, (list, tuple)):
                        r = [r]
                    for ins in r:
                        ins.then_inc(sem, amt)
            return f

        with self.nc.Block() as block:
            block.tensor(mk("pe"))
            block.scalar(mk("act"))
            block.vector(mk("dve"))
            block.gpsimd(mk("pool"))
            block.sync(mk("sp"))


def MM(S, out, lhsT, rhs, start, stop, rd, wr):
    S.op("pe", lambda e: e.matmul(out, lhsT=lhsT, rhs=rhs, start=start, stop=stop), reads=rd, writes=wr)


def TR(S, out, in_, ident, rd, wr):
    S.op("pe", lambda e: e.transpose(out=out, in_=in_, identity=ident), reads=rd, writes=wr)


def ACT(S, out, in_, func, rd, wr, scale=None, bias=None, accum=None):
    kw = {}
    if scale is not None:
        kw["scale"] = scale
    if bias is not None:
        kw["bias"] = bias
    if accum is not None:
        kw["accum_out"] = accum
    S.op("act", lambda e: e.activation(out=out, in_=in_, func=func, **kw), reads=rd, writes=wr)


def TT(S, eng, out, in0, in1, op, rd, wr):
    S.op(eng, lambda e: e.tensor_tensor(out=out, in0=in0, in1=in1, op=op), reads=rd, writes=wr)


def TS(S, eng, out, in0, s1, op0, rd, wr, s2=None, op1=None):
    if op1 is None:
        S.op(eng, lambda e: e.tensor_scalar(out=out, in0=in0, scalar1=s1, scalar2=None, op0=op0),
             reads=rd, writes=wr)
    else:
        S.op(eng, lambda e: e.tensor_scalar(out=out, in0=in0, scalar1=s1, scalar2=s2, op0=op0, op1=op1),
             reads=rd, writes=wr)


def STT(S, eng, out, in0, scalar, in1, op0, op1, rd, wr):
    S.op(eng, lambda e: e.scalar_tensor_tensor(out=out, in0=in0, scalar=scalar, in1=in1, op0=op0, op1=op1),
         reads=rd, writes=wr)


def CP(S, eng, out, in_, rd, wr):
    if eng == "act":
        S.op("act", lambda e: e.activation(out=out, in_=in_, func=AF.Copy), reads=rd, writes=wr)
    else:
        S.op(eng, lambda e: e.tensor_copy(out=out, in_=in_), reads=rd, writes=wr)


def RECIP(S, out, in_, rd, wr):
    S.op("dve", lambda e: e.reciprocal(out=out, in_=in_), reads=rd, writes=wr)


def DMA(S, eng, out, in_, rd, wr, dma):
    S.op(eng, lambda e: e.dma_start(out=out, in_=in_), reads=rd, writes=wr, dma=dma)


def MEMSET(S, eng, ap, val, wr):
    S.op(eng, lambda e: e.memset(ap, val), writes=wr)


class Ctx:
    pass


class T:
    def __init__(self, S, name, shape, dtype):
        self.h = S.sb(name, shape, dtype)
        self.b = S.buf(name)

    def __getitem__(self, k):
        return self.h[k]

    @classmethod
    def view(cls, S, name, ap):
        o = cls.__new__(cls)
        o.h = ap
        o.b = S.buf(name)
        return o


MB_LIST = [1, 2, 4, 8, 16, 32, 64]


def make_consts():
    i = np.arange(128)
    cf = {}
    cb = {}
    cb["ident"] = np.eye(128)
    cf["U"] = (i[:, None] <= i[None, :]) * 1.0
    cf["Un16"] = (i[:, None] <= i[None, :]) * (-1.0 / 16.0)
    cf["ones"] = np.ones((128, 128))
    cf["Lstr"] = (i[:, None] > i[None, :]) * 1.0
    for b in MB_LIST:
        bi = i // b
        m = ((bi[:, None] % 2 == 1) & (bi[None, :] == bi[:, None] - 1)) * 1.0
        cb["m%d" % b] = m
        cb["mT%d" % b] = m.T.copy()
    cf["bm2"] = ((i[:, None] // 64) == (i[None, :] // 64)) * 1.0
    cb["bm2"] = cf["bm2"]
    cb["U"] = cf["U"]
    cb["Un16"] = cf["Un16"]
    cb["ones"] = cf["ones"]
    j = np.arange(256)
    cf["bmC"] = ((i[:, None] // 32) == (j[None, :] // 64)) * 1.0
    cf["hm"] = ((i[:, None] // 32) == np.arange(4)[None, :]) * 1.0
    cf["pm"] = ((i[:, None] // 64) == np.arange(2)[None, :]) * 1.0
    cb["cm"] = np.concatenate([np.tile((i[None, :] // 64 == q) * 1.0, (128, 1)) for q in range(2)], axis=1)

    def pack(cols):
        off = {}
        arrs = []
        o = 0
        for k, v in cols.items():
            off[k] = (o, v.shape[1])
            o += v.shape[1]
            arrs.append(v.astype(np.float32))
        return np.concatenate(arrs, axis=1), off

    return pack(cf), pack(cb)


(CSTF_NP, CSTF_OFF), (CSTB_NP, CSTB_OFF) = make_consts()
NCSTF = CSTF_NP.shape[1]
NCSTB = CSTB_NP.shape[1]


def next_ps(cx):
    i = cx.psi
    cx.psi = (cx.psi + 1) % 8
    return cx.ps[i], cx.psb[i]


def CF(cx, name):
    o, n = CSTF_OFF[name]
    return cx.CST[:, o:o + n]


def CB(cx, name):
    o, n = CSTB_OFF[name]
    return cx.CSTB[:, o:o + n]


def bc4(ap, n=4):
    return ap.unsqueeze(1).to_broadcast([128, n, ap.shape[1]])


def bcl(ap, m):
    return ap.unsqueeze(2).to_broadcast([128, ap.shape[1], m])


def alloc_all(S, nc, cx):
    cx.ps = [S.ps("ps%d" % i, [128, 512], F32) for i in range(8)]
    cx.psb = S.bufs(8, "ps")
    cx.psi = 0
    cx.CST = S.sb("CST", [128, NCSTF], F32)
    cx.CSTB = S.sb("CSTB", [128, NCSTB], BF16)
    cx.cstb = S.buf("cst")
    cx.X = S.sb("X", [128, 4, D], F32)
    cx.Xb = S.bufs(4, "X")
    cx.XT = S.sb("XT", [128, 8, 512], BF16)
    cx.XTb = S.bufs(4, "XT")
    cx.MIX = S.sb("MIX", [128, 4, D], BF16)
    cx.MIXb = S.bufs(4, "MIX")
    cx.lng = T(S, "lng", [128, D], F32)
    cx.lnb = T(S, "lnb", [128, D], F32)
    cx.w13 = [T(S, "w13_%d" % i, [128, 2, 8, 128], BF16) for i in range(2)]
    cx.w2t = [T(S, "w2_%d" % i, [128, D], BF16) for i in range(2)]
    cx.sa = [T(S, "sa%d" % i, [128, 512], F32) for i in range(1)] * 2
    cx.gTflat = S.sb("gT", [128, NF * 512], BF16)
    cx.gT = cx.gTflat[:, :].rearrange("p (f t) -> p f t", f=NF)
    cx.gTb = S.bufs(NF, "gT")
    cx.lt = T(S, "lt", [128, D], F32)
    cx.stat = T(S, "stat", [128, 8], F32)
    cx.junk = T(S, "junk", [128, D], BF16)
    cx.xb16 = T(S, "xb16", [128, D], BF16)
    cx.wf = [T(S, "wf%d" % i, [128, 8, 128], BF16) for i in range(3)]
    cx.wt = [T.view(S, "wt%d" % i, cx.gTflat[:, i * 4096:(i + 1) * 4096].rearrange("p (k c) -> p k c", k=8))
             for i in range(2)] + [T(S, "wt2", [128, 8, 512], BF16)]
    cx.wo = [T(S, "wo%d" % i, [128, D], BF16) for i in range(2)]
    cx.sgg = T(S, "sgg", [128, 256], F32)
    cx.sgb = T(S, "sgb", [128, 256], F32)
    cx.sbT = T(S, "sbT", [128, 4], F32)
    cx.WmT = T(S, "WmT", [128, 4, 128], BF16)
    cx.ga = T(S, "ga", [128, 512], F32)
    cx.gt1 = T(S, "gt1", [128, 512], F32)
    cx.wstg = T.view(S, "wstg", cx.ga[:].rearrange("p (h c) -> p h c", h=4))
    cx.wstg.b = cx.ga.b
    cx.rn = T.view(S, "rn", cx.gt1[:])
    cx.rn.b = cx.gt1.b
    cx.vln = T(S, "vln", [128, 256], BF16)
    cx.cw = T(S, "cw", [128, 12, 4], F32)
    cx.halo = T(S, "halo", [128, 12, 4], F32)
    cx.ci = [T(S, "ci%d" % i, [128, 516], F32) for i in range(2)]
    cx.cy = [T(S, "cy%d" % i, [128, 512], F32) for i in range(1)] * 2
    cx.cs = [T(S, "cs%d" % i, [128, 512], F32) for i in range(1)] * 2
    cx.sq = [T(S, "sq%d" % i, [128, 512], BF16) for i in range(1)] * 2
    cx.qT = S.sb("qT", [128, 4, 512], BF16)
    cx.qTb = S.bufs(4, "qT")
    cx.kT = S.sb("kT", [128, 4, 512], BF16)
    cx.kTb = S.bufs(4, "kT")
    cx.vT = S.sb("vT", [128, 4, 512], BF16)
    cx.vTb = S.bufs(4, "vT")
    cx.dtb = T(S, "dtb", [128, 8], F32)
    cx.nega = T(S, "nega", [128, 8], F32)
    cx.gng = T(S, "gng", [128, 64], F32)
    cx.sm = T(S, "sm", [128, 96], F32)
    cx.sm2 = T(S, "sm2", [128, 96], F32)
    cx.osqc = T(S, "osqc", [128, 256], F32)
    cx.lgb = T(S, "lgb", [128, 8, 128], BF16)
    cx.lgl = T(S, "lgl", [128, 8, 128], BF16)
    cx.smb = T(S, "smb", [128, 32], BF16)
    cx.kTm = T(S, "kTm", [128, 4, 2, 128], BF16)
    cx.bekm = T(S, "bekm", [128, 4, 2, 128], BF16)
    cx.lgp = T(S, "lgp", [128, 8, 64], BF16)
    cx.lgpl = T(S, "lgpl", [128, 8, 64], BF16)
    cx.lah = T(S, "lah", [128, 128], BF16)
    cx.lal = T(S, "lal", [128, 128], BF16)
    cx.gbb = T(S, "gbb", [128, 128], F32)
    cx.gupb = T(S, "gupb", [16, 128], BF16)
    cx.cgTb = T(S, "cgTb", [16, 512], BF16)
    cx.EG = T(S, "EG", [128, 4, 128], F32)
    cx.BS = []
    for q in range(2):
        B = Ctx()
        B.tmpD = T(S, "tmpD%d" % q, [128, 4, 128], F32)
        B.tmpE = T(S, "tmpE%d" % q, [128, 4, 128], F32)
        B.E = T(S, "E%d" % q, [128, 4, 128], BF16)
        B.ET = T(S, "ET%d" % q, [128, 4, 128], BF16)
        if q == 0:
            for nm in ("L", "N", "P", "Q", "Xa", "Xb2"):
                setattr(B, nm, T(S, nm + "0", [128, 4, 128], BF16))
        else:
            for i_, nm in enumerate(("L", "N", "P", "Q", "Xa", "Xb2")):
                o = 8192 + 512 * i_
                setattr(B, nm, T.view(S, nm + "1", cx.gTflat[:, o:o + 512].rearrange("p (h c) -> p h c", h=4)))
        cx.BS.append(B)
    cx.gt_alias = [getattr(cx.BS[1], nm).b for nm in ("L", "N", "P", "Q", "Xa", "Xb2")]
    cx.pT = T(S, "pT", [128, 8, 128], BF16)
    cx.ktok = T(S, "ktok", [128, 512], BF16)
    cx.vtok = T(S, "vtok", [128, 512], BF16)
    cx.bv = T(S, "bv", [128, 512], BF16)
    cx.bek = T(S, "bek", [128, 512], BF16)
    cx.kdec = T(S, "kdec", [128, 512], BF16)
    cx.ub = T(S, "ub", [128, 512], F32)
    cx.wT = T(S, "wT", [128, 4, 128], BF16)
    cx.qdT = T(S, "qdT", [128, 4, 128], BF16)
    cx.gcol = T(S, "gcol", [128, 4], F32)
    cx.SB = [T(S, "SB%d" % i, [128, 128], F32) for i in range(4)]
    cx.SBb = [T(S, "SBb%d" % i, [128, 128], BF16) for i in range(4)]
    cx.ubf = [T(S, "ubf%d" % i, [128, 128], BF16) for i in range(2)]
    cx.stmp = [T(S, "stmp%d" % i, [128, 256], F32) for i in range(1)] * 2
    cx.ob = T(S, "ob", [128, 512], F32)
    cx.osq = cx.gt1
    cx.zs = T(S, "zs", [128, 512], F32)
    cx.gup = T(S, "gup", [16, 128], F32)
    cx.cng = T(S, "cng", [128, 64], F32)
    cx.cqT = T(S, "cqT", [128, 512], BF16)
    cx.ckT = T(S, "ckT", [128, 512], BF16)
    cx.la = T(S, "la", [128, 128], F32)
    cx.eb = T(S, "eb", [128, 128], F32)
    cx.enb = T(S, "enb", [128, 128], F32)
    cx.cqd = T(S, "cqd", [128, 128], BF16)
    cx.ckd = T(S, "ckd", [128, 128], BF16)
    cx.ckm = T(S, "ckm", [128, 4, 128], BF16)
    cx.ckdecT = T(S, "ckdecT", [128, 128], BF16)
    cx.ckdec = T(S, "ckdec", [128, 128], BF16)
    cx.cpT = T(S, "cpT", [128, 4, 128], BF16)
    cx.cv = T(S, "cv", [128, 256], BF16)
    cx.SC = T(S, "SC", [128, 256], F32)
    cx.SCb = T(S, "SCb", [128, 256], BF16)
    cx.oc = T(S, "oc", [128, 256], F32)
    cx.rs = T(S, "rs", [128, 256], F32)


def emit_make_T(S, cx, src_bf, src_b, dst, dst_b, t):
    ps, psb = next_ps(cx)
    psv = ps[:].bitcast(BF16)
    for k in range(8):
        TR(S, psv[:, k * 128:(k + 1) * 128], src_bf[:, k * 128:(k + 1) * 128], CB(cx, "ident"),
           [src_b, cx.cstb], [psb])
    CP(S, "dve", dst[:, :, t * 128:(t + 1) * 128],
       psv[:, 0:1024].rearrange("p (k c) -> p k c", k=8), [psb], [dst_b])


def emit_xt(S, cx, t):
    CP(S, "act", cx.xb16[:], cx.X[:, t, :], [cx.Xb[t]], [cx.xb16.b])
    emit_make_T(S, cx, cx.xb16, cx.xb16.b, cx.XT, cx.XTb[t], t)


def emit_ln(S, cx, src, srcb, dst, dstb, n, g, gb, b, bb):
    st, stb = cx.stat, cx.stat.b
    junk, junkb = cx.junk, cx.junk.b
    MEMSET(S, "dve", st[:, 0:2], 0.0, [stb])
    ACT(S, junk[:, 0:n], src[:, 0:n], AF.Identity, [srcb], [junkb, stb], accum=st[:, 0:1])
    ACT(S, junk[:, 0:n], src[:, 0:n], AF.Square, [srcb], [junkb, stb], accum=st[:, 1:2])
    TS(S, "dve", st[:, 2:4], st[:, 0:2], 1.0 / n, ALU.mult, [stb], [stb])
    TT(S, "dve", st[:, 4:5], st[:, 2:3], st[:, 2:3], ALU.mult, [stb], [stb])
    TT(S, "dve", st[:, 5:6], st[:, 3:4], st[:, 4:5], ALU.subtract, [stb], [stb])
    TS(S, "dve", st[:, 5:6], st[:, 5:6], EPS, ALU.add, [stb], [stb])
    ACT(S, st[:, 7:8], st[:, 5:6], AF.Sqrt, [stb], [stb])
    RECIP(S, st[:, 6:7], st[:, 7:8], [stb], [stb])
    TS(S, "dve", src[:, 0:n], src[:, 0:n], st[:, 2:3], ALU.subtract, [srcb, stb], [srcb],
       s2=st[:, 6:7], op1=ALU.mult)
    TT(S, "pool", src[:, 0:n], src[:, 0:n], g, ALU.mult, [srcb, gb], [srcb])
    TT(S, "pool", dst, src[:, 0:n], b, ALU.add, [srcb, bb], [dstb])


def emit_res_ln(S, cx, t, ybanks):
    for h in range(2):
        bi = ybanks[h]
        STT(S, "dve", cx.lt[:, h * 512:(h + 1) * 512], cx.X[:, t, h * 512:(h + 1) * 512], ALPHA,
            cx.ps[bi][:], ALU.mult, ALU.add, [cx.Xb[t], cx.psb[bi]], [cx.lt.b])
    emit_ln(S, cx, cx.lt, cx.lt.b, cx.X[:, t, :], cx.Xb[t], D, cx.lng[:], cx.lng.b, cx.lnb[:], cx.lnb.b)


def emit_load_ln(S, cx, g_ap, b_ap):
    DMA(S, "sp", cx.lng[:], g_ap.partition_broadcast(128), [], [cx.lng.b], cx.lng.b)
    DMA(S, "sp", cx.lnb[:], b_ap.partition_broadcast(128), [], [cx.lnb.b], cx.lnb.b)


def emit_ffn_ln(S, cx, w1, w3, w2, g_ap, b_ap):
    emit_load_ln(S, cx, g_ap, b_ap)
    w1v = w1.rearrange("(kc kp) f -> kp kc f", kp=128)
    w3v = w3.rearrange("(kc kp) f -> kp kc f", kp=128)
    w2v = w2.rearrange("(fc fp) d -> fp fc d", fp=128)
    for f in range(NF):
        wb = cx.w13[f % 2]
        S.op("pool", lambda e, f=f, wb=wb: [
            e.dma_start(out=wb[:, 0], in_=w1v[:, :, f * 128:(f + 1) * 128]),
            e.dma_start(out=wb[:, 1], in_=w3v[:, :, f * 128:(f + 1) * 128])],
            writes=[wb.b], dma=wb.b, ndma=2)
        pa, pab = next_ps(cx)
        pb, pbb = next_ps(cx)
        rd = [wb.b] + cx.XTb
        for k in range(8):
            MM(S, pa[:], wb[:, 0, k, :], cx.XT[:, k, :], k == 0, k == 7, rd, [pab])
        for k in range(8):
            MM(S, pb[:], wb[:, 1, k, :], cx.XT[:, k, :], k == 0, k == 7, rd, [pbb])
        sa = cx.sa[f % 2]
        ACT(S, sa[:], pa[:], AF.Silu, [pab], [sa.b])
        STT(S, "dve", cx.gT[:, f, :], sa[:], 0.5, pb[:], ALU.mult, ALU.mult, [sa.b, pbb],
            [cx.gTb[f], cx.wt[0].b, cx.wt[1].b] + (cx.gt_alias if f >= 16 else []))
    for f in range(NF):
        wb = cx.w2t[f % 2]
        DMA(S, "pool", wb[:], w2v[:, f, :], [], [wb.b], wb.b)
        for j in range(4):
            for h in range(2):
                bi = j * 2 + h
                MM(S, cx.ps[bi][:], cx.gT[:, f, j * 128:(j + 1) * 128], wb[:, h * 512:(h + 1) * 512],
                   f == 0, f == NF - 1, [wb.b, cx.gTb[f]], [cx.psb[bi]])
    for j in range(4):
        emit_res_ln(S, cx, j, (j * 2, j * 2 + 1))
    cx.psi = 0
    for j in range(4):
        emit_xt(S, cx, j)


C_AU, C_AV, C_BQ, C_BK, C_BV, C_BZ, C_BS, C_CQ, C_CK, C_CV, C_CR, C_CG = (
    0, 256, 512, 1024, 1536, 2048, 2560, 2576, 2704, 2832, 3088, 3344)
GELU_C = 1.5957691216057308


def emit_layer_setup(S, cx, P):
    DMA(S, "sp", cx.sgg[:], P["sgu_ln_g"].partition_broadcast(128), [], [cx.sgg.b], cx.sgg.b)
    DMA(S, "sp", cx.sgb[:], P["sgu_ln_b"].partition_broadcast(128), [], [cx.sgb.b], cx.sgb.b)
    DMA(S, "sp", cx.sbT[:], P["sgu_bT"], [], [cx.sbT.b], cx.sbT.b)
    DMA(S, "sp", cx.wstg[:], P["sgu_wT"].rearrange("h j i -> j h i"), [], [cx.wstg.b], cx.wstg.b)
    TT(S, "dve", cx.WmT[:], cx.wstg[:], bc4(CF(cx, "U")), ALU.mult, [cx.wstg.b, cx.cstb], [cx.WmT.b])
    DMA(S, "sp", cx.cw[:], P["conv_wT"].rearrange("(c p) i -> p c i", p=128), [], [cx.cw.b], cx.cw.b)
    DMA(S, "sp", cx.dtb[:], P["dt_bias"].partition_broadcast(128), [], [cx.dtb.b], cx.dtb.b)
    DMA(S, "sp", cx.nega[:], P["a_log"].partition_broadcast(128), [], [cx.nega.b], cx.nega.b)
    ACT(S, cx.nega[:], cx.nega[:], AF.Exp, [cx.nega.b], [cx.nega.b])
    TS(S, "dve", cx.nega[:], cx.nega[:], -1.0, ALU.mult, [cx.nega.b], [cx.nega.b])
    DMA(S, "sp", cx.gng[:], P["gdn_norm_g"].partition_broadcast(128), [], [cx.gng.b], cx.gng.b)
    DMA(S, "sp", cx.cng[:], P["gla_norm_g"].partition_broadcast(128), [], [cx.cng.b], cx.cng.b)
    DMA(S, "sp", cx.gup[:], P["gate_up"], [], [cx.gup.b], cx.gup.b)
    DMA(S, "sp", cx.gbb[:], P["gate_b"].partition_broadcast(128), [], [cx.gbb.b], cx.gbb.b)
    CP(S, "dve", cx.gupb[:], cx.gup[:], [cx.gup.b], [cx.gupb.b])
    MEMSET(S, "dve", cx.halo[:], 0.0, [cx.halo.b])
    for c in range(4):
        MEMSET(S, "dve", cx.SB[c][:], 0.0, [cx.SB[c].b])
        MEMSET(S, "dve", cx.SBb[c][:], 0.0, [cx.SBb[c].b])
    MEMSET(S, "dve", cx.SC[:], 0.0, [cx.SC.b])
    MEMSET(S, "dve", cx.SCb[:], 0.0, [cx.SCb.b])


def proj_fm(S, cx, winv, col0, ncols, wf):
    DMA(S, "pool", wf[:, :, 0:ncols], winv[:, :, col0:col0 + ncols], [], [wf.b], wf.b)
    ps, psb = next_ps(cx)
    for k in range(8):
        MM(S, ps[0:ncols, :], wf[:, k, 0:ncols], cx.XT[:, k, :], k == 0, k == 7, [wf.b] + cx.XTb, [psb])
    return ps, psb


def emit_rms_gate(S, cx, o, ob, nh, gtile, gate_ap, gate_b, dst, dstb, sq, sm=None):
    n = nh * 64
    sm = cx.sm if sm is None else sm
    TT(S, "dve", sq[:, 0:n], o[:, 0:n], o[:, 0:n], ALU.mult, [ob], [sq.b])
    S.op("dve", lambda e: e.tensor_reduce(out=sm[:, 64:64 + nh],
                                          in_=sq[:, 0:n].rearrange("p (h d) -> p h d", d=64),
                                          axis=AX.X, op=ALU.add), reads=[sq.b], writes=[sm.b])
    TS(S, "dve", sm[:, 64:64 + nh], sm[:, 64:64 + nh], 1.0 / 64, ALU.mult, [sm.b], [sm.b], s2=EPS, op1=ALU.add)
    ACT(S, sm[:, 72:72 + nh], sm[:, 64:64 + nh], AF.Sqrt, [sm.b], [sm.b])
    RECIP(S, sm[:, 80:80 + nh], sm[:, 72:72 + nh], [sm.b], [sm.b])
    ov = o[:, 0:n].rearrange("p (h d) -> p h d", d=64)
    TT(S, "dve", ov, ov, bcl(sm[:, 80:80 + nh], 64), ALU.mult, [ob, sm.b], [ob])
    TT(S, "dve", ov, ov, gtile[:].unsqueeze(1).to_broadcast([128, nh, 64]), ALU.mult, [ob, gtile.b], [ob])
    TT(S, "dve", dst, o[:, 0:n], gate_ap, ALU.mult, [ob, gate_b], [dstb])


def emit_mixer_group(S, cx, P):
    winv = P["w_in"].rearrange("(kc kp) f -> kp kc f", kp=128)
    U = CF(cx, "U")
    import os as _os
    STG = _os.environ.get("MIX_STAGES", "Bfm,Cfm,A,C,B").split(",")
    S.op("dve", lambda e: e.memset(cx.stat[:, 0:1], 0.0), writes=[cx.stat.b] + cx.gTb[16:] + cx.gt_alias)
    for ci in range(12 if "Bfm" in STG else 0):
        wf = cx.wf[ci % 3]
        ps, psb = proj_fm(S, cx, winv, C_BQ + 128 * ci, 128, wf)
        cit = cx.ci[ci % 2]
        CP(S, "dve", cit[:, 0:3], cx.halo[:, ci, 0:3], [cx.halo.b], [cit.b])
        CP(S, "act", cit[:, 3:515], ps[:], [psb], [cit.b])
        CP(S, "dve", cx.halo[:, ci, 0:3], cit[:, 512:515], [cit.b], [cx.halo.b])
        cy = cx.cy[ci % 2]
        TS(S, "dve", cy[:], cit[:, 0:512], cx.cw[:, ci, 0:1], ALU.mult, [cit.b, cx.cw.b], [cy.b])
        for i in range(1, 4):
            STT(S, "dve", cy[:], cit[:, i:i + 512], cx.cw[:, ci, i:i + 1], cy[:], ALU.mult, ALU.add,
                [cit.b, cx.cw.b, cy.b], [cy.b])
        cs = cx.cs[ci % 2]
        ACT(S, cs[:], cy[:], AF.Silu, [cy.b], [cs.b])
        c = ci % 4
        if ci < 8:
            sq = cx.sq[ci % 2]
            TT(S, "dve", sq[:], cs[:], cs[:], ALU.mult, [cs.b], [sq.b])
            ps2, ps2b = next_ps(cx)
            MM(S, ps2[:], CB(cx, "bm2"), sq[:], True, True, [sq.b, cx.cstb], [ps2b])
            TS(S, "dve", cx.rn[:], ps2[:], 1e-6, ALU.add, [ps2b], [cx.rn.b])
            ACT(S, cx.rn[:], cx.rn[:], AF.Sqrt, [cx.rn.b], [cx.rn.b])
            RECIP(S, cx.rn[:], cx.rn[:], [cx.rn.b], [cx.rn.b])
            if ci < 4:
                STT(S, "dve", cx.qT[:, c, :], cs[:], 0.125, cx.rn[:], ALU.mult, ALU.mult,
                    [cs.b, cx.rn.b], [cx.qTb[c]])
            else:
                TT(S, "dve", cx.kT[:, c, :], cs[:], cx.rn[:], ALU.mult, [cs.b, cx.rn.b], [cx.kTb[c]])
        else:
            CP(S, "act", cx.vT[:, c, :], cs[:], [cs.b], [cx.vTb[c]])
    if "Cfm" not in STG:
        return
    ps, psb = proj_fm(S, cx, winv, C_CQ, 128, cx.wf[0])
    CP(S, "act", cx.cqT[:], ps[:], [psb], [cx.cqT.b])
    ps, psb = proj_fm(S, cx, winv, C_CK, 128, cx.wf[1])
    CP(S, "act", cx.ckT[:], ps[:], [psb], [cx.ckT.b])
    ps, psb = proj_fm(S, cx, winv, C_CG, 16, cx.wf[2])
    CP(S, "act", cx.cgTb[:], ps[0:16, :], [psb], [cx.cgTb.b])
    wA, wZ, wC = cx.wt
    DMA(S, "pool", wA[:], winv[:, :, C_AU:C_AU + 512], [], [wA.b] + cx.gTb, wA.b)
    DMA(S, "pool", wZ[:], winv[:, :, C_BZ:C_BZ + 512], [], [wZ.b] + cx.gTb, wZ.b)
    DMA(S, "pool", wC[:], winv[:, :, C_CV:C_CV + 512], [], [wC.b], wC.b)
    wS = cx.wf[0]
    DMA(S, "pool", wS[:, :, 0:16], winv[:, :, C_BS:C_BS + 16], [], [wS.b], wS.b)
    for j in range(4):
        ts = slice(j * 128, (j + 1) * 128)
        extra = []
        if "A" in STG:
            extra.append(emit_mixer_A(S, cx, j, ts, wA))
        if "C" in STG:
            extra.append(emit_mixer_C(S, cx, j, ts, wC))
        if "B" in STG:
            emit_mixer_B(S, cx, j, ts, wZ, wS, extra)
        else:
            for g_ in extra:
                for _ in g_:
                    pass


def proj_tm(S, cx, ts, wt, ncols):
    ps, psb = next_ps(cx)
    for k in range(8):
        MM(S, ps[:, 0:ncols], cx.XT[:, k, ts], wt[:, k, 0:ncols], k == 0, k == 7, [wt.b] + cx.XTb, [psb])
    return ps, psb


def emit_mixer_A(S, cx, j, ts, wA):
    ps, psb = proj_tm(S, cx, ts, wA, 512)
    ga, t1 = cx.ga, cx.gt1
    CP(S, "act", ga[:], ps[:], [psb], [ga.b])
    yield
    TT(S, "dve", t1[:], ga[:], ga[:], ALU.mult, [ga.b], [t1.b])
    yield
    TS(S, "dve", t1[:], t1[:], 0.044715, ALU.mult, [t1.b], [t1.b], s2=1.0, op1=ALU.add)
    yield
    TT(S, "dve", t1[:], t1[:], ga[:], ALU.mult, [t1.b, ga.b], [t1.b])
    yield
    ACT(S, t1[:], t1[:], AF.Sigmoid, [t1.b], [t1.b], scale=GELU_C)
    yield
    TT(S, "dve", ga[:], ga[:], t1[:], ALU.mult, [ga.b, t1.b], [ga.b])
    yield
    emit_ln(S, cx, ga[:, 256:512], ga.b, cx.vln[:], cx.vln.b, 256, cx.sgg[:], cx.sgg.b, cx.sgb[:], cx.sgb.b)
    yield
    pz, pzb = next_ps(cx)
    for h in range(4):
        MM(S, pz[:, h * 64:(h + 1) * 64], cx.WmT[:, h, :], cx.vln[:, h * 64:(h + 1) * 64], True, True,
           [cx.WmT.b, cx.vln.b], [pzb])
    for h in range(4):
        STT(S, "dve", cx.MIX[:, j, h * 64:(h + 1) * 64], pz[:, h * 64:(h + 1) * 64], cx.sbT[:, h:h + 1],
            ga[:, h * 64:(h + 1) * 64], ALU.add, ALU.mult, [pzb, cx.sbT.b, ga.b], [cx.MIXb[j]])


def emit_mixer_C(S, cx, j, ts, wC):
    psC, psCb = proj_tm(S, cx, ts, wC, 512)
    CP(S, "act", cx.cv[:], psC[:, 0:256], [psCb], [cx.cv.b])
    yield
    ACT(S, cx.rs[:], psC[:, 256:512], AF.Silu, [psCb], [cx.rs.b])
    yield
    pl, plb = next_ps(cx)
    MM(S, pl[:, 0:128], cx.cgTb[0:16, ts], cx.gupb[0:16, :], True, True, [cx.cgTb.b, cx.gupb.b], [plb])
    TT(S, "dve", cx.la[:], pl[:, 0:128], cx.gbb[:], ALU.add, [plb, cx.gbb.b], [cx.la.b])
    yield
    ACT(S, cx.la[:], cx.la[:], AF.Exp, [cx.la.b], [cx.la.b], scale=-1.0)
    yield
    ACT(S, cx.la[:], cx.la[:], AF.Ln, [cx.la.b], [cx.la.b], bias=1.0)
    yield
    pb, pbb = next_ps(cx)
    CP(S, "dve", cx.lah[:], cx.la[:], [cx.la.b], [cx.lah.b])
    yield
    TT(S, "dve", cx.lal[:], cx.la[:], cx.lah[:], ALU.subtract, [cx.la.b, cx.lah.b], [cx.lal.b])
    yield
    MM(S, pb[:, 0:128], cx.lah[:], CB(cx, "Un16"), True, False, [cx.lah.b, cx.cstb], [pbb])
    MM(S, pb[:, 0:128], cx.lal[:], CB(cx, "Un16"), False, True, [cx.lal.b, cx.cstb], [pbb])
    ACT(S, cx.eb[:], pb[:, 0:128], AF.Exp, [pbb], [cx.eb.b])
    yield
    ACT(S, cx.enb[:], pb[:, 0:128], AF.Exp, [pbb], [cx.enb.b], scale=-1.0)
    yield
    STT(S, "dve", cx.cqd[:], cx.cqT[:, ts], 32.0 ** -0.5, cx.eb[:], ALU.mult, ALU.mult,
        [cx.cqT.b, cx.eb.b], [cx.cqd.b])
    yield
    TT(S, "dve", cx.ckd[:], cx.ckT[:, ts], cx.enb[:], ALU.mult, [cx.ckT.b, cx.enb.b], [cx.ckd.b])
    yield
    TS(S, "dve", cx.ckdecT[:], cx.ckd[:], cx.eb[:, 127:128], ALU.mult, [cx.ckd.b, cx.eb.b], [cx.ckdecT.b])
    yield
    pt, ptb = next_ps(cx)
    ptv = pt[:].bitcast(BF16)
    TR(S, ptv[:, 0:128], cx.ckdecT[:], CB(cx, "ident"), [cx.ckdecT.b, cx.cstb], [ptb])
    CP(S, "act", cx.ckdec[:], ptv[:, 0:128], [ptb], [cx.ckdec.b])
    yield
    for h in range(4):
        TS(S, "dve", cx.ckm[:, h, :], cx.ckd[:], CF(cx, "hm")[:, h:h + 1], ALU.mult,
           [cx.ckd.b, cx.cstb], [cx.ckm.b])
    pp, ppb = next_ps(cx)
    for h in range(4):
        MM(S, pp[:, h * 128:(h + 1) * 128], cx.ckm[:, h, :], cx.cqd[:], True, True,
           [cx.ckm.b, cx.cqd.b], [ppb])
    TT(S, "dve", cx.cpT[:], pp[:].rearrange("p (h c) -> p h c", h=4), bc4(CF(cx, "U")), ALU.mult,
       [ppb, cx.cstb], [cx.cpT.b])
    yield
    po, pob = next_ps(cx)
    for h in range(4):
        hs = slice(h * 64, (h + 1) * 64)
        MM(S, po[:, hs], cx.cqd[:], cx.SCb[:, hs], True, False, [cx.cqd.b, cx.SCb.b], [pob])
        MM(S, po[:, hs], cx.cpT[:, h, :], cx.cv[:, hs], False, True, [cx.cpT.b, cx.cv.b], [pob])
    CP(S, "act", cx.oc[:], po[:, 0:256], [pob], [cx.oc.b])
    yield
    pS, pSb = next_ps(cx)
    MM(S, pS[:, 0:256], cx.ckdec[:], cx.cv[:], True, True, [cx.ckdec.b, cx.cv.b], [pSb])
    st = cx.stmp[0]
    TT(S, "dve", st[:], pS[:, 0:256], CF(cx, "bmC"), ALU.mult, [pSb, cx.cstb], [st.b])
    yield
    STT(S, "dve", cx.SC[:], cx.SC[:], cx.eb[:, 127:128], st[:], ALU.mult, ALU.add,
        [cx.SC.b, cx.eb.b, st.b], [cx.SC.b])
    yield
    CP(S, "act", cx.SCb[:], cx.SC[:], [cx.SC.b], [cx.SCb.b])
    yield
    emit_rms_gate(S, cx, cx.oc, cx.oc.b, 4, cx.cng, cx.rs[:], cx.rs.b, cx.MIX[:, j, 768:1024], cx.MIXb[j],
                  cx.osqc, cx.sm2)
    yield


def emit_mixer_B(S, cx, j, ts, wZ, wS, extra=()):
    sm = cx.sm
    cst = cx.cstb
    pZ, pZb = proj_tm(S, cx, ts, wZ, 512)
    ACT(S, cx.zs[:], pZ[:], AF.Silu, [pZb], [cx.zs.b])
    p16, p16b = proj_tm(S, cx, ts, wS, 16)
    CP(S, "act", sm[:, 0:16], p16[:, 0:16], [p16b], [sm.b])
    ACT(S, sm[:, 0:8], sm[:, 0:8], AF.Sigmoid, [sm.b], [sm.b])
    TT(S, "dve", sm[:, 8:16], sm[:, 8:16], cx.dtb[:], ALU.add, [sm.b, cx.dtb.b], [sm.b])
    ACT(S, sm[:, 8:16], sm[:, 8:16], AF.Exp, [sm.b], [sm.b])
    ACT(S, sm[:, 8:16], sm[:, 8:16], AF.Ln, [sm.b], [sm.b], bias=1.0)
    TT(S, "dve", sm[:, 8:16], sm[:, 8:16], cx.nega[:], ALU.mult, [sm.b, cx.nega.b], [sm.b])
    pg, pgb = next_ps(cx)
    smb = cx.smb
    CP(S, "dve", smb[:, 0:8], sm[:, 8:16], [sm.b], [smb.b])
    TT(S, "dve", smb[:, 8:16], sm[:, 8:16], smb[:, 0:8], ALU.subtract, [sm.b, smb.b], [smb.b])
    MM(S, pg[:, 0:8], CB(cx, "U"), smb[:, 0:8], True, False, [smb.b, cst], [pgb])
    MM(S, pg[:, 0:8], CB(cx, "U"), smb[:, 8:16], False, True, [smb.b, cst], [pgb])
    MM(S, pg[:, 8:16], CB(cx, "ones"), smb[:, 0:8], True, False, [smb.b, cst], [pgb])
    MM(S, pg[:, 8:16], CB(cx, "ones"), smb[:, 8:16], False, True, [smb.b, cst], [pgb])
    CP(S, "dve", sm[:, 16:32], pg[:, 0:16], [pgb], [sm.b])
    ACT(S, sm[:, 32:40], sm[:, 16:24], AF.Exp, [sm.b], [sm.b])
    TT(S, "dve", sm[:, 40:48], sm[:, 24:32], sm[:, 16:24], ALU.subtract, [sm.b], [sm.b])
    ACT(S, sm[:, 40:48], sm[:, 40:48], AF.Exp, [sm.b], [sm.b])
    TT(S, "dve", sm[:, 48:56], sm[:, 0:8], sm[:, 32:40], ALU.mult, [sm.b], [sm.b])
    TT(S, "dve", cx.lgb[:], bcl(smb[:, 0:8], 128), bc4(CB(cx, "ones"), 8), ALU.mult, [smb.b, cst], [cx.lgb.b])
    TT(S, "dve", cx.lgl[:], bcl(smb[:, 8:16], 128), bc4(CB(cx, "ones"), 8), ALU.mult, [smb.b, cst], [cx.lgl.b])
    import os as _os
    CUT = int(_os.environ.get("MIXB_CUT", "99"))
    if CUT <= 1:
        return
    pk, pkb = next_ps(cx)
    pkv = pk[:].bitcast(BF16)
    for c in range(4):
        TR(S, pkv[:, c * 128:(c + 1) * 128], cx.kT[:, c, ts], CB(cx, "ident"), [cx.kTb[c], cst], [pkb])
    CP(S, "act", cx.ktok[:], pkv[:, 0:512], [pkb], [cx.ktok.b])
    pv, pvb = next_ps(cx)
    pvv = pv[:].bitcast(BF16)
    for c in range(4):
        TR(S, pvv[:, c * 128:(c + 1) * 128], cx.vT[:, c, ts], CB(cx, "ident"), [cx.vTb[c], cst], [pvb])
    CP(S, "act", cx.vtok[:], pvv[:, 0:512], [pvb], [cx.vtok.b])

    def hv(t):
        return t[:].rearrange("p (h d) -> p h d", d=64)

    TT(S, "dve", hv(cx.bv), hv(cx.vtok), bcl(sm[:, 0:8], 64), ALU.mult, [cx.vtok.b, sm.b], [cx.bv.b])
    TT(S, "dve", hv(cx.bek), hv(cx.ktok), bcl(sm[:, 48:56], 64), ALU.mult, [cx.ktok.b, sm.b], [cx.bek.b])
    TT(S, "dve", hv(cx.kdec), hv(cx.ktok), bcl(sm[:, 40:48], 64), ALU.mult, [cx.ktok.b, sm.b], [cx.kdec.b])

    for par in range(2):
        TS(S, "dve", cx.kTm[:, :, par, :], cx.kT[:, :, ts], CF(cx, "pm")[:, par:par + 1], ALU.mult,
           cx.kTb + [cst], [cx.kTm.b])
        TT(S, "dve", cx.bekm[:, :, par, :], cx.bek[:].rearrange("p (c x) -> p c x", c=4),
           bc4(CB(cx, "cm")[:, par * 128:(par + 1) * 128]), ALU.mult, [cx.bek.b, cst], [cx.bekm.b])
    TT(S, "dve", cx.lgp[:], bcl(smb[:, 0:8], 64), bc4(CB(cx, "ones")[:, 0:64], 8), ALU.mult, [smb.b, cst], [cx.lgp.b])
    TT(S, "dve", cx.lgpl[:], bcl(smb[:, 8:16], 64), bc4(CB(cx, "ones")[:, 0:64], 8), ALU.mult, [smb.b, cst], [cx.lgpl.b])
    pGp, pGpb = next_ps(cx)
    lgpv = cx.lgp[:].rearrange("p (c q) d -> p c (q d)", q=2)
    lgplv = cx.lgpl[:].rearrange("p (c q) d -> p c (q d)", q=2)
    for c in range(4):
        MM(S, pGp[:, c * 128:(c + 1) * 128], lgpv[:, c, :], CB(cx, "U"), True, False, [cx.lgp.b, cst], [pGpb])
        MM(S, pGp[:, c * 128:(c + 1) * 128], lgplv[:, c, :], CB(cx, "U"), False, True, [cx.lgpl.b, cst], [pGpb])
    ACT(S, cx.EG[:], pGp[:].rearrange("p (c x) -> p c x", c=4), AF.Exp, [pGpb], [cx.EG.b])
    TT(S, "dve", cx.qdT[:], cx.qT[:, :, ts], cx.EG[:], ALU.mult, cx.qTb + [cx.EG.b], [cx.qdT.b])
    CP(S, "dve", cx.gcol[:], cx.EG[:, :, 127], [cx.EG.b], [cx.gcol.b])

    if CUT <= 2:
        return
    def hg_chain(hg, B):
        h0 = 4 * hg

        def kTh(hh):
            h = h0 + hh
            pb = 64 * (h % 2)
            return cx.kT[pb:pb + 64, h // 2, ts], cx.kTb[h // 2]

        def qTh(hh):
            h = h0 + hh
            pb = 64 * (h % 2)
            return cx.qT[pb:pb + 64, h // 2, ts], cx.qTb[h // 2]

        pG, pGb = next_ps(cx)
        pGv = pG[:].rearrange("p (h c) -> p h c", h=4)
        for hh in range(4):
            MM(S, pG[:, hh * 128:(hh + 1) * 128], cx.lgb[:, h0 + hh, :], CB(cx, "U"), True, False,
               [cx.lgb.b, cst], [pGb])
            MM(S, pG[:, hh * 128:(hh + 1) * 128], cx.lgl[:, h0 + hh, :], CB(cx, "U"), False, True,
               [cx.lgl.b, cst], [pGb])
        TT(S, "dve", B.tmpD[:], pGv, bcl(sm[:, 16 + h0:20 + h0], 128), ALU.subtract, [pGb, sm.b], [B.tmpD.b])
        yield
        TS(S, "dve", B.tmpE[:], B.tmpD[:], 0.0, ALU.max, [B.tmpD.b], [B.tmpE.b])
        yield
        ACT(S, B.E[:], B.tmpE[:], AF.Exp, [B.tmpE.b], [B.E.b], scale=-1.0)
        yield
        TS(S, "dve", B.tmpE[:], B.tmpD[:], 0.0, ALU.min, [B.tmpD.b, B.E.b], [B.tmpE.b])
        yield
        ACT(S, B.ET[:], B.tmpE[:], AF.Exp, [B.tmpE.b], [B.ET.b])
        yield
        TT(S, "dve", B.E[:], B.E[:], bc4(CF(cx, "Lstr")), ALU.mult, [B.E.b, cst], [B.E.b])
        yield
        TT(S, "dve", B.ET[:], B.ET[:], bc4(CF(cx, "U")), ALU.mult, [B.ET.b, cst], [B.ET.b])
        yield
        if CUT <= 3:
            return
        pK, pKb = next_ps(cx)
        for hh in range(4):
            h = h0 + hh
            MM(S, pK[:, hh * 128:(hh + 1) * 128], cx.kTm[:, h // 2, h % 2, :], cx.kT[:, h // 2, ts], True, True,
               [cx.kTm.b, cx.kTb[h // 2]], [pKb])
        TT(S, "dve", B.tmpD[:], pK[:].rearrange("p (h c) -> p h c", h=4), B.E[:], ALU.mult,
           [pKb, B.E.b], [B.tmpD.b])
        yield
        TT(S, "dve", B.L[:], B.tmpD[:], bcl(sm[:, h0:h0 + 4], 128), ALU.mult, [B.tmpD.b, sm.b], [B.L.b])
        yield
        pN, pNb = next_ps(cx)
        pNv = pN[:].bitcast(BF16)
        for hh in range(4):
            TR(S, pNv[:, hh * 128:(hh + 1) * 128], B.L[:, hh, :], CB(cx, "ident"), [B.L.b, cst], [pNb])
        CP(S, "act", B.N[:], pNv[:, 0:512].rearrange("p (h c) -> p h c", h=4), [pNb], [B.N.b])
        yield
        if CUT <= 4:
            return
        I4 = bc4(CB(cx, "ident"))
        TT(S, "dve", B.Xa[:], B.L[:], bc4(CB(cx, "m1")), ALU.mult, [B.L.b, cst], [B.Xa.b])
        yield
        STT(S, "dve", B.P[:], B.Xa[:], -1.0, I4, ALU.mult, ALU.add, [B.Xa.b, cst], [B.P.b])
        yield
        TT(S, "dve", B.Xb2[:], B.N[:], bc4(CB(cx, "mT1")), ALU.mult, [B.N.b, cst], [B.Xb2.b])
        yield
        STT(S, "dve", B.Q[:], B.Xb2[:], -1.0, I4, ALU.mult, ALU.add, [B.Xb2.b, cst], [B.Q.b])
        yield
        for b in MB_LIST[1:]:
            last = b == MB_LIST[-1]
            p1, p1b = next_ps(cx)
            for hh in range(4):
                MM(S, p1[:, hh * 128:(hh + 1) * 128], B.N[:, hh, :], B.P[:, hh, :], True, True,
                   [B.N.b, B.P.b], [p1b])
            TT(S, "dve", B.Xa[:], p1[:].rearrange("p (h c) -> p h c", h=4), bc4(CB(cx, "m%d" % b)), ALU.mult,
               [p1b, cst], [B.Xa.b])
            yield
            if not last:
                p2, p2b = next_ps(cx)
                for hh in range(4):
                    MM(S, p2[:, hh * 128:(hh + 1) * 128], B.L[:, hh, :], B.Q[:, hh, :], True, True,
                       [B.L.b, B.Q.b], [p2b])
                TT(S, "dve", B.Xb2[:], p2[:].rearrange("p (h c) -> p h c", h=4), bc4(CB(cx, "mT%d" % b)),
                   ALU.mult, [p2b, cst], [B.Xb2.b])
                yield
            p3, p3b = next_ps(cx)
            for hh in range(4):
                MM(S, p3[:, hh * 128:(hh + 1) * 128], B.Xa[:, hh, :], B.Q[:, hh, :], True, True,
                   [B.Xa.b, B.Q.b], [p3b])
            if not last:
                p4, p4b = next_ps(cx)
                for hh in range(4):
                    MM(S, p4[:, hh * 128:(hh + 1) * 128], B.Xb2[:, hh, :], B.P[:, hh, :], True, True,
                       [B.Xb2.b, B.P.b], [p4b])
            TT(S, "dve", B.Q[:], B.Q[:], p3[:].rearrange("p (h c) -> p h c", h=4), ALU.subtract,
               [B.Q.b, p3b], [B.Q.b])
            yield
            if not last:
                TT(S, "dve", B.P[:], B.P[:], p4[:].rearrange("p (h c) -> p h c", h=4), ALU.subtract,
                   [B.P.b, p4b], [B.P.b])
                yield
        if CUT <= 5:
            return
        pu, pub = next_ps(cx)
        for hh in range(4):
            h = h0 + hh
            MM(S, pu[:, hh * 64:(hh + 1) * 64], B.Q[:, hh, :], cx.bv[:, h * 64:(h + 1) * 64], True, True,
               [B.Q.b, cx.bv.b], [pub])
        CP(S, "act", cx.ub[:, hg * 256:(hg + 1) * 256], pu[:, 0:256], [pub], [cx.ub.b])
        yield
        pw, pwb = next_ps(cx)
        for cc in range(2):
            c = 2 * hg + cc
            for par in range(2):
                MM(S, pw[:, cc * 128:(cc + 1) * 128], cx.bekm[:, c, par, :], B.Q[:, 2 * cc + par, :],
                   par == 0, par == 1, [cx.bekm.b, B.Q.b], [pwb])
        CP(S, "act", cx.wT[:, 2 * hg:2 * hg + 2, :], pw[:, 0:256].rearrange("p (c x) -> p c x", c=2),
           [pwb], [cx.wT.b])
        yield
        pq, pqb = next_ps(cx)
        for hh in range(4):
            h = h0 + hh
            MM(S, pq[:, hh * 128:(hh + 1) * 128], cx.kTm[:, h // 2, h % 2, :], cx.qT[:, h // 2, ts], True, True,
               [cx.kTm.b, cx.qTb[h // 2]], [pqb])
        TT(S, "dve", cx.pT[:, h0:h0 + 4, :], pq[:].rearrange("p (h c) -> p h c", h=4), B.ET[:], ALU.mult,
           [pqb, B.ET.b], [cx.pT.b])
        yield

    gens = [hg_chain(0, cx.BS[0]), hg_chain(1, cx.BS[1])] + list(extra)
    while gens:
        for g_ in list(gens):
            try:
                next(g_)
            except StopIteration:
                gens.remove(g_)
    if CUT <= 6:
        return
    for c in range(4):
        SBc, SBb = cx.SB[c], cx.SBb[c]
        ubf = cx.ubf[c % 2]
        pu2, pu2b = next_ps(cx)
        MM(S, pu2[:, 0:128], cx.wT[:, c, :], SBb[:], True, True, [cx.wT.b, SBb.b], [pu2b])
        TT(S, "dve", ubf[:], cx.ub[:, c * 128:(c + 1) * 128], pu2[:, 0:128], ALU.subtract,
           [cx.ub.b, pu2b], [ubf.b])
        po, pob = next_ps(cx)
        for par in range(2):
            h = 2 * c + par
            hs = slice(par * 64, par * 64 + 64)
            MM(S, po[:, hs], cx.qdT[:, c, :], SBb[:, hs], True, False, [cx.qdT.b, SBb.b], [pob])
            MM(S, po[:, hs], cx.pT[:, h, :], ubf[:, hs], False, True, [cx.pT.b, ubf.b], [pob])
        CP(S, "act", cx.ob[:, c * 128:(c + 1) * 128], po[:, 0:128], [pob], [cx.ob.b])
        pS, pSb = next_ps(cx)
        MM(S, pS[:, 0:128], cx.kdec[:, c * 128:(c + 1) * 128], ubf[:], True, True, [cx.kdec.b, ubf.b], [pSb])
        st = cx.stmp[c % 2]
        TT(S, "dve", st[:, 0:128], pS[:, 0:128], CF(cx, "bm2"), ALU.mult, [pSb, cst], [st.b])
        STT(S, "dve", SBc[:], SBc[:], cx.gcol[:, c:c + 1], st[:, 0:128], ALU.mult, ALU.add,
            [SBc.b, cx.gcol.b, st.b], [SBc.b])
        CP(S, "act", SBb[:], SBc[:], [SBc.b], [SBb.b])
    if CUT <= 7:
        return
    emit_rms_gate(S, cx, cx.ob, cx.ob.b, 8, cx.gng, cx.zs[:], cx.zs.b, cx.MIX[:, j, 256:768], cx.MIXb[j],
                  cx.osq)


def emit_wout_ln(S, cx, P):
    for j in range(4):
        emit_make_T(S, cx, cx.MIX[:, j, :], cx.MIXb[j], cx.XT, cx.XTb[j], j)
    emit_load_ln(S, cx, P["ln2_g"], P["ln2_b"])
    wov = P["w_out"].rearrange("(kc kp) d -> kp kc d", kp=128)
    cx.psi = 0
    for k in range(8):
        wo = cx.wo[k % 2]
        DMA(S, "pool", wo[:], wov[:, k, :], [], [wo.b], wo.b)
        for j in range(4):
            for h in range(2):
                bi = j * 2 + h
                MM(S, cx.ps[bi][:], cx.XT[:, k, j * 128:(j + 1) * 128], wo[:, h * 512:(h + 1) * 512],
                   k == 0, k == 7, [wo.b, cx.XTb[j]], [cx.psb[bi]])
    for j in range(4):
        emit_res_ln(S, cx, j, (j * 2, j * 2 + 1))
    cx.psi = 0
    for j in range(4):
        emit_xt(S, cx, j)


PNAMES = ["ffn1_w1", "ffn1_w3", "ffn1_w2", "ln1_g", "ln1_b", "w_in", "sgu_ln_g", "sgu_ln_b", "sgu_wT", "sgu_bT",
          "conv_wT", "a_log", "dt_bias", "gdn_norm_g", "gate_up", "gate_b", "gla_norm_g", "w_out", "ln2_g",
          "ln2_b", "ffn2_w1", "ffn2_w3", "ffn2_w2", "ln3_g", "ln3_b"]
PSHAPES = {"ffn1_w1": [D, DFF], "ffn1_w3": [D, DFF], "ffn1_w2": [DFF, D], "ln1_g": [D], "ln1_b": [D],
           "w_in": [D, DIN], "sgu_ln_g": [256], "sgu_ln_b": [256], "sgu_wT": [4, 128, 128], "sgu_bT": [128, 4],
           "conv_wT": [1536, 4], "a_log": [8], "dt_bias": [8], "gdn_norm_g": [64], "gate_up": [16, 128],
           "gate_b": [128], "gla_norm_g": [64], "w_out": [D, D], "ln2_g": [D], "ln2_b": [D],
           "ffn2_w1": [D, DFF], "ffn2_w3": [D, DFF], "ffn2_w2": [DFF, D], "ln3_g": [D], "ln3_b": [D]}


def build_program(T_tok, depth, stop_after=None, dbg=False):
    nc = bass.Bass("TRN2", target_bir_lowering=False)
    x = nc.dram_tensor("x", [T_tok, D], F32, kind="ExternalInput").ap()
    y = nc.dram_tensor("y", [T_tok, D], F32, kind="ExternalOutput").ap()
    cst = nc.dram_tensor("cstf", [128, NCSTF], F32, kind="ExternalInput").ap()
    cstb = nc.dram_tensor("cstb", [128, NCSTB], F32, kind="ExternalInput").ap()
    prm = {}
    for n in PNAMES:
        prm[n] = nc.dram_tensor(n, [depth] + PSHAPES[n], F32, kind="ExternalInput").ap()
    xs = [nc.dram_tensor("xs%d" % i, [T_tok, D], F32, kind="Internal").ap() for i in range(2)]
    NG = T_tok // 512
    with ExitStack() as stack:
        S = Sched(nc, stack)
        cx = Ctx()
        alloc_all(S, nc, cx)
        DMA(S, "sp", cx.CST[:], cst, [], [cx.cstb], cx.cstb)
        cstb2 = S.buf("cstb2")
        DMA(S, "pool", cx.CSTB[:], cstb, [], [cstb2], cstb2)
        S.op("dve", lambda e: e.memset(cx.stat[:, 0:1], 0.0), reads=[cstb2], writes=[cx.cstb, cx.stat.b])
        xsb = [[S.buf("xs%d_%d" % (i, g)) for g in range(NG)] for i in range(2)]
        yb = S.buf("y")
        for l in range(depth):
            P = {n: prm[n][l] for n in PNAMES}
            emit_layer_setup(S, cx, P)
            src = x if l == 0 else xs[(l - 1) % 2]
            dst = y if l == depth - 1 else xs[l % 2]
            srcv = src.rearrange("(g t p) d -> g p t d", p=128, t=4)
            dstv = dst.rearrange("(g t p) d -> g p t d", p=128, t=4)
            for g in range(NG):
                for t in range(4):
                    rd = [] if l == 0 else [xsb[(l - 1) % 2][g]]
                    DMA(S, "sp", cx.X[:, t, :], srcv[g, :, t, :], rd, [cx.Xb[t]], cx.Xb[t])
                for t in range(4):
                    emit_xt(S, cx, t)
                emit_ffn_ln(S, cx, P["ffn1_w1"], P["ffn1_w3"], P["ffn1_w2"], P["ln1_g"], P["ln1_b"])
                if stop_after != "ffn1":
                    emit_mixer_group(S, cx, P)
                    if dbg and g == 0 and l == 0:
                        S.dump("mix", cx.MIX[:], cx.MIXb[3], [128, 4, D], BF16)
                    emit_wout_ln(S, cx, P)
                    if stop_after != "mix":
                        emit_ffn_ln(S, cx, P["ffn2_w1"], P["ffn2_w3"], P["ffn2_w2"], P["ln3_g"], P["ln3_b"])
                for t in range(4):
                    wr = [yb] if l == depth - 1 else [xsb[l % 2][g]]
                    DMA(S, "sp", dstv[g, :, t, :], cx.X[:, t, :], [cx.Xb[t]], wr, cx.Xb[t])
        S.finish([yb] + cx.Xb)
        S.emit()
    return nc


def host_params(inputs, depth=DEPTH):
    f = lambda a: np.ascontiguousarray(np.asarray(a, dtype=np.float32))
    p = {}
    for n in ["ffn1_w1", "ffn1_w3", "ffn1_w2", "ln1_g", "ln1_b", "w_in", "sgu_ln_g", "sgu_ln_b", "w_out",
              "ln2_g", "ln2_b", "ffn2_w1", "ffn2_w3", "ffn2_w2", "ln3_g", "ln3_b"]:
        p[n] = f(inputs[n])[:depth]
    p["sgu_wT"] = f(np.transpose(np.asarray(inputs["sgu_w"]), (0, 1, 3, 2)))[:depth]
    p["sgu_bT"] = f(np.transpose(np.asarray(inputs["sgu_b"]), (0, 2, 1)))[:depth]
    p["conv_wT"] = f(np.transpose(np.asarray(inputs["gdn_conv_w"]), (0, 2, 1)))[:depth]
    p["a_log"] = f(inputs["gdn_a_log"])[:depth]
    p["dt_bias"] = f(inputs["gdn_dt_bias"])[:depth]
    p["gdn_norm_g"] = f(inputs["gdn_norm_g"])[:depth]
    p["gate_up"] = f(inputs["gla_gate_up"])[:depth]
    p["gate_b"] = f(inputs["gla_gate_b"])[:depth]
    p["gla_norm_g"] = f(inputs["gla_norm_g"])[:depth]
    p["cstf"] = CSTF_NP
    p["cstb"] = CSTB_NP
    return p


_PROG = {}


def kernel(**inputs):
    x = np.asarray(inputs["x"], dtype=np.float32)
    B, T_tok, _ = x.shape
    p = host_params(inputs)
    key = (T_tok, DEPTH)
    if key not in _PROG:
        _PROG[key] = build_program(T_tok, DEPTH)
    nc = _PROG[key]
    in_maps = []
    for b in range(B):
        m = dict(p)
        m["x"] = np.ascontiguousarray(x[b])
        in_maps.append(m)
    res = run_bass_kernel_spmd(nc, in_maps, core_ids=list(range(B)))
    return np.stack([res.results[b]["y"] for b in range(B)], axis=0).astype(np.float32)
```

```python
import numpy as np
from contextlib import ExitStack
import concourse.bass as bass
import concourse.mybir as mybir
from concourse.bass_utils import run_bass_kernel_spmd

F32 = mybir.dt.float32
BF16 = mybir.dt.bfloat16
AF = mybir.ActivationFunctionType
ALU = mybir.AluOpType
AX = mybir.AxisListType

D = 1024
DFF = 2816
NF = DFF // 128
DEPTH = 4
TOK = 2048
NT = TOK // 128
ALPHA = (2.0 * DEPTH) ** 0.25
EPS = 1e-5
DIN = 3360


import os as _os0
SKIP_SAME = tuple(_os0.environ.get("SKIP_SAME", "pe").split(","))


class Buf:
    __slots__ = ("name", "wev", "revs", "dsem")

    def __init__(self, name):
        self.name = name
        self.wev = None
        self.revs = {}
        self.dsem = None


class Sched:
    ENG = ("pe", "act", "dve", "pool", "sp")

    def __init__(self, nc, stack):
        self.nc = nc
        self.stack = stack
        self.prog = {e: [] for e in self.ENG}
        self.esem = {e: stack.enter_context(nc.semaphore("s_" + e)) for e in self.ENG}
        self.cnt = {}
        self.seen = {e: {} for e in self.ENG}
        self.nsem = len(self.ENG)
        self.nbuf = 0
        self.dbg = []

    def buf(self, name=None):
        self.nbuf += 1
        return Buf(name or "b%d" % self.nbuf)

    def bufs(self, n, name="b"):
        return [self.buf("%s%d" % (name, i)) for i in range(n)]

    def sb(self, name, shape, dtype):
        return self.stack.enter_context(self.nc.sbuf_tensor(name, list(shape), dtype))

    def ps(self, name, shape, dtype):
        return self.stack.enter_context(self.nc.psum_tensor(name, list(shape), dtype))

    def op(self, eng, fn, reads=(), writes=(), dma=None, ndma=1):
        waits = {}

        def need(ev):
            if ev is None:
                return
            s, v = ev
            if v > waits.get(s, 0):
                waits[s] = v

        for b in reads:
            need(b.wev)
        for b in writes:
            need(b.wev)
            for ev in b.revs.values():
                need(ev)
        if dma is not None:
            if dma.dsem is None:
                self.nsem += 1
                dma.dsem = self.stack.enter_context(self.nc.semaphore("d%d" % self.nsem))
            sem = dma.dsem
            amt = 16
            total = 16 * ndma
        else:
            sem = self.esem[eng]
            amt = 1
            total = 1
        own = self.esem[eng]
        wl = []
        for s, v in waits.items():
            if s is own and eng in SKIP_SAME:
                continue
            if self.seen[eng].get(s, 0) >= v:
                continue
            self.seen[eng][s] = v
            wl.append((s, v))
        self.cnt[sem] = self.cnt.get(sem, 0) + total
        ev = (sem, self.cnt[sem])
        self.prog[eng].append((wl, fn, sem, amt))
        for b in reads:
            b.revs[sem] = ev
        for b in writes:
            b.wev = ev
            b.revs = {}
        return ev

    def dump(self, name, ap, buf, shape, dtype=F32):
        d = self.nc.dram_tensor("dbg_" + name, list(shape), dtype, kind="ExternalOutput").ap()
        db = self.buf("dbg_" + name)
        self.dbg.append(db)
        self.op("sp", lambda e: e.dma_start(out=d, in_=ap), reads=[buf], writes=[db], dma=buf)

    def finish(self, bufs):
        bufs = list(bufs) + self.dbg
        waits = {}
        for b in bufs:
            for ev in [b.wev] + list(b.revs.values()):
                if ev is not None and ev[1] > waits.get(ev[0], 0):
                    waits[ev[0]] = ev[1]
        self.prog["sp"].append((list(waits.items()), None, None, 0))

    def emit(self):
        prog = self.prog

        def mk(name):
            def f(e):
                for wl, fn, sem, amt in prog[name]:
                    for s, v in wl:
                        e.wait_ge(s, v)
                    if fn is None:
                        continue
                    r = fn(e)
                    if not isinstance(r, (list, tuple)):
                        r = [r]
                    for ins in r:
                        ins.then_inc(sem, amt)
            return f

        with self.nc.Block() as block:
            block.tensor(mk("pe"))
            block.scalar(mk("act"))
            block.vector(mk("dve"))
            block.gpsimd(mk("pool"))
            block.sync(mk("sp"))


def MM(S, out, lhsT, rhs, start, stop, rd, wr):
    S.op("pe", lambda e: e.matmul(out, lhsT=lhsT, rhs=rhs, start=start, stop=stop), reads=rd, writes=wr)


def TR(S, out, in_, ident, rd, wr):
    S.op("pe", lambda e: e.transpose(out=out, in_=in_, identity=ident), reads=rd, writes=wr)


def ACT(S, out, in_, func, rd, wr, scale=None, bias=None, accum=None):
    kw = {}
    if scale is not None:
        kw["scale"] = scale
    if bias is not None:
        kw["bias"] = bias
    if accum is not None:
        kw["accum_out"] = accum
    S.op("act", lambda e: e.activation(out=out, in_=in_, func=func, **kw), reads=rd, writes=wr)


def TT(S, eng, out, in0, in1, op, rd, wr):
    S.op(eng, lambda e: e.tensor_tensor(out=out, in0=in0, in1=in1, op=op), reads=rd, writes=wr)


def TS(S, eng, out, in0, s1, op0, rd, wr, s2=None, op1=None):
    if op1 is None:
        S.op(eng, lambda e: e.tensor_scalar(out=out, in0=in0, scalar1=s1, scalar2=None, op0=op0),
             reads=rd, writes=wr)
    else:
        S.op(eng, lambda e: e.tensor_scalar(out=out, in0=in0, scalar1=s1, scalar2=s2, op0=op0, op1=op1),
             reads=rd, writes=wr)


def STT(S, eng, out, in0, scalar, in1, op0, op1, rd, wr):
    S.op(eng, lambda e: e.scalar_tensor_tensor(out=out, in0=in0, scalar=scalar, in1=in1, op0=op0, op1=op1),
         reads=rd, writes=wr)


def CP(S, eng, out, in_, rd, wr):
    if eng == "act":
        S.op("act", lambda e: e.activation(out=out, in_=in_, func=AF.Copy), reads=rd, writes=wr)
    else:
        S.op(eng, lambda e: e.tensor_copy(out=out, in_=in_), reads=rd, writes=wr)


def RECIP(S, out, in_, rd, wr):
    S.op("dve", lambda e: e.reciprocal(out=out, in_=in_), reads=rd, writes=wr)


def DMA(S, eng, out, in_, rd, wr, dma):
    S.op(eng, lambda e: e.dma_start(out=out, in_=in_), reads=rd, writes=wr, dma=dma)


def WLOAD(S, cx, key, tile, flat_ap, cast_fn):
    if key not in cx.scr:
        n = flat_ap.shape[1]
        cx.scr[key] = cx.nc.dram_tensor("scr_" + key, [128, n], BF16).ap()
        cx.scrb[key] = S.buf("scr_" + key)
    scr, scrb = cx.scr[key], cx.scrb[key]
    if not hasattr(tile, "hw"):
        tile.hw = S.buf("hw")
    if cx.first:
        cast_fn()
        DMA(S, "sp", scr, flat_ap, [tile.b], [scrb], tile.hw)
    else:
        DMA(S, "sp", flat_ap, scr, [scrb], [tile.b], tile.hw)


def MEMSET(S, eng, ap, val, wr):
    S.op(eng, lambda e: e.memset(ap, val), writes=wr)


class Ctx:
    pass


class T:
    def __init__(self, S, name, shape, dtype):
        self.h = S.sb(name, shape, dtype)
        self.b = S.buf(name)

    def __getitem__(self, k):
        return self.h[k]

    @classmethod
    def view(cls, S, name, ap):
        o = cls.__new__(cls)
        o.h = ap
        o.b = S.buf(name)
        return o


MB_LIST = [1, 2, 4, 8, 16, 32, 64]


def make_consts():
    i = np.arange(128)
    cf = {}
    cb = {}
    cb["ident"] = np.eye(128)
    cf["U"] = (i[:, None] <= i[None, :]) * 1.0
    cf["Un16"] = (i[:, None] <= i[None, :]) * (-1.0 / 16.0)
    cf["ones"] = np.ones((128, 128))
    cf["Lstr"] = (i[:, None] > i[None, :]) * 1.0
    for b in MB_LIST:
        bi = i // b
        m = ((bi[:, None] % 2 == 1) & (bi[None, :] == bi[:, None] - 1)) * 1.0
        cb["m%d" % b] = m
        cb["mT%d" % b] = m.T.copy()
    cf["bm2"] = ((i[:, None] // 64) == (i[None, :] // 64)) * 1.0
    cb["bm2"] = cf["bm2"]
    cb["U"] = cf["U"]
    cb["Un16"] = cf["Un16"]
    cb["ones"] = cf["ones"]
    j = np.arange(256)
    cf["bmC"] = ((i[:, None] // 32) == (j[None, :] // 64)) * 1.0
    cf["hm"] = ((i[:, None] // 32) == np.arange(4)[None, :]) * 1.0
    cf["pm"] = ((i[:, None] // 64) == np.arange(2)[None, :]) * 1.0
    cb["cm"] = np.concatenate([np.tile((i[None, :] // 64 == q) * 1.0, (128, 1)) for q in range(2)], axis=1)

    def pack(cols):
        off = {}
        arrs = []
        o = 0
        for k, v in cols.items():
            off[k] = (o, v.shape[1])
            o += v.shape[1]
            arrs.append(v.astype(np.float32))
        return np.concatenate(arrs, axis=1), off

    return pack(cf), pack(cb)


(CSTF_NP, CSTF_OFF), (CSTB_NP, CSTB_OFF) = make_consts()
NCSTF = CSTF_NP.shape[1]
NCSTB = CSTB_NP.shape[1]


def next_ps(cx):
    i = cx.psi
    cx.psi = (cx.psi + 1) % 8
    return cx.ps[i], cx.psb[i]


def CF(cx, name):
    o, n = CSTF_OFF[name]
    return cx.CST[:, o:o + n]


def CB(cx, name):
    o, n = CSTB_OFF[name]
    return cx.CSTB[:, o:o + n]


def bc4(ap, n=4):
    return ap.unsqueeze(1).to_broadcast([128, n, ap.shape[1]])


def bcl(ap, m):
    return ap.unsqueeze(2).to_broadcast([128, ap.shape[1], m])


def alloc_all(S, nc, cx):
    cx.ps = [S.ps("ps%d" % i, [128, 512], F32) for i in range(8)]
    cx.psb = S.bufs(8, "ps")
    cx.psi = 0
    cx.CST = S.sb("CST", [128, NCSTF], F32)
    cx.CSTB = S.sb("CSTB", [128, NCSTB], BF16)
    cx.cstb = S.buf("cst")
    cx.X = S.sb("X", [128, 4, D], F32)
    cx.Xb = S.bufs(4, "X")
    cx.XT = S.sb("XT", [128, 8, 512], BF16)
    cx.XTb = S.bufs(4, "XT")
    cx.MIX = S.sb("MIX", [128, 4, D], BF16)
    cx.MIXb = S.bufs(4, "MIX")
    cx.lng = T(S, "lng", [128, D], F32)
    cx.lnb = T(S, "lnb", [128, D], F32)
    cx.w13 = [T(S, "w13_%d" % i, [128, 2, 8, 128], BF16) for i in range(2)]
    cx.w2t = [T(S, "w2_%d" % i, [128, D], BF16) for i in range(2)]
    cx.sa = [T(S, "sa%d" % i, [128, 512], F32) for i in range(1)] * 2
    cx.gTflat = S.sb("gT", [128, NF * 512], BF16)
    cx.gT = cx.gTflat[:, :].rearrange("p (f t) -> p f t", f=NF)
    cx.gTb = S.bufs(NF, "gT")
    cx.lt = T(S, "lt", [128, D], F32)
    cx.stat = T(S, "stat", [128, 8], F32)
    cx.junk = T(S, "junk", [128, D], BF16)
    cx.xb16 = T(S, "xb16", [128, D], BF16)
    cx.wf = [T(S, "wf%d" % i, [128, 8, 128], BF16) for i in range(3)]
    cx.wt = [T.view(S, "wt%d" % i, cx.gTflat[:, i * 4096:(i + 1) * 4096].rearrange("p (k c) -> p k c", k=8))
             for i in range(2)] + [T(S, "wt2", [128, 8, 512], BF16)]
    cx.wo = [T(S, "wo%d" % i, [128, D], BF16) for i in range(2)]
    cx.sgg = T(S, "sgg", [128, 256], F32)
    cx.sgb = T(S, "sgb", [128, 256], F32)
    cx.sbT = T(S, "sbT", [128, 4], F32)
    cx.WmT = T(S, "WmT", [128, 4, 128], BF16)
    cx.ga = T(S, "ga", [128, 512], F32)
    cx.gt1 = T(S, "gt1", [128, 512], F32)
    cx.wstg = T.view(S, "wstg", cx.ga[:].rearrange("p (h c) -> p h c", h=4))
    cx.wstg.b = cx.ga.b
    cx.rn = T.view(S, "rn", cx.gt1[:])
    cx.rn.b = cx.gt1.b
    cx.vln = T(S, "vln", [128, 256], BF16)
    cx.cw = T(S, "cw", [128, 12, 4], F32)
    cx.halo = T(S, "halo", [128, 12, 4], F32)
    cx.ci = [T(S, "ci%d" % i, [128, 516], F32) for i in range(2)]
    cx.cy = [T(S, "cy%d" % i, [128, 512], F32) for i in range(1)] * 2
    cx.cs = [T(S, "cs%d" % i, [128, 512], F32) for i in range(1)] * 2
    cx.sq = [T(S, "sq%d" % i, [128, 512], BF16) for i in range(1)] * 2
    cx.qT = S.sb("qT", [128, 4, 512], BF16)
    cx.qTb = S.bufs(4, "qT")
    cx.kT = S.sb("kT", [128, 4, 512], BF16)
    cx.kTb = S.bufs(4, "kT")
    cx.vT = S.sb("vT", [128, 4, 512], BF16)
    cx.vTb = S.bufs(4, "vT")
    cx.dtb = T(S, "dtb", [128, 8], F32)
    cx.nega = T(S, "nega", [128, 8], F32)
    cx.gng = T(S, "gng", [128, 64], F32)
    cx.sm = T(S, "sm", [128, 96], F32)
    cx.sm2 = T(S, "sm2", [128, 96], F32)
    cx.osqc = T(S, "osqc", [128, 256], F32)
    cx.lgb = T(S, "lgb", [128, 8, 128], BF16)
    cx.lgl = T(S, "lgl", [128, 8, 128], BF16)
    cx.smb = T(S, "smb", [128, 32], BF16)
    cx.kTm = T(S, "kTm", [128, 4, 2, 128], BF16)
    cx.bekm = T(S, "bekm", [128, 4, 2, 128], BF16)
    cx.lgp = T(S, "lgp", [128, 8, 64], BF16)
    cx.lgpl = T(S, "lgpl", [128, 8, 64], BF16)
    cx.lah = T(S, "lah", [128, 128], BF16)
    cx.lal = T(S, "lal", [128, 128], BF16)
    cx.gbb = T(S, "gbb", [128, 128], F32)
    cx.gupb = T(S, "gupb", [16, 128], BF16)
    cx.cgTb = T(S, "cgTb", [16, 512], BF16)
    cx.EG = T(S, "EG", [128, 4, 128], F32)
    cx.BS = []
    for q in range(2):
        B = Ctx()
        B.tmpD = T(S, "tmpD%d" % q, [128, 4, 128], F32)
        B.tmpE = T(S, "tmpE%d" % q, [128, 4, 128], F32)
        B.E = T(S, "E%d" % q, [128, 4, 128], BF16)
        B.ET = T(S, "ET%d" % q, [128, 4, 128], BF16)
        if q == 0:
            for nm in ("L", "N", "P", "Q", "Xa", "Xb2"):
                setattr(B, nm, T(S, nm + "0", [128, 4, 128], BF16))
        else:
            for i_, nm in enumerate(("L", "N", "P", "Q", "Xa", "Xb2")):
                o = 8192 + 512 * i_
                setattr(B, nm, T.view(S, nm + "1", cx.gTflat[:, o:o + 512].rearrange("p (h c) -> p h c", h=4)))
        cx.BS.append(B)
    cx.gt_alias = [getattr(cx.BS[1], nm).b for nm in ("L", "N", "P", "Q", "Xa", "Xb2")]
    cx.pT = T(S, "pT", [128, 8, 128], BF16)
    cx.ktok = T(S, "ktok", [128, 512], BF16)
    cx.vtok = T(S, "vtok", [128, 512], BF16)
    cx.bv = T(S, "bv", [128, 512], BF16)
    cx.bek = T(S, "bek", [128, 512], BF16)
    cx.kdec = T(S, "kdec", [128, 512], BF16)
    cx.ub = T(S, "ub", [128, 512], F32)
    cx.wT = T(S, "wT", [128, 4, 128], BF16)
    cx.qdT = T(S, "qdT", [128, 4, 128], BF16)
    cx.gcol = T(S, "gcol", [128, 4], F32)
    cx.SB = [T(S, "SB%d" % i, [128, 128], F32) for i in range(4)]
    cx.SBb = [T(S, "SBb%d" % i, [128, 128], BF16) for i in range(4)]
    cx.ubf = [T(S, "ubf%d" % i, [128, 128], BF16) for i in range(2)]
    cx.stmp = [T(S, "stmp%d" % i, [128, 256], F32) for i in range(1)] * 2
    cx.ob = T(S, "ob", [128, 512], F32)
    cx.osq = cx.gt1
    cx.zs = T(S, "zs", [128, 512], F32)
    cx.gup = T(S, "gup", [16, 128], F32)
    cx.cng = T(S, "cng", [128, 64], F32)
    cx.cqT = T(S, "cqT", [128, 512], BF16)
    cx.ckT = T(S, "ckT", [128, 512], BF16)
    cx.la = T(S, "la", [128, 128], F32)
    cx.eb = T(S, "eb", [128, 128], F32)
    cx.enb = T(S, "enb", [128, 128], F32)
    cx.cqd = T(S, "cqd", [128, 128], BF16)
    cx.ckd = T(S, "ckd", [128, 128], BF16)
    cx.ckm = T(S, "ckm", [128, 4, 128], BF16)
    cx.ckdecT = T(S, "ckdecT", [128, 128], BF16)
    cx.ckdec = T(S, "ckdec", [128, 128], BF16)
    cx.cpT = T(S, "cpT", [128, 4, 128], BF16)
    cx.cv = T(S, "cv", [128, 256], BF16)
    cx.SC = T(S, "SC", [128, 256], F32)
    cx.SCb = T(S, "SCb", [128, 256], BF16)
    cx.oc = T(S, "oc", [128, 256], F32)
    cx.rs = T(S, "rs", [128, 256], F32)


def emit_make_T(S, cx, src_bf, src_b, dst, dst_b, t):
    ps, psb = next_ps(cx)
    psv = ps[:].bitcast(BF16)
    for k in range(8):
        TR(S, psv[:, k * 128:(k + 1) * 128], src_bf[:, k * 128:(k + 1) * 128], CB(cx, "ident"),
           [src_b, cx.cstb], [psb])
    CP(S, "dve", dst[:, :, t * 128:(t + 1) * 128],
       psv[:, 0:1024].rearrange("p (k c) -> p k c", k=8), [psb], [dst_b])


def emit_xt(S, cx, t):
    CP(S, "act", cx.xb16[:], cx.X[:, t, :], [cx.Xb[t]], [cx.xb16.b])
    emit_make_T(S, cx, cx.xb16, cx.xb16.b, cx.XT, cx.XTb[t], t)


def emit_ln(S, cx, src, srcb, dst, dstb, n, g, gb, b, bb):
    st, stb = cx.stat, cx.stat.b
    junk, junkb = cx.junk, cx.junk.b
    MEMSET(S, "dve", st[:, 0:2], 0.0, [stb])
    ACT(S, junk[:, 0:n], src[:, 0:n], AF.Identity, [srcb], [junkb, stb], accum=st[:, 0:1])
    ACT(S, junk[:, 0:n], src[:, 0:n], AF.Square, [srcb], [junkb, stb], accum=st[:, 1:2])
    TS(S, "dve", st[:, 2:4], st[:, 0:2], 1.0 / n, ALU.mult, [stb], [stb])
    TT(S, "dve", st[:, 4:5], st[:, 2:3], st[:, 2:3], ALU.mult, [stb], [stb])
    TT(S, "dve", st[:, 5:6], st[:, 3:4], st[:, 4:5], ALU.subtract, [stb], [stb])
    TS(S, "dve", st[:, 5:6], st[:, 5:6], EPS, ALU.add, [stb], [stb])
    ACT(S, st[:, 7:8], st[:, 5:6], AF.Sqrt, [stb], [stb])
    RECIP(S, st[:, 6:7], st[:, 7:8], [stb], [stb])
    TS(S, "dve", src[:, 0:n], src[:, 0:n], st[:, 2:3], ALU.subtract, [srcb, stb], [srcb],
       s2=st[:, 6:7], op1=ALU.mult)
    TT(S, "pool", src[:, 0:n], src[:, 0:n], g, ALU.mult, [srcb, gb], [srcb])
    TT(S, "pool", dst, src[:, 0:n], b, ALU.add, [srcb, bb], [dstb])


def emit_res_ln(S, cx, t, ybanks):
    for h in range(2):
        bi = ybanks[h]
        STT(S, "dve", cx.lt[:, h * 512:(h + 1) * 512], cx.X[:, t, h * 512:(h + 1) * 512], ALPHA,
            cx.ps[bi][:], ALU.mult, ALU.add, [cx.Xb[t], cx.psb[bi]], [cx.lt.b])
    emit_ln(S, cx, cx.lt, cx.lt.b, cx.X[:, t, :], cx.Xb[t], D, cx.lng[:], cx.lng.b, cx.lnb[:], cx.lnb.b)


def emit_load_ln(S, cx, g_ap, b_ap):
    DMA(S, "sp", cx.lng[:], g_ap.partition_broadcast(128), [], [cx.lng.b], cx.lng.b)
    DMA(S, "sp", cx.lnb[:], b_ap.partition_broadcast(128), [], [cx.lnb.b], cx.lnb.b)


def emit_ffn_ln(S, cx, w1, w3, w2, g_ap, b_ap, tag):
    emit_load_ln(S, cx, g_ap, b_ap)
    w1v = w1.rearrange("(kc kp) f -> kp kc f", kp=128)
    w3v = w3.rearrange("(kc kp) f -> kp kc f", kp=128)
    w2v = w2.rearrange("(fc fp) d -> fp fc d", fp=128)
    for f in range(NF):
        wb = cx.w13[f % 2]

        def _cast13(f=f, wb=wb):
            S.op("pool", lambda e: [
                e.dma_start(out=wb[:, 0], in_=w1v[:, :, f * 128:(f + 1) * 128]),
                e.dma_start(out=wb[:, 1], in_=w3v[:, :, f * 128:(f + 1) * 128])],
                writes=[wb.b], dma=wb.b, ndma=2)
        WLOAD(S, cx, "%s_w13_%d" % (tag, f), wb, wb[:].rearrange("p a k c -> p (a k c)"), _cast13)
        pa, pab = next_ps(cx)
        pb, pbb = next_ps(cx)
        rd = [wb.b] + cx.XTb
        for k in range(8):
            MM(S, pa[:], wb[:, 0, k, :], cx.XT[:, k, :], k == 0, k == 7, rd, [pab])
        for k in range(8):
            MM(S, pb[:], wb[:, 1, k, :], cx.XT[:, k, :], k == 0, k == 7, rd, [pbb])
        sa = cx.sa[f % 2]
        ACT(S, sa[:], pa[:], AF.Silu, [pab], [sa.b])
        STT(S, "dve", cx.gT[:, f, :], sa[:], 0.5, pb[:], ALU.mult, ALU.mult, [sa.b, pbb],
            [cx.gTb[f], cx.wt[0].b, cx.wt[1].b] + (cx.gt_alias if f >= 16 else []))
    for f in range(NF):
        wb = cx.w2t[f % 2]
        WLOAD(S, cx, "%s_w2_%d" % (tag, f), wb, wb[:],
              lambda f=f, wb=wb: DMA(S, "pool", wb[:], w2v[:, f, :], [], [wb.b], wb.b))
        for j in range(4):
            for h in range(2):
                bi = j * 2 + h
                MM(S, cx.ps[bi][:], cx.gT[:, f, j * 128:(j + 1) * 128], wb[:, h * 512:(h + 1) * 512],
                   f == 0, f == NF - 1, [wb.b, cx.gTb[f]], [cx.psb[bi]])
    for j in range(4):
        emit_res_ln(S, cx, j, (j * 2, j * 2 + 1))
    cx.psi = 0
    for j in range(4):
        emit_xt(S, cx, j)


C_AU, C_AV, C_BQ, C_BK, C_BV, C_BZ, C_BS, C_CQ, C_CK, C_CV, C_CR, C_CG = (
    0, 256, 512, 1024, 1536, 2048, 2560, 2576, 2704, 2832, 3088, 3344)
GELU_C = 1.5957691216057308


def emit_layer_setup(S, cx, P):
    DMA(S, "sp", cx.sgg[:], P["sgu_ln_g"].partition_broadcast(128), [], [cx.sgg.b], cx.sgg.b)
    DMA(S, "sp", cx.sgb[:], P["sgu_ln_b"].partition_broadcast(128), [], [cx.sgb.b], cx.sgb.b)
    DMA(S, "sp", cx.sbT[:], P["sgu_bT"], [], [cx.sbT.b], cx.sbT.b)
    DMA(S, "sp", cx.wstg[:], P["sgu_wT"].rearrange("h j i -> j h i"), [], [cx.wstg.b], cx.wstg.b)
    TT(S, "dve", cx.WmT[:], cx.wstg[:], bc4(CF(cx, "U")), ALU.mult, [cx.wstg.b, cx.cstb], [cx.WmT.b])
    DMA(S, "sp", cx.cw[:], P["conv_wT"].rearrange("(c p) i -> p c i", p=128), [], [cx.cw.b], cx.cw.b)
    DMA(S, "sp", cx.dtb[:], P["dt_bias"].partition_broadcast(128), [], [cx.dtb.b], cx.dtb.b)
    DMA(S, "sp", cx.nega[:], P["a_log"].partition_broadcast(128), [], [cx.nega.b], cx.nega.b)
    ACT(S, cx.nega[:], cx.nega[:], AF.Exp, [cx.nega.b], [cx.nega.b])
    TS(S, "dve", cx.nega[:], cx.nega[:], -1.0, ALU.mult, [cx.nega.b], [cx.nega.b])
    DMA(S, "sp", cx.gng[:], P["gdn_norm_g"].partition_broadcast(128), [], [cx.gng.b], cx.gng.b)
    DMA(S, "sp", cx.cng[:], P["gla_norm_g"].partition_broadcast(128), [], [cx.cng.b], cx.cng.b)
    DMA(S, "sp", cx.gup[:], P["gate_up"], [], [cx.gup.b], cx.gup.b)
    DMA(S, "sp", cx.gbb[:], P["gate_b"].partition_broadcast(128), [], [cx.gbb.b], cx.gbb.b)
    CP(S, "dve", cx.gupb[:], cx.gup[:], [cx.gup.b], [cx.gupb.b])
    MEMSET(S, "dve", cx.halo[:], 0.0, [cx.halo.b])
    for c in range(4):
        MEMSET(S, "dve", cx.SB[c][:], 0.0, [cx.SB[c].b])
        MEMSET(S, "dve", cx.SBb[c][:], 0.0, [cx.SBb[c].b])
    MEMSET(S, "dve", cx.SC[:], 0.0, [cx.SC.b])
    MEMSET(S, "dve", cx.SCb[:], 0.0, [cx.SCb.b])


def proj_fm(S, cx, winv, col0, ncols, wf):
    if ncols == 128:
        WLOAD(S, cx, "wf_%d" % col0, wf, wf[:].rearrange("p k c -> p (k c)"),
              lambda: DMA(S, "pool", wf[:, :, 0:ncols], winv[:, :, col0:col0 + ncols], [], [wf.b], wf.b))
    else:
        DMA(S, "pool", wf[:, :, 0:ncols], winv[:, :, col0:col0 + ncols], [], [wf.b], wf.b)
    ps, psb = next_ps(cx)
    for k in range(8):
        MM(S, ps[0:ncols, :], wf[:, k, 0:ncols], cx.XT[:, k, :], k == 0, k == 7, [wf.b] + cx.XTb, [psb])
    return ps, psb


def emit_rms_gate(S, cx, o, ob, nh, gtile, gate_ap, gate_b, dst, dstb, sq, sm=None):
    n = nh * 64
    sm = cx.sm if sm is None else sm
    TT(S, "dve", sq[:, 0:n], o[:, 0:n], o[:, 0:n], ALU.mult, [ob], [sq.b])
    S.op("dve", lambda e: e.tensor_reduce(out=sm[:, 64:64 + nh],
                                          in_=sq[:, 0:n].rearrange("p (h d) -> p h d", d=64),
                                          axis=AX.X, op=ALU.add), reads=[sq.b], writes=[sm.b])
    TS(S, "dve", sm[:, 64:64 + nh], sm[:, 64:64 + nh], 1.0 / 64, ALU.mult, [sm.b], [sm.b], s2=EPS, op1=ALU.add)
    ACT(S, sm[:, 72:72 + nh], sm[:, 64:64 + nh], AF.Sqrt, [sm.b], [sm.b])
    RECIP(S, sm[:, 80:80 + nh], sm[:, 72:72 + nh], [sm.b], [sm.b])
    ov = o[:, 0:n].rearrange("p (h d) -> p h d", d=64)
    TT(S, "dve", ov, ov, bcl(sm[:, 80:80 + nh], 64), ALU.mult, [ob, sm.b], [ob])
    TT(S, "dve", ov, ov, gtile[:].unsqueeze(1).to_broadcast([128, nh, 64]), ALU.mult, [ob, gtile.b], [ob])
    TT(S, "dve", dst, o[:, 0:n], gate_ap, ALU.mult, [ob, gate_b], [dstb])


def emit_mixer_group(S, cx, P):
    winv = P["w_in"].rearrange("(kc kp) f -> kp kc f", kp=128)
    U = CF(cx, "U")
    import os as _os
    STG = _os.environ.get("MIX_STAGES", "Bfm,Cfm,A,C,B").split(",")
    S.op("dve", lambda e: e.memset(cx.stat[:, 0:1], 0.0), writes=[cx.stat.b] + cx.gTb[16:] + cx.gt_alias)
    for ci in range(12 if "Bfm" in STG else 0):
        wf = cx.wf[ci % 3]
        ps, psb = proj_fm(S, cx, winv, C_BQ + 128 * ci, 128, wf)
        cit = cx.ci[ci % 2]
        CP(S, "dve", cit[:, 0:3], cx.halo[:, ci, 0:3], [cx.halo.b], [cit.b])
        CP(S, "act", cit[:, 3:515], ps[:], [psb], [cit.b])
        CP(S, "dve", cx.halo[:, ci, 0:3], cit[:, 512:515], [cit.b], [cx.halo.b])
        cy = cx.cy[ci % 2]
        TS(S, "dve", cy[:], cit[:, 0:512], cx.cw[:, ci, 0:1], ALU.mult, [cit.b, cx.cw.b], [cy.b])
        for i in range(1, 4):
            STT(S, "dve", cy[:], cit[:, i:i + 512], cx.cw[:, ci, i:i + 1], cy[:], ALU.mult, ALU.add,
                [cit.b, cx.cw.b, cy.b], [cy.b])
        cs = cx.cs[ci % 2]
        ACT(S, cs[:], cy[:], AF.Silu, [cy.b], [cs.b])
        c = ci % 4
        if ci < 8:
            sq = cx.sq[ci % 2]
            TT(S, "dve", sq[:], cs[:], cs[:], ALU.mult, [cs.b], [sq.b])
            ps2, ps2b = next_ps(cx)
            MM(S, ps2[:], CB(cx, "bm2"), sq[:], True, True, [sq.b, cx.cstb], [ps2b])
            TS(S, "dve", cx.rn[:], ps2[:], 1e-6, ALU.add, [ps2b], [cx.rn.b])
            ACT(S, cx.rn[:], cx.rn[:], AF.Sqrt, [cx.rn.b], [cx.rn.b])
            RECIP(S, cx.rn[:], cx.rn[:], [cx.rn.b], [cx.rn.b])
            if ci < 4:
                STT(S, "dve", cx.qT[:, c, :], cs[:], 0.125, cx.rn[:], ALU.mult, ALU.mult,
                    [cs.b, cx.rn.b], [cx.qTb[c]])
            else:
                TT(S, "dve", cx.kT[:, c, :], cs[:], cx.rn[:], ALU.mult, [cs.b, cx.rn.b], [cx.kTb[c]])
        else:
            CP(S, "act", cx.vT[:, c, :], cs[:], [cs.b], [cx.vTb[c]])
    if "Cfm" not in STG:
        return
    ps, psb = proj_fm(S, cx, winv, C_CQ, 128, cx.wf[0])
    CP(S, "act", cx.cqT[:], ps[:], [psb], [cx.cqT.b])
    ps, psb = proj_fm(S, cx, winv, C_CK, 128, cx.wf[1])
    CP(S, "act", cx.ckT[:], ps[:], [psb], [cx.ckT.b])
    ps, psb = proj_fm(S, cx, winv, C_CG, 16, cx.wf[2])
    CP(S, "act", cx.cgTb[:], ps[0:16, :], [psb], [cx.cgTb.b])
    wA, wZ, wC = cx.wt
    S.op("dve", lambda e: e.memset(cx.stat[:, 0:1], 0.0), writes=[cx.stat.b, wA.b, wZ.b] + cx.gTb[0:16])
    for key_, wt_, c0_ in (("wA", wA, C_AU), ("wZ", wZ, C_BZ), ("wC", wC, C_CV)):
        WLOAD(S, cx, key_, wt_, wt_[:].rearrange("p k c -> p (k c)"),
              lambda wt_=wt_, c0_=c0_: DMA(S, "pool", wt_[:], winv[:, :, c0_:c0_ + 512], [], [wt_.b], wt_.b))
    wS = cx.wf[0]
    DMA(S, "pool", wS[:, :, 0:16], winv[:, :, C_BS:C_BS + 16], [], [wS.b], wS.b)
    for j in range(4):
        ts = slice(j * 128, (j + 1) * 128)
        extra = []
        if "A" in STG:
            extra.append(emit_mixer_A(S, cx, j, ts, wA))
        if "C" in STG:
            extra.append(emit_mixer_C(S, cx, j, ts, wC))
        if "B" in STG:
            emit_mixer_B(S, cx, j, ts, wZ, wS, extra)
        else:
            for g_ in extra:
                for _ in g_:
                    pass


def proj_tm(S, cx, ts, wt, ncols):
    ps, psb = next_ps(cx)
    for k in range(8):
        MM(S, ps[:, 0:ncols], cx.XT[:, k, ts], wt[:, k, 0:ncols], k == 0, k == 7, [wt.b] + cx.XTb, [psb])
    return ps, psb


def emit_mixer_A(S, cx, j, ts, wA):
    ps, psb = proj_tm(S, cx, ts, wA, 512)
    ga, t1 = cx.ga, cx.gt1
    CP(S, "act", ga[:], ps[:], [psb], [ga.b])
    yield
    TT(S, "dve", t1[:], ga[:], ga[:], ALU.mult, [ga.b], [t1.b])
    yield
    TS(S, "dve", t1[:], t1[:], 0.044715, ALU.mult, [t1.b], [t1.b], s2=1.0, op1=ALU.add)
    yield
    TT(S, "dve", t1[:], t1[:], ga[:], ALU.mult, [t1.b, ga.b], [t1.b])
    yield
    ACT(S, t1[:], t1[:], AF.Sigmoid, [t1.b], [t1.b], scale=GELU_C)
    yield
    TT(S, "dve", ga[:], ga[:], t1[:], ALU.mult, [ga.b, t1.b], [ga.b])
    yield
    emit_ln(S, cx, ga[:, 256:512], ga.b, cx.vln[:], cx.vln.b, 256, cx.sgg[:], cx.sgg.b, cx.sgb[:], cx.sgb.b)
    yield
    pz, pzb = next_ps(cx)
    for h in range(4):
        MM(S, pz[:, h * 64:(h + 1) * 64], cx.WmT[:, h, :], cx.vln[:, h * 64:(h + 1) * 64], True, True,
           [cx.WmT.b, cx.vln.b], [pzb])
    for h in range(4):
        STT(S, "dve", cx.MIX[:, j, h * 64:(h + 1) * 64], pz[:, h * 64:(h + 1) * 64], cx.sbT[:, h:h + 1],
            ga[:, h * 64:(h + 1) * 64], ALU.add, ALU.mult, [pzb, cx.sbT.b, ga.b], [cx.MIXb[j]])


def emit_mixer_C(S, cx, j, ts, wC):
    psC, psCb = proj_tm(S, cx, ts, wC, 512)
    CP(S, "act", cx.cv[:], psC[:, 0:256], [psCb], [cx.cv.b])
    yield
    ACT(S, cx.rs[:], psC[:, 256:512], AF.Silu, [psCb], [cx.rs.b])
    yield
    pl, plb = next_ps(cx)
    MM(S, pl[:, 0:128], cx.cgTb[0:16, ts], cx.gupb[0:16, :], True, True, [cx.cgTb.b, cx.gupb.b], [plb])
    TT(S, "dve", cx.la[:], pl[:, 0:128], cx.gbb[:], ALU.add, [plb, cx.gbb.b], [cx.la.b])
    yield
    ACT(S, cx.la[:], cx.la[:], AF.Exp, [cx.la.b], [cx.la.b], scale=-1.0)
    yield
    ACT(S, cx.la[:], cx.la[:], AF.Ln, [cx.la.b], [cx.la.b], bias=1.0)
    yield
    pb, pbb = next_ps(cx)
    CP(S, "dve", cx.lah[:], cx.la[:], [cx.la.b], [cx.lah.b])
    yield
    TT(S, "dve", cx.lal[:], cx.la[:], cx.lah[:], ALU.subtract, [cx.la.b, cx.lah.b], [cx.lal.b])
    yield
    MM(S, pb[:, 0:128], cx.lah[:], CB(cx, "Un16"), True, False, [cx.lah.b, cx.cstb], [pbb])
    MM(S, pb[:, 0:128], cx.lal[:], CB(cx, "Un16"), False, True, [cx.lal.b, cx.cstb], [pbb])
    ACT(S, cx.eb[:], pb[:, 0:128], AF.Exp, [pbb], [cx.eb.b])
    yield
    ACT(S, cx.enb[:], pb[:, 0:128], AF.Exp, [pbb], [cx.enb.b], scale=-1.0)
    yield
    STT(S, "dve", cx.cqd[:], cx.cqT[:, ts], 32.0 ** -0.5, cx.eb[:], ALU.mult, ALU.mult,
        [cx.cqT.b, cx.eb.b], [cx.cqd.b])
    yield
    TT(S, "dve", cx.ckd[:], cx.ckT[:, ts], cx.enb[:], ALU.mult, [cx.ckT.b, cx.enb.b], [cx.ckd.b])
    yield
    TS(S, "dve", cx.ckdecT[:], cx.ckd[:], cx.eb[:, 127:128], ALU.mult, [cx.ckd.b, cx.eb.b], [cx.ckdecT.b])
    yield
    pt, ptb = next_ps(cx)
    ptv = pt[:].bitcast(BF16)
    TR(S, ptv[:, 0:128], cx.ckdecT[:], CB(cx, "ident"), [cx.ckdecT.b, cx.cstb], [ptb])
    CP(S, "act", cx.ckdec[:], ptv[:, 0:128], [ptb], [cx.ckdec.b])
    yield
    for h in range(4):
        TS(S, "dve", cx.ckm[:, h, :], cx.ckd[:], CF(cx, "hm")[:, h:h + 1], ALU.mult,
           [cx.ckd.b, cx.cstb], [cx.ckm.b])
    pp, ppb = next_ps(cx)
    for h in range(4):
        MM(S, pp[:, h * 128:(h + 1) * 128], cx.ckm[:, h, :], cx.cqd[:], True, True,
           [cx.ckm.b, cx.cqd.b], [ppb])
    TT(S, "dve", cx.cpT[:], pp[:].rearrange("p (h c) -> p h c", h=4), bc4(CF(cx, "U")), ALU.mult,
       [ppb, cx.cstb], [cx.cpT.b])
    yield
    po, pob = next_ps(cx)
    for h in range(4):
        hs = slice(h * 64, (h + 1) * 64)
        MM(S, po[:, hs], cx.cqd[:], cx.SCb[:, hs], True, False, [cx.cqd.b, cx.SCb.b], [pob])
        MM(S, po[:, hs], cx.cpT[:, h, :], cx.cv[:, hs], False, True, [cx.cpT.b, cx.cv.b], [pob])
    CP(S, "act", cx.oc[:], po[:, 0:256], [pob], [cx.oc.b])
    yield
    pS, pSb = next_ps(cx)
    MM(S, pS[:, 0:256], cx.ckdec[:], cx.cv[:], True, True, [cx.ckdec.b, cx.cv.b], [pSb])
    st = cx.stmp[0]
    TT(S, "dve", st[:], pS[:, 0:256], CF(cx, "bmC"), ALU.mult, [pSb, cx.cstb], [st.b])
    yield
    STT(S, "dve", cx.SC[:], cx.SC[:], cx.eb[:, 127:128], st[:], ALU.mult, ALU.add,
        [cx.SC.b, cx.eb.b, st.b], [cx.SC.b])
    yield
    CP(S, "act", cx.SCb[:], cx.SC[:], [cx.SC.b], [cx.SCb.b])
    yield
    emit_rms_gate(S, cx, cx.oc, cx.oc.b, 4, cx.cng, cx.rs[:], cx.rs.b, cx.MIX[:, j, 768:1024], cx.MIXb[j],
                  cx.osqc, cx.sm2)
    yield


def emit_mixer_B(S, cx, j, ts, wZ, wS, extra=()):
    sm = cx.sm
    cst = cx.cstb
    pZ, pZb = proj_tm(S, cx, ts, wZ, 512)
    ACT(S, cx.zs[:], pZ[:], AF.Silu, [pZb], [cx.zs.b])
    p16, p16b = proj_tm(S, cx, ts, wS, 16)
    CP(S, "act", sm[:, 0:16], p16[:, 0:16], [p16b], [sm.b])
    ACT(S, sm[:, 0:8], sm[:, 0:8], AF.Sigmoid, [sm.b], [sm.b])
    TT(S, "dve", sm[:, 8:16], sm[:, 8:16], cx.dtb[:], ALU.add, [sm.b, cx.dtb.b], [sm.b])
    ACT(S, sm[:, 8:16], sm[:, 8:16], AF.Exp, [sm.b], [sm.b])
    ACT(S, sm[:, 8:16], sm[:, 8:16], AF.Ln, [sm.b], [sm.b], bias=1.0)
    TT(S, "dve", sm[:, 8:16], sm[:, 8:16], cx.nega[:], ALU.mult, [sm.b, cx.nega.b], [sm.b])
    pg, pgb = next_ps(cx)
    smb = cx.smb
    CP(S, "dve", smb[:, 0:8], sm[:, 8:16], [sm.b], [smb.b])
    TT(S, "dve", smb[:, 8:16], sm[:, 8:16], smb[:, 0:8], ALU.subtract, [sm.b, smb.b], [smb.b])
    MM(S, pg[:, 0:8], CB(cx, "U"), smb[:, 0:8], True, False, [smb.b, cst], [pgb])
    MM(S, pg[:, 0:8], CB(cx, "U"), smb[:, 8:16], False, True, [smb.b, cst], [pgb])
    MM(S, pg[:, 8:16], CB(cx, "ones"), smb[:, 0:8], True, False, [smb.b, cst], [pgb])
    MM(S, pg[:, 8:16], CB(cx, "ones"), smb[:, 8:16], False, True, [smb.b, cst], [pgb])
    CP(S, "dve", sm[:, 16:32], pg[:, 0:16], [pgb], [sm.b])
    ACT(S, sm[:, 32:40], sm[:, 16:24], AF.Exp, [sm.b], [sm.b])
    TT(S, "dve", sm[:, 40:48], sm[:, 24:32], sm[:, 16:24], ALU.subtract, [sm.b], [sm.b])
    ACT(S, sm[:, 40:48], sm[:, 40:48], AF.Exp, [sm.b], [sm.b])
    TT(S, "dve", sm[:, 48:56], sm[:, 0:8], sm[:, 32:40], ALU.mult, [sm.b], [sm.b])
    TT(S, "dve", cx.lgb[:], bcl(smb[:, 0:8], 128), bc4(CB(cx, "ones"), 8), ALU.mult, [smb.b, cst], [cx.lgb.b])
    TT(S, "dve", cx.lgl[:], bcl(smb[:, 8:16], 128), bc4(CB(cx, "ones"), 8), ALU.mult, [smb.b, cst], [cx.lgl.b])
    import os as _os
    CUT = int(_os.environ.get("MIXB_CUT", "99"))
    if CUT <= 1:
        return
    pk, pkb = next_ps(cx)
    pkv = pk[:].bitcast(BF16)
    for c in range(4):
        TR(S, pkv[:, c * 128:(c + 1) * 128], cx.kT[:, c, ts], CB(cx, "ident"), [cx.kTb[c], cst], [pkb])
    CP(S, "act", cx.ktok[:], pkv[:, 0:512], [pkb], [cx.ktok.b])
    pv, pvb = next_ps(cx)
    pvv = pv[:].bitcast(BF16)
    for c in range(4):
        TR(S, pvv[:, c * 128:(c + 1) * 128], cx.vT[:, c, ts], CB(cx, "ident"), [cx.vTb[c], cst], [pvb])
    CP(S, "act", cx.vtok[:], pvv[:, 0:512], [pvb], [cx.vtok.b])

    def hv(t):
        return t[:].rearrange("p (h d) -> p h d", d=64)

    TT(S, "dve", hv(cx.bv), hv(cx.vtok), bcl(sm[:, 0:8], 64), ALU.mult, [cx.vtok.b, sm.b], [cx.bv.b])
    TT(S, "dve", hv(cx.bek), hv(cx.ktok), bcl(sm[:, 48:56], 64), ALU.mult, [cx.ktok.b, sm.b], [cx.bek.b])
    TT(S, "dve", hv(cx.kdec), hv(cx.ktok), bcl(sm[:, 40:48], 64), ALU.mult, [cx.ktok.b, sm.b], [cx.kdec.b])

    for par in range(2):
        TS(S, "dve", cx.kTm[:, :, par, :], cx.kT[:, :, ts], CF(cx, "pm")[:, par:par + 1], ALU.mult,
           cx.kTb + [cst], [cx.kTm.b])
        TT(S, "dve", cx.bekm[:, :, par, :], cx.bek[:].rearrange("p (c x) -> p c x", c=4),
           bc4(CB(cx, "cm")[:, par * 128:(par + 1) * 128]), ALU.mult, [cx.bek.b, cst], [cx.bekm.b])
    TT(S, "dve", cx.lgp[:], bcl(smb[:, 0:8], 64), bc4(CB(cx, "ones")[:, 0:64], 8), ALU.mult, [smb.b, cst], [cx.lgp.b])
    TT(S, "dve", cx.lgpl[:], bcl(smb[:, 8:16], 64), bc4(CB(cx, "ones")[:, 0:64], 8), ALU.mult, [smb.b, cst], [cx.lgpl.b])
    pGp, pGpb = next_ps(cx)
    lgpv = cx.lgp[:].rearrange("p (c q) d -> p c (q d)", q=2)
    lgplv = cx.lgpl[:].rearrange("p (c q) d -> p c (q d)", q=2)
    for c in range(4):
        MM(S, pGp[:, c * 128:(c + 1) * 128], lgpv[:, c, :], CB(cx, "U"), True, False, [cx.lgp.b, cst], [pGpb])
        MM(S, pGp[:, c * 128:(c + 1) * 128], lgplv[:, c, :], CB(cx, "U"), False, True, [cx.lgpl.b, cst], [pGpb])
    ACT(S, cx.EG[:], pGp[:].rearrange("p (c x) -> p c x", c=4), AF.Exp, [pGpb], [cx.EG.b])
    TT(S, "dve", cx.qdT[:], cx.qT[:, :, ts], cx.EG[:], ALU.mult, cx.qTb + [cx.EG.b], [cx.qdT.b])
    CP(S, "dve", cx.gcol[:], cx.EG[:, :, 127], [cx.EG.b], [cx.gcol.b])

    if CUT <= 2:
        return
    def hg_chain(hg, B):
        h0 = 4 * hg

        def kTh(hh):
            h = h0 + hh
            pb = 64 * (h % 2)
            return cx.kT[pb:pb + 64, h // 2, ts], cx.kTb[h // 2]

        def qTh(hh):
            h = h0 + hh
            pb = 64 * (h % 2)
            return cx.qT[pb:pb + 64, h // 2, ts], cx.qTb[h // 2]

        pG, pGb = next_ps(cx)
        pGv = pG[:].rearrange("p (h c) -> p h c", h=4)
        for hh in range(4):
            MM(S, pG[:, hh * 128:(hh + 1) * 128], cx.lgb[:, h0 + hh, :], CB(cx, "U"), True, False,
               [cx.lgb.b, cst], [pGb])
            MM(S, pG[:, hh * 128:(hh + 1) * 128], cx.lgl[:, h0 + hh, :], CB(cx, "U"), False, True,
               [cx.lgl.b, cst], [pGb])
        TT(S, "dve", B.tmpD[:], pGv, bcl(sm[:, 16 + h0:20 + h0], 128), ALU.subtract, [pGb, sm.b], [B.tmpD.b])
        yield
        TS(S, "dve", B.tmpE[:], B.tmpD[:], 0.0, ALU.max, [B.tmpD.b], [B.tmpE.b])
        yield
        ACT(S, B.E[:], B.tmpE[:], AF.Exp, [B.tmpE.b], [B.E.b], scale=-1.0)
        yield
        TS(S, "dve", B.tmpE[:], B.tmpD[:], 0.0, ALU.min, [B.tmpD.b, B.E.b], [B.tmpE.b])
        yield
        ACT(S, B.ET[:], B.tmpE[:], AF.Exp, [B.tmpE.b], [B.ET.b])
        yield
        TT(S, "dve", B.E[:], B.E[:], bc4(CF(cx, "Lstr")), ALU.mult, [B.E.b, cst], [B.E.b])
        yield
        TT(S, "dve", B.ET[:], B.ET[:], bc4(CF(cx, "U")), ALU.mult, [B.ET.b, cst], [B.ET.b])
        yield
        if CUT <= 3:
            return
        pK, pKb = next_ps(cx)
        for hh in range(4):
            h = h0 + hh
            MM(S, pK[:, hh * 128:(hh + 1) * 128], cx.kTm[:, h // 2, h % 2, :], cx.kT[:, h // 2, ts], True, True,
               [cx.kTm.b, cx.kTb[h // 2]], [pKb])
        TT(S, "dve", B.tmpD[:], pK[:].rearrange("p (h c) -> p h c", h=4), B.E[:], ALU.mult,
           [pKb, B.E.b], [B.tmpD.b])
        yield
        TT(S, "dve", B.L[:], B.tmpD[:], bcl(sm[:, h0:h0 + 4], 128), ALU.mult, [B.tmpD.b, sm.b], [B.L.b])
        yield
        pN, pNb = next_ps(cx)
        pNv = pN[:].bitcast(BF16)
        for hh in range(4):
            TR(S, pNv[:, hh * 128:(hh + 1) * 128], B.L[:, hh, :], CB(cx, "ident"), [B.L.b, cst], [pNb])
        CP(S, "act", B.N[:], pNv[:, 0:512].rearrange("p (h c) -> p h c", h=4), [pNb], [B.N.b])
        yield
        if CUT <= 4:
            return
        I4 = bc4(CB(cx, "ident"))
        TT(S, "dve", B.Xa[:], B.L[:], bc4(CB(cx, "m1")), ALU.mult, [B.L.b, cst], [B.Xa.b])
        yield
        STT(S, "dve", B.P[:], B.Xa[:], -1.0, I4, ALU.mult, ALU.add, [B.Xa.b, cst], [B.P.b])
        yield
        TT(S, "dve", B.Xb2[:], B.N[:], bc4(CB(cx, "mT1")), ALU.mult, [B.N.b, cst], [B.Xb2.b])
        yield
        STT(S, "dve", B.Q[:], B.Xb2[:], -1.0, I4, ALU.mult, ALU.add, [B.Xb2.b, cst], [B.Q.b])
        yield
        for b in MB_LIST[1:]:
            last = b == MB_LIST[-1]
            p1, p1b = next_ps(cx)
            for hh in range(4):
                MM(S, p1[:, hh * 128:(hh + 1) * 128], B.N[:, hh, :], B.P[:, hh, :], True, True,
                   [B.N.b, B.P.b], [p1b])
            TT(S, "dve", B.Xa[:], p1[:].rearrange("p (h c) -> p h c", h=4), bc4(CB(cx, "m%d" % b)), ALU.mult,
               [p1b, cst], [B.Xa.b])
            yield
            if not last:
                p2, p2b = next_ps(cx)
                for hh in range(4):
                    MM(S, p2[:, hh * 128:(hh + 1) * 128], B.L[:, hh, :], B.Q[:, hh, :], True, True,
                       [B.L.b, B.Q.b], [p2b])
                TT(S, "dve", B.Xb2[:], p2[:].rearrange("p (h c) -> p h c", h=4), bc4(CB(cx, "mT%d" % b)),
                   ALU.mult, [p2b, cst], [B.Xb2.b])
                yield
            p3, p3b = next_ps(cx)
            for hh in range(4):
                MM(S, p3[:, hh * 128:(hh + 1) * 128], B.Xa[:, hh, :], B.Q[:, hh, :], True, True,
                   [B.Xa.b, B.Q.b], [p3b])
            if not last:
                p4, p4b = next_ps(cx)
                for hh in range(4):
                    MM(S, p4[:, hh * 128:(hh + 1) * 128], B.Xb2[:, hh, :], B.P[:, hh, :], True, True,
                       [B.Xb2.b, B.P.b], [p4b])
            TT(S, "dve", B.Q[:], B.Q[:], p3[:].rearrange("p (h c) -> p h c", h=4), ALU.subtract,
               [B.Q.b, p3b], [B.Q.b])
            yield
            if not last:
                TT(S, "dve", B.P[:], B.P[:], p4[:].rearrange("p (h c) -> p h c", h=4), ALU.subtract,
                   [B.P.b, p4b], [B.P.b])
                yield
        if CUT <= 5:
            return
        pu, pub = next_ps(cx)
        for hh in range(4):
            h = h0 + hh
            MM(S, pu[:, hh * 64:(hh + 1) * 64], B.Q[:, hh, :], cx.bv[:, h * 64:(h + 1) * 64], True, True,
               [B.Q.b, cx.bv.b], [pub])
        CP(S, "act", cx.ub[:, hg * 256:(hg + 1) * 256], pu[:, 0:256], [pub], [cx.ub.b])
        yield
        pw, pwb = next_ps(cx)
        for cc in range(2):
            c = 2 * hg + cc
            for par in range(2):
                MM(S, pw[:, cc * 128:(cc + 1) * 128], cx.bekm[:, c, par, :], B.Q[:, 2 * cc + par, :],
                   par == 0, par == 1, [cx.bekm.b, B.Q.b], [pwb])
        CP(S, "act", cx.wT[:, 2 * hg:2 * hg + 2, :], pw[:, 0:256].rearrange("p (c x) -> p c x", c=2),
           [pwb], [cx.wT.b])
        yield
        pq, pqb = next_ps(cx)
        for hh in range(4):
            h = h0 + hh
            MM(S, pq[:, hh * 128:(hh + 1) * 128], cx.kTm[:, h // 2, h % 2, :], cx.qT[:, h // 2, ts], True, True,
               [cx.kTm.b, cx.qTb[h // 2]], [pqb])
        TT(S, "dve", cx.pT[:, h0:h0 + 4, :], pq[:].rearrange("p (h c) -> p h c", h=4), B.ET[:], ALU.mult,
           [pqb, B.ET.b], [cx.pT.b])
        yield

    gens = [hg_chain(0, cx.BS[0]), hg_chain(1, cx.BS[1])] + list(extra)
    while gens:
        for g_ in list(gens):
            try:
                next(g_)
            except StopIteration:
                gens.remove(g_)
    if CUT <= 6:
        return
    for c in range(4):
        SBc, SBb = cx.SB[c], cx.SBb[c]
        ubf = cx.ubf[c % 2]
        pu2, pu2b = next_ps(cx)
        MM(S, pu2[:, 0:128], cx.wT[:, c, :], SBb[:], True, True, [cx.wT.b, SBb.b], [pu2b])
        TT(S, "dve", ubf[:], cx.ub[:, c * 128:(c + 1) * 128], pu2[:, 0:128], ALU.subtract,
           [cx.ub.b, pu2b], [ubf.b])
        po, pob = next_ps(cx)
        for par in range(2):
            h = 2 * c + par
            hs = slice(par * 64, par * 64 + 64)
            MM(S, po[:, hs], cx.qdT[:, c, :], SBb[:, hs], True, False, [cx.qdT.b, SBb.b], [pob])
            MM(S, po[:, hs], cx.pT[:, h, :], ubf[:, hs], False, True, [cx.pT.b, ubf.b], [pob])
        CP(S, "act", cx.ob[:, c * 128:(c + 1) * 128], po[:, 0:128], [pob], [cx.ob.b])
        pS, pSb = next_ps(cx)
        MM(S, pS[:, 0:128], cx.kdec[:, c * 128:(c + 1) * 128], ubf[:], True, True, [cx.kdec.b, ubf.b], [pSb])
        st = cx.stmp[c % 2]
        TT(S, "dve", st[:, 0:128], pS[:, 0:128], CF(cx, "bm2"), ALU.mult, [pSb, cst], [st.b])
        STT(S, "dve", SBc[:], SBc[:], cx.gcol[:, c:c + 1], st[:, 0:128], ALU.mult, ALU.add,
            [SBc.b, cx.gcol.b, st.b], [SBc.b])
        CP(S, "act", SBb[:], SBc[:], [SBc.b], [SBb.b])
    if CUT <= 7:
        return
    emit_rms_gate(S, cx, cx.ob, cx.ob.b, 8, cx.gng, cx.zs[:], cx.zs.b, cx.MIX[:, j, 256:768], cx.MIXb[j],
                  cx.osq)


def emit_wout_ln(S, cx, P):
    for j in range(4):
        emit_make_T(S, cx, cx.MIX[:, j, :], cx.MIXb[j], cx.XT, cx.XTb[j], j)
    emit_load_ln(S, cx, P["ln2_g"], P["ln2_b"])
    wov = P["w_out"].rearrange("(kc kp) d -> kp kc d", kp=128)
    cx.psi = 0
    for k in range(8):
        wo = cx.wo[k % 2]
        WLOAD(S, cx, "wo_%d" % k, wo, wo[:],
              lambda k=k, wo=wo: DMA(S, "pool", wo[:], wov[:, k, :], [], [wo.b], wo.b))
        for j in range(4):
            for h in range(2):
                bi = j * 2 + h
                MM(S, cx.ps[bi][:], cx.XT[:, k, j * 128:(j + 1) * 128], wo[:, h * 512:(h + 1) * 512],
                   k == 0, k == 7, [wo.b, cx.XTb[j]], [cx.psb[bi]])
    for j in range(4):
        emit_res_ln(S, cx, j, (j * 2, j * 2 + 1))
    cx.psi = 0
    for j in range(4):
        emit_xt(S, cx, j)


PNAMES = ["ffn1_w1", "ffn1_w3", "ffn1_w2", "ln1_g", "ln1_b", "w_in", "sgu_ln_g", "sgu_ln_b", "sgu_wT", "sgu_bT",
          "conv_wT", "a_log", "dt_bias", "gdn_norm_g", "gate_up", "gate_b", "gla_norm_g", "w_out", "ln2_g",
          "ln2_b", "ffn2_w1", "ffn2_w3", "ffn2_w2", "ln3_g", "ln3_b"]
PSHAPES = {"ffn1_w1": [D, DFF], "ffn1_w3": [D, DFF], "ffn1_w2": [DFF, D], "ln1_g": [D], "ln1_b": [D],
           "w_in": [D, DIN], "sgu_ln_g": [256], "sgu_ln_b": [256], "sgu_wT": [4, 128, 128], "sgu_bT": [128, 4],
           "conv_wT": [1536, 4], "a_log": [8], "dt_bias": [8], "gdn_norm_g": [64], "gate_up": [16, 128],
           "gate_b": [128], "gla_norm_g": [64], "w_out": [D, D], "ln2_g": [D], "ln2_b": [D],
           "ffn2_w1": [D, DFF], "ffn2_w3": [D, DFF], "ffn2_w2": [DFF, D], "ln3_g": [D], "ln3_b": [D]}


def build_program(T_tok, depth, stop_after=None, dbg=False):
    nc = bass.Bass("TRN2", target_bir_lowering=False)
    x = nc.dram_tensor("x", [T_tok, D], F32, kind="ExternalInput").ap()
    y = nc.dram_tensor("y", [T_tok, D], F32, kind="ExternalOutput").ap()
    cst = nc.dram_tensor("cstf", [128, NCSTF], F32, kind="ExternalInput").ap()
    cstb = nc.dram_tensor("cstb", [128, NCSTB], F32, kind="ExternalInput").ap()
    prm = {}
    for n in PNAMES:
        prm[n] = nc.dram_tensor(n, [depth] + PSHAPES[n], F32, kind="ExternalInput").ap()
    xs = [nc.dram_tensor("xs%d" % i, [T_tok, D], F32, kind="Internal").ap() for i in range(2)]
    NG = T_tok // 512
    with ExitStack() as stack:
        S = Sched(nc, stack)
        cx = Ctx()
        alloc_all(S, nc, cx)
        cx.nc = nc
        cx.scr = {}
        cx.scrb = {}
        cx.first = True
        DMA(S, "sp", cx.CST[:], cst, [], [cx.cstb], cx.cstb)
        cstb2 = S.buf("cstb2")
        DMA(S, "pool", cx.CSTB[:], cstb, [], [cstb2], cstb2)
        S.op("dve", lambda e: e.memset(cx.stat[:, 0:1], 0.0), reads=[cstb2], writes=[cx.cstb, cx.stat.b])
        xsb = [[S.buf("xs%d_%d" % (i, g)) for g in range(NG)] for i in range(2)]
        yb = S.buf("y")
        for l in range(depth):
            P = {n: prm[n][l] for n in PNAMES}
            emit_layer_setup(S, cx, P)
            src = x if l == 0 else xs[(l - 1) % 2]
            dst = y if l == depth - 1 else xs[l % 2]
            srcv = src.rearrange("(g t p) d -> g p t d", p=128, t=4)
            dstv = dst.rearrange("(g t p) d -> g p t d", p=128, t=4)
            for g in range(NG):
                cx.first = (g == 0)
                for t in range(4):
                    rd = [] if l == 0 else [xsb[(l - 1) % 2][g]]
                    DMA(S, "sp", cx.X[:, t, :], srcv[g, :, t, :], rd, [cx.Xb[t]], cx.Xb[t])
                for t in range(4):
                    emit_xt(S, cx, t)
                emit_ffn_ln(S, cx, P["ffn1_w1"], P["ffn1_w3"], P["ffn1_w2"], P["ln1_g"], P["ln1_b"], "f1")
                if stop_after != "ffn1":
                    emit_mixer_group(S, cx, P)
                    if dbg and g == 0 and l == 0:
                        S.dump("mix", cx.MIX[:], cx.MIXb[3], [128, 4, D], BF16)
                    emit_wout_ln(S, cx, P)
                    if stop_after != "mix":
                        emit_ffn_ln(S, cx, P["ffn2_w1"], P["ffn2_w3"], P["ffn2_w2"], P["ln3_g"], P["ln3_b"], "f2")
                for t in range(4):
                    wr = [yb] if l == depth - 1 else [xsb[l % 2][g]]
                    DMA(S, "sp", dstv[g, :, t, :], cx.X[:, t, :], [cx.Xb[t]], wr, cx.Xb[t])
        S.finish([yb] + cx.Xb)
        S.emit()
    return nc


def host_params(inputs, depth=DEPTH):
    f = lambda a: np.ascontiguousarray(np.asarray(a, dtype=np.float32))
    p = {}
    for n in ["ffn1_w1", "ffn1_w3", "ffn1_w2", "ln1_g", "ln1_b", "w_in", "sgu_ln_g", "sgu_ln_b", "w_out",
              "ln2_g", "ln2_b", "ffn2_w1", "ffn2_w3", "ffn2_w2", "ln3_g", "ln3_b"]:
        p[n] = f(inputs[n])[:depth]
    p["sgu_wT"] = f(np.transpose(np.asarray(inputs["sgu_w"]), (0, 1, 3, 2)))[:depth]
    p["sgu_bT"] = f(np.transpose(np.asarray(inputs["sgu_b"]), (0, 2, 1)))[:depth]
    p["conv_wT"] = f(np.transpose(np.asarray(inputs["gdn_conv_w"]), (0, 2, 1)))[:depth]
    p["a_log"] = f(inputs["gdn_a_log"])[:depth]
    p["dt_bias"] = f(inputs["gdn_dt_bias"])[:depth]
    p["gdn_norm_g"] = f(inputs["gdn_norm_g"])[:depth]
    p["gate_up"] = f(inputs["gla_gate_up"])[:depth]
    p["gate_b"] = f(inputs["gla_gate_b"])[:depth]
    p["gla_norm_g"] = f(inputs["gla_norm_g"])[:depth]
    p["cstf"] = CSTF_NP
    p["cstb"] = CSTB_NP
    return p


_PROG = {}


def kernel(**inputs):
    x = np.asarray(inputs["x"], dtype=np.float32)
    B, T_tok, _ = x.shape
    p = host_params(inputs)
    key = (T_tok, DEPTH)
    if key not in _PROG:
        _PROG[key] = build_program(T_tok, DEPTH)
    nc = _PROG[key]
    in_maps = []
    for b in range(B):
        m = dict(p)
        m["x"] = np.ascontiguousarray(x[b])
        in_maps.append(m)
    res = run_bass_kernel_spmd(nc, in_maps, core_ids=list(range(B)))
    return np.stack([res.results[b]["y"] for b in range(B)], axis=0).astype(np.float32)
```

```python
import numpy as np
from contextlib import ExitStack
import concourse.bass as bass
import concourse.mybir as mybir
from concourse.bass_utils import run_bass_kernel_spmd

F32 = mybir.dt.float32
BF16 = mybir.dt.bfloat16
AF = mybir.ActivationFunctionType
ALU = mybir.AluOpType
AX = mybir.AxisListType

D = 1024
DFF = 2816
NF = DFF // 128
DEPTH = 4
TOK = 2048
NT = TOK // 128
ALPHA = (2.0 * DEPTH) ** 0.25
EPS = 1e-5
DIN = 3360


import os as _os0
SKIP_SAME = tuple(_os0.environ.get("SKIP_SAME", "pe").split(","))


class Buf:
    __slots__ = ("name", "wev", "revs", "dsem")

    def __init__(self, name):
        self.name = name
        self.wev = None
        self.revs = {}
        self.dsem = None


class Sched:
    ENG = ("pe", "act", "dve", "pool", "sp")

    def __init__(self, nc, stack):
        self.nc = nc
        self.stack = stack
        self.prog = {e: [] for e in self.ENG}
        self.esem = {e: stack.enter_context(nc.semaphore("s_" + e)) for e in self.ENG}
        self.cnt = {}
        self.seen = {e: {} for e in self.ENG}
        self.nsem = len(self.ENG)
        self.nbuf = 0
        self.dbg = []

    def buf(self, name=None):
        self.nbuf += 1
        return Buf(name or "b%d" % self.nbuf)

    def bufs(self, n, name="b"):
        return [self.buf("%s%d" % (name, i)) for i in range(n)]

    def sb(self, name, shape, dtype):
        return self.stack.enter_context(self.nc.sbuf_tensor(name, list(shape), dtype))

    def ps(self, name, shape, dtype):
        return self.stack.enter_context(self.nc.psum_tensor(name, list(shape), dtype))

    def op(self, eng, fn, reads=(), writes=(), dma=None, ndma=1):
        waits = {}

        def need(ev):
            if ev is None:
                return
            s, v = ev
            if v > waits.get(s, 0):
                waits[s] = v

        for b in reads:
            need(b.wev)
        for b in writes:
            need(b.wev)
            for ev in b.revs.values():
                need(ev)
        if dma is not None:
            if dma.dsem is None:
                self.nsem += 1
                dma.dsem = self.stack.enter_context(self.nc.semaphore("d%d" % self.nsem))
            sem = dma.dsem
            amt = 16
            total = 16 * ndma
        else:
            sem = self.esem[eng]
            amt = 1
            total = 1
        own = self.esem[eng]
        wl = []
        for s, v in waits.items():
            if s is own and eng in SKIP_SAME:
                continue
            if self.seen[eng].get(s, 0) >= v:
                continue
            self.seen[eng][s] = v
            wl.append((s, v))
        self.cnt[sem] = self.cnt.get(sem, 0) + total
        ev = (sem, self.cnt[sem])
        self.prog[eng].append((wl, fn, sem, amt))
        for b in reads:
            b.revs[sem] = ev
        for b in writes:
            b.wev = ev
            b.revs = {}
        return ev

    def dump(self, name, ap, buf, shape, dtype=F32):
        d = self.nc.dram_tensor("dbg_" + name, list(shape), dtype, kind="ExternalOutput").ap()
        db = self.buf("dbg_" + name)
        self.dbg.append(db)
        self.op("sp", lambda e: e.dma_start(out=d, in_=ap), reads=[buf], writes=[db], dma=buf)

    def finish(self, bufs):
        bufs = list(bufs) + self.dbg
        waits = {}
        for b in bufs:
            for ev in [b.wev] + list(b.revs.values()):
                if ev is not None and ev[1] > waits.get(ev[0], 0):
                    waits[ev[0]] = ev[1]
        self.prog["sp"].append((list(waits.items()), None, None, 0))

    def emit(self):
        prog = self.prog

        def mk(name):
            def f(e):
                for wl, fn, sem, amt in prog[name]:
                    for s, v in wl:
                        e.wait_ge(s, v)
                    if fn is None:
                        continue
                    r = fn(e)
                    if not isinstance(r, (list, tuple)):
                        r = [r]
                    for ins in r:
                        ins.then_inc(sem, amt)
            return f

        with self.nc.Block() as block:
            block.tensor(mk("pe"))
            block.scalar(mk("act"))
            block.vector(mk("dve"))
            block.gpsimd(mk("pool"))
            block.sync(mk("sp"))


def MM(S, out, lhsT, rhs, start, stop, rd, wr):
    S.op("pe", lambda e: e.matmul(out, lhsT=lhsT, rhs=rhs, start=start, stop=stop), reads=rd, writes=wr)


def TR(S, out, in_, ident, rd, wr):
    S.op("pe", lambda e: e.transpose(out=out, in_=in_, identity=ident), reads=rd, writes=wr)


def ACT(S, out, in_, func, rd, wr, scale=None, bias=None, accum=None):
    kw = {}
    if scale is not None:
        kw["scale"] = scale
    if bias is not None:
        kw["bias"] = bias
    if accum is not None:
        kw["accum_out"] = accum
    S.op("act", lambda e: e.activation(out=out, in_=in_, func=func, **kw), reads=rd, writes=wr)


def TT(S, eng, out, in0, in1, op, rd, wr):
    S.op(eng, lambda e: e.tensor_tensor(out=out, in0=in0, in1=in1, op=op), reads=rd, writes=wr)


def TS(S, eng, out, in0, s1, op0, rd, wr, s2=None, op1=None):
    if op1 is None:
        S.op(eng, lambda e: e.tensor_scalar(out=out, in0=in0, scalar1=s1, scalar2=None, op0=op0),
             reads=rd, writes=wr)
    else:
        S.op(eng, lambda e: e.tensor_scalar(out=out, in0=in0, scalar1=s1, scalar2=s2, op0=op0, op1=op1),
             reads=rd, writes=wr)


def STT(S, eng, out, in0, scalar, in1, op0, op1, rd, wr):
    S.op(eng, lambda e: e.scalar_tensor_tensor(out=out, in0=in0, scalar=scalar, in1=in1, op0=op0, op1=op1),
         reads=rd, writes=wr)


def CP(S, eng, out, in_, rd, wr):
    if eng == "act":
        S.op("act", lambda e: e.activation(out=out, in_=in_, func=AF.Copy), reads=rd, writes=wr)
    else:
        S.op(eng, lambda e: e.tensor_copy(out=out, in_=in_), reads=rd, writes=wr)


def RECIP(S, out, in_, rd, wr):
    S.op("dve", lambda e: e.reciprocal(out=out, in_=in_), reads=rd, writes=wr)


def DMA(S, eng, out, in_, rd, wr, dma):
    S.op(eng, lambda e: e.dma_start(out=out, in_=in_), reads=rd, writes=wr, dma=dma)


def WLOAD(S, cx, key, tile, flat_ap, cast_fn):
    if key not in cx.scr:
        n = flat_ap.shape[1]
        cx.scr[key] = cx.nc.dram_tensor("scr_" + key, [128, n], BF16).ap()
        cx.scrb[key] = S.buf("scr_" + key)
    scr, scrb = cx.scr[key], cx.scrb[key]
    if not hasattr(tile, "hw"):
        tile.hw = S.buf("hw")
    if cx.first:
        cast_fn()
        DMA(S, "sp", scr, flat_ap, [tile.b], [scrb], tile.hw)
    else:
        DMA(S, "sp", flat_ap, scr, [scrb], [tile.b], tile.hw)


def MEMSET(S, eng, ap, val, wr):
    S.op(eng, lambda e: e.memset(ap, val), writes=wr)


class Ctx:
    pass


class T:
    def __init__(self, S, name, shape, dtype):
        self.h = S.sb(name, shape, dtype)
        self.b = S.buf(name)

    def __getitem__(self, k):
        return self.h[k]

    @classmethod
    def view(cls, S, name, ap):
        o = cls.__new__(cls)
        o.h = ap
        o.b = S.buf(name)
        return o


MB_LIST = [1, 2, 4, 8, 16, 32, 64]


def make_consts():
    i = np.arange(128)
    cf = {}
    cb = {}
    cb["ident"] = np.eye(128)
    cf["U"] = (i[:, None] <= i[None, :]) * 1.0
    cf["Un16"] = (i[:, None] <= i[None, :]) * (-1.0 / 16.0)
    cf["ones"] = np.ones((128, 128))
    cf["Lstr"] = (i[:, None] > i[None, :]) * 1.0
    for b in MB_LIST:
        bi = i // b
        m = ((bi[:, None] % 2 == 1) & (bi[None, :] == bi[:, None] - 1)) * 1.0
        cb["m%d" % b] = m
        cb["mT%d" % b] = m.T.copy()
    cf["bm2"] = ((i[:, None] // 64) == (i[None, :] // 64)) * 1.0
    cb["bm2"] = cf["bm2"]
    cb["U"] = cf["U"]
    cb["Un16"] = cf["Un16"]
    cb["ones"] = cf["ones"]
    j = np.arange(256)
    cf["bmC"] = ((i[:, None] // 32) == (j[None, :] // 64)) * 1.0
    cf["hm"] = ((i[:, None] // 32) == np.arange(4)[None, :]) * 1.0
    cf["pm"] = ((i[:, None] // 64) == np.arange(2)[None, :]) * 1.0
    cb["cm"] = np.concatenate([np.tile((i[None, :] // 64 == q) * 1.0, (128, 1)) for q in range(2)], axis=1)

    def pack(cols):
        off = {}
        arrs = []
        o = 0
        for k, v in cols.items():
            off[k] = (o, v.shape[1])
            o += v.shape[1]
            arrs.append(v.astype(np.float32))
        return np.concatenate(arrs, axis=1), off

    return pack(cf), pack(cb)


(CSTF_NP, CSTF_OFF), (CSTB_NP, CSTB_OFF) = make_consts()
NCSTF = CSTF_NP.shape[1]
NCSTB = CSTB_NP.shape[1]


def next_ps(cx):
    i = cx.psi
    cx.psi = (cx.psi + 1) % 8
    return cx.ps[i], cx.psb[i]


def CF(cx, name):
    o, n = CSTF_OFF[name]
    return cx.CST[:, o:o + n]


def CB(cx, name):
    o, n = CSTB_OFF[name]
    return cx.CSTB[:, o:o + n]


def bc4(ap, n=4):
    return ap.unsqueeze(1).to_broadcast([128, n, ap.shape[1]])


def bcl(ap, m):
    return ap.unsqueeze(2).to_broadcast([128, ap.shape[1], m])


def alloc_all(S, nc, cx):
    cx.ps = [S.ps("ps%d" % i, [128, 512], F32) for i in range(8)]
    cx.psb = S.bufs(8, "ps")
    cx.psi = 0
    cx.CST = S.sb("CST", [128, NCSTF], F32)
    cx.CSTB = S.sb("CSTB", [128, NCSTB], BF16)
    cx.cstb = S.buf("cst")
    cx.X = S.sb("X", [128, 4, D], F32)
    cx.Xb = S.bufs(4, "X")
    cx.XT = S.sb("XT", [128, 8, 512], BF16)
    cx.XTb = S.bufs(4, "XT")
    cx.MIX = S.sb("MIX", [128, 4, D], BF16)
    cx.MIXb = S.bufs(4, "MIX")
    cx.lng = T(S, "lng", [128, D], F32)
    cx.lnb = T(S, "lnb", [128, D], F32)
    cx.w13 = [T(S, "w13_%d" % i, [128, 2, 8, 128], BF16) for i in range(3)]
    cx.w2t = [T(S, "w2_%d" % i, [128, D], BF16) for i in range(3)]
    cx.sa = [T(S, "sa%d" % i, [128, 512], F32) for i in range(1)] * 2
    cx.gTflat = S.sb("gT", [128, NF * 512], BF16)
    cx.gT = cx.gTflat[:, :].rearrange("p (f t) -> p f t", f=NF)
    cx.gTb = S.bufs(NF, "gT")
    cx.lt = T(S, "lt", [128, D], F32)
    cx.stat = T(S, "stat", [128, 8], F32)
    cx.xb16 = T(S, "xb16", [128, D], BF16)
    cx.junk = cx.xb16
    cx.wf = [T(S, "wf%d" % i, [128, 8, 128], BF16) for i in range(3)]
    cx.wt = [T.view(S, "wt%d" % i, cx.gTflat[:, i * 4096:(i + 1) * 4096].rearrange("p (k c) -> p k c", k=8))
             for i in range(2)] + [T(S, "wt2", [128, 8, 512], BF16)]
    cx.wo = cx.w2t
    cx.sgg = T(S, "sgg", [128, 256], F32)
    cx.sgb = T(S, "sgb", [128, 256], F32)
    cx.sbT = T(S, "sbT", [128, 4], F32)
    cx.WmT = T(S, "WmT", [128, 4, 128], BF16)
    cx.ga = T(S, "ga", [128, 512], F32)
    cx.gt1 = T(S, "gt1", [128, 512], F32)
    cx.wstg = T.view(S, "wstg", cx.ga[:].rearrange("p (h c) -> p h c", h=4))
    cx.wstg.b = cx.ga.b
    cx.rn = T.view(S, "rn", cx.gt1[:])
    cx.rn.b = cx.gt1.b
    cx.vln = T(S, "vln", [128, 256], BF16)
    cx.cw = T(S, "cw", [128, 12, 4], F32)
    cx.halo = T(S, "halo", [128, 12, 4], F32)
    cx.ci = [T(S, "ci%d" % i, [128, 516], F32) for i in range(2)]
    cx.cy = [T(S, "cy%d" % i, [128, 512], F32) for i in range(1)] * 2
    cx.cs = [T(S, "cs%d" % i, [128, 512], F32) for i in range(1)] * 2
    cx.sq = [T(S, "sq%d" % i, [128, 512], BF16) for i in range(1)] * 2
    cx.qT = S.sb("qT", [128, 4, 512], BF16)
    cx.qTb = S.bufs(4, "qT")
    cx.kT = S.sb("kT", [128, 4, 512], BF16)
    cx.kTb = S.bufs(4, "kT")
    cx.vT = S.sb("vT", [128, 4, 512], BF16)
    cx.vTb = S.bufs(4, "vT")
    cx.dtb = T(S, "dtb", [128, 8], F32)
    cx.nega = T(S, "nega", [128, 8], F32)
    cx.gng = T(S, "gng", [128, 64], F32)
    cx.sm = T(S, "sm", [128, 96], F32)
    cx.sm2 = T(S, "sm2", [128, 96], F32)
    cx.osqc = T(S, "osqc", [128, 256], F32)
    cx.lgb = T(S, "lgb", [128, 8, 128], BF16)
    cx.lgl = T(S, "lgl", [128, 8, 128], BF16)
    cx.smb = T(S, "smb", [128, 32], BF16)
    cx.kTm = T(S, "kTm", [128, 4, 2, 128], BF16)
    cx.bekm = T(S, "bekm", [128, 4, 2, 128], BF16)
    cx.lgp = T(S, "lgp", [128, 8, 64], BF16)
    cx.lgpl = T(S, "lgpl", [128, 8, 64], BF16)
    cx.lah = T(S, "lah", [128, 128], BF16)
    cx.lal = T(S, "lal", [128, 128], BF16)
    cx.gbb = T(S, "gbb", [128, 128], F32)
    cx.gupb = T(S, "gupb", [16, 128], BF16)
    cx.cgTb = T(S, "cgTb", [16, 512], BF16)
    cx.EG = T(S, "EG", [128, 4, 128], F32)
    cx.BS = []
    for q in range(2):
        B = Ctx()
        B.tmpD = T(S, "tmpD%d" % q, [128, 4, 128], F32)
        B.tmpE = T(S, "tmpE%d" % q, [128, 4, 128], F32)
        B.E = T(S, "E%d" % q, [128, 4, 128], BF16)
        B.ET = T(S, "ET%d" % q, [128, 4, 128], BF16)
        if q == 0:
            for nm in ("L", "N", "P", "Q", "Xa", "Xb2"):
                setattr(B, nm, T(S, nm + "0", [128, 4, 128], BF16))
        else:
            for i_, nm in enumerate(("L", "N", "P", "Q", "Xa", "Xb2")):
                o = 8192 + 512 * i_
                setattr(B, nm, T.view(S, nm + "1", cx.gTflat[:, o:o + 512].rearrange("p (h c) -> p h c", h=4)))
        cx.BS.append(B)
    cx.gt_alias = [getattr(cx.BS[1], nm).b for nm in ("L", "N", "P", "Q", "Xa", "Xb2")]
    cx.pT = T(S, "pT", [128, 8, 128], BF16)
    cx.ktok = T(S, "ktok", [128, 512], BF16)
    cx.vtok = T(S, "vtok", [128, 512], BF16)
    cx.bv = T(S, "bv", [128, 512], BF16)
    cx.bek = T(S, "bek", [128, 512], BF16)
    cx.kdec = T(S, "kdec", [128, 512], BF16)
    cx.ub = T(S, "ub", [128, 512], F32)
    cx.wT = T(S, "wT", [128, 4, 128], BF16)
    cx.qdT = T(S, "qdT", [128, 4, 128], BF16)
    cx.gcol = T(S, "gcol", [128, 4], F32)
    cx.SB = [T(S, "SB%d" % i, [128, 128], F32) for i in range(4)]
    cx.SBb = [T(S, "SBb%d" % i, [128, 128], BF16) for i in range(4)]
    cx.ubf = [T(S, "ubf%d" % i, [128, 128], BF16) for i in range(2)]
    cx.stmp = [T(S, "stmp%d" % i, [128, 256], F32) for i in range(1)] * 2
    cx.ob = T(S, "ob", [128, 512], F32)
    cx.osq = cx.gt1
    cx.zs = T(S, "zs", [128, 512], F32)
    cx.gup = T(S, "gup", [16, 128], F32)
    cx.cng = T(S, "cng", [128, 64], F32)
    cx.cqT = T(S, "cqT", [128, 512], BF16)
    cx.ckT = T(S, "ckT", [128, 512], BF16)
    cx.la = T(S, "la", [128, 128], F32)
    cx.eb = T(S, "eb", [128, 128], F32)
    cx.enb = T(S, "enb", [128, 128], F32)
    cx.cqd = T(S, "cqd", [128, 128], BF16)
    cx.ckd = T(S, "ckd", [128, 128], BF16)
    cx.ckm = T(S, "ckm", [128, 4, 128], BF16)
    cx.ckdecT = T(S, "ckdecT", [128, 128], BF16)
    cx.ckdec = T(S, "ckdec", [128, 128], BF16)
    cx.cpT = T(S, "cpT", [128, 4, 128], BF16)
    cx.cv = T(S, "cv", [128, 256], BF16)
    cx.SC = T(S, "SC", [128, 256], F32)
    cx.SCb = T(S, "SCb", [128, 256], BF16)
    cx.oc = T(S, "oc", [128, 256], F32)
    cx.rs = T(S, "rs", [128, 256], F32)


def emit_make_T(S, cx, src_bf, src_b, dst, dst_b, t):
    ps, psb = next_ps(cx)
    psv = ps[:].bitcast(BF16)
    for k in range(8):
        TR(S, psv[:, k * 128:(k + 1) * 128], src_bf[:, k * 128:(k + 1) * 128], CB(cx, "ident"),
           [src_b, cx.cstb], [psb])
    CP(S, "dve", dst[:, :, t * 128:(t + 1) * 128],
       psv[:, 0:1024].rearrange("p (k c) -> p k c", k=8), [psb], [dst_b])


def emit_xt(S, cx, t):
    CP(S, "act", cx.xb16[:], cx.X[:, t, :], [cx.Xb[t]], [cx.xb16.b])
    emit_make_T(S, cx, cx.xb16, cx.xb16.b, cx.XT, cx.XTb[t], t)


def emit_ln(S, cx, src, srcb, dst, dstb, n, g, gb, b, bb):
    st, stb = cx.stat, cx.stat.b
    junk, junkb = cx.junk, cx.junk.b
    MEMSET(S, "dve", st[:, 0:2], 0.0, [stb])
    ACT(S, junk[:, 0:n], src[:, 0:n], AF.Identity, [srcb], [junkb, stb], accum=st[:, 0:1])
    ACT(S, junk[:, 0:n], src[:, 0:n], AF.Square, [srcb], [junkb, stb], accum=st[:, 1:2])
    TS(S, "dve", st[:, 2:4], st[:, 0:2], 1.0 / n, ALU.mult, [stb], [stb])
    TT(S, "dve", st[:, 4:5], st[:, 2:3], st[:, 2:3], ALU.mult, [stb], [stb])
    TT(S, "dve", st[:, 5:6], st[:, 3:4], st[:, 4:5], ALU.subtract, [stb], [stb])
    ACT(S, st[:, 7:8], st[:, 5:6], AF.Ln, [stb], [stb], bias=EPS)
    ACT(S, st[:, 6:7], st[:, 7:8], AF.Exp, [stb], [stb], scale=-0.5)
    TS(S, "dve", src[:, 0:n], src[:, 0:n], st[:, 2:3], ALU.subtract, [srcb, stb], [srcb],
       s2=st[:, 6:7], op1=ALU.mult)
    TT(S, "pool", src[:, 0:n], src[:, 0:n], g, ALU.mult, [srcb, gb], [srcb])
    TT(S, "pool", dst, src[:, 0:n], b, ALU.add, [srcb, bb], [dstb])


def emit_res_ln(S, cx, t, ybanks):
    for h in range(2):
        bi = ybanks[h]
        STT(S, "dve", cx.lt[:, h * 512:(h + 1) * 512], cx.X[:, t, h * 512:(h + 1) * 512], ALPHA,
            cx.ps[bi][:], ALU.mult, ALU.add, [cx.Xb[t], cx.psb[bi]], [cx.lt.b])
    emit_ln(S, cx, cx.lt, cx.lt.b, cx.X[:, t, :], cx.Xb[t], D, cx.lng[:], cx.lng.b, cx.lnb[:], cx.lnb.b)


def emit_load_ln(S, cx, g_ap, b_ap):
    DMA(S, "sp", cx.lng[:], g_ap.partition_broadcast(128), [], [cx.lng.b], cx.lng.b)
    DMA(S, "sp", cx.lnb[:], b_ap.partition_broadcast(128), [], [cx.lnb.b], cx.lnb.b)


def emit_ffn_ln(S, cx, w1, w3, w2, g_ap, b_ap, tag):
    emit_load_ln(S, cx, g_ap, b_ap)
    w1v = w1.rearrange("(kc kp) f -> kp kc f", kp=128)
    w3v = w3.rearrange("(kc kp) f -> kp kc f", kp=128)
    w2v = w2.rearrange("(fc fp) d -> fp fc d", fp=128)
    for f in range(NF):
        wb = cx.w13[f % 3]

        def _cast13(f=f, wb=wb):
            S.op("pool", lambda e: [
                e.dma_start(out=wb[:, 0], in_=w1v[:, :, f * 128:(f + 1) * 128]),
                e.dma_start(out=wb[:, 1], in_=w3v[:, :, f * 128:(f + 1) * 128])],
                writes=[wb.b], dma=wb.b, ndma=2)
        WLOAD(S, cx, "%s_w13_%d" % (tag, f), wb, wb[:].rearrange("p a k c -> p (a k c)"), _cast13)
        pa, pab = next_ps(cx)
        pb, pbb = next_ps(cx)
        rd = [wb.b] + cx.XTb
        for k in range(8):
            MM(S, pa[:], wb[:, 0, k, :], cx.XT[:, k, :], k == 0, k == 7, rd, [pab])
        for k in range(8):
            MM(S, pb[:], wb[:, 1, k, :], cx.XT[:, k, :], k == 0, k == 7, rd, [pbb])
        sa = cx.sa[f % 2]
        ACT(S, sa[:], pa[:], AF.Silu, [pab], [sa.b])
        STT(S, "dve", cx.gT[:, f, :], sa[:], 0.5, pb[:], ALU.mult, ALU.mult, [sa.b, pbb],
            [cx.gTb[f], cx.wt[0].b, cx.wt[1].b] + (cx.gt_alias if f >= 16 else []))
    for f in range(NF):
        wb = cx.w2t[f % 3]
        WLOAD(S, cx, "%s_w2_%d" % (tag, f), wb, wb[:],
              lambda f=f, wb=wb: DMA(S, "pool", wb[:], w2v[:, f, :], [], [wb.b], wb.b))
        for j in range(4):
            for h in range(2):
                bi = j * 2 + h
                MM(S, cx.ps[bi][:], cx.gT[:, f, j * 128:(j + 1) * 128], wb[:, h * 512:(h + 1) * 512],
                   f == 0, f == NF - 1, [wb.b, cx.gTb[f]], [cx.psb[bi]])
    for j in range(4):
        emit_res_ln(S, cx, j, (j * 2, j * 2 + 1))
    cx.psi = 0
    for j in range(4):
        emit_xt(S, cx, j)


C_AU, C_AV, C_BQ, C_BK, C_BV, C_BZ, C_BS, C_CQ, C_CK, C_CV, C_CR, C_CG = (
    0, 256, 512, 1024, 1536, 2048, 2560, 2576, 2704, 2832, 3088, 3344)
GELU_C = 1.5957691216057308


def emit_layer_setup(S, cx, P):
    DMA(S, "sp", cx.sgg[:], P["sgu_ln_g"].partition_broadcast(128), [], [cx.sgg.b], cx.sgg.b)
    DMA(S, "sp", cx.sgb[:], P["sgu_ln_b"].partition_broadcast(128), [], [cx.sgb.b], cx.sgb.b)
    DMA(S, "sp", cx.sbT[:], P["sgu_bT"], [], [cx.sbT.b], cx.sbT.b)
    DMA(S, "sp", cx.wstg[:], P["sgu_wT"].rearrange("h j i -> j h i"), [], [cx.wstg.b], cx.wstg.b)
    TT(S, "dve", cx.WmT[:], cx.wstg[:], bc4(CF(cx, "U")), ALU.mult, [cx.wstg.b, cx.cstb], [cx.WmT.b])
    DMA(S, "sp", cx.cw[:], P["conv_wT"].rearrange("(c p) i -> p c i", p=128), [], [cx.cw.b], cx.cw.b)
    DMA(S, "sp", cx.dtb[:], P["dt_bias"].partition_broadcast(128), [], [cx.dtb.b], cx.dtb.b)
    DMA(S, "sp", cx.nega[:], P["a_log"].partition_broadcast(128), [], [cx.nega.b], cx.nega.b)
    ACT(S, cx.nega[:], cx.nega[:], AF.Exp, [cx.nega.b], [cx.nega.b])
    TS(S, "dve", cx.nega[:], cx.nega[:], -1.0, ALU.mult, [cx.nega.b], [cx.nega.b])
    DMA(S, "sp", cx.gng[:], P["gdn_norm_g"].partition_broadcast(128), [], [cx.gng.b], cx.gng.b)
    DMA(S, "sp", cx.cng[:], P["gla_norm_g"].partition_broadcast(128), [], [cx.cng.b], cx.cng.b)
    DMA(S, "sp", cx.gup[:], P["gate_up"], [], [cx.gup.b], cx.gup.b)
    DMA(S, "sp", cx.gbb[:], P["gate_b"].partition_broadcast(128), [], [cx.gbb.b], cx.gbb.b)
    CP(S, "dve", cx.gupb[:], cx.gup[:], [cx.gup.b], [cx.gupb.b])
    MEMSET(S, "dve", cx.halo[:], 0.0, [cx.halo.b])
    for c in range(4):
        MEMSET(S, "dve", cx.SB[c][:], 0.0, [cx.SB[c].b])
        MEMSET(S, "dve", cx.SBb[c][:], 0.0, [cx.SBb[c].b])
    MEMSET(S, "dve", cx.SC[:], 0.0, [cx.SC.b])
    MEMSET(S, "dve", cx.SCb[:], 0.0, [cx.SCb.b])


def proj_fm(S, cx, winv, col0, ncols, wf):
    if ncols == 128:
        WLOAD(S, cx, "wf_%d" % col0, wf, wf[:].rearrange("p k c -> p (k c)"),
              lambda: DMA(S, "pool", wf[:, :, 0:ncols], winv[:, :, col0:col0 + ncols], [], [wf.b], wf.b))
    else:
        DMA(S, "pool", wf[:, :, 0:ncols], winv[:, :, col0:col0 + ncols], [], [wf.b], wf.b)
    ps, psb = next_ps(cx)
    for k in range(8):
        MM(S, ps[0:ncols, :], wf[:, k, 0:ncols], cx.XT[:, k, :], k == 0, k == 7, [wf.b] + cx.XTb, [psb])
    return ps, psb


def emit_rms_gate(S, cx, o, ob, nh, gtile, gate_ap, gate_b, dst, dstb, sq, sm=None):
    n = nh * 64
    sm = cx.sm if sm is None else sm
    TT(S, "dve", sq[:, 0:n], o[:, 0:n], o[:, 0:n], ALU.mult, [ob], [sq.b])
    S.op("dve", lambda e: e.tensor_reduce(out=sm[:, 64:64 + nh],
                                          in_=sq[:, 0:n].rearrange("p (h d) -> p h d", d=64),
                                          axis=AX.X, op=ALU.add), reads=[sq.b], writes=[sm.b])
    ACT(S, sm[:, 72:72 + nh], sm[:, 64:64 + nh], AF.Ln, [sm.b], [sm.b], scale=1.0 / 64, bias=EPS)
    ACT(S, sm[:, 80:80 + nh], sm[:, 72:72 + nh], AF.Exp, [sm.b], [sm.b], scale=-0.5)
    ov = o[:, 0:n].rearrange("p (h d) -> p h d", d=64)
    TT(S, "dve", ov, ov, bcl(sm[:, 80:80 + nh], 64), ALU.mult, [ob, sm.b], [ob])
    TT(S, "dve", ov, ov, gtile[:].unsqueeze(1).to_broadcast([128, nh, 64]), ALU.mult, [ob, gtile.b], [ob])
    TT(S, "dve", dst, o[:, 0:n], gate_ap, ALU.mult, [ob, gate_b], [dstb])


def emit_mixer_group(S, cx, P):
    winv = P["w_in"].rearrange("(kc kp) f -> kp kc f", kp=128)
    U = CF(cx, "U")
    import os as _os
    STG = _os.environ.get("MIX_STAGES", "Bfm,Cfm,A,C,B").split(",")
    S.op("dve", lambda e: e.memset(cx.stat[:, 0:1], 0.0), writes=[cx.stat.b] + cx.gTb[16:] + cx.gt_alias)
    for ci in range(12 if "Bfm" in STG else 0):
        wf = cx.wf[ci % 3]
        ps, psb = proj_fm(S, cx, winv, C_BQ + 128 * ci, 128, wf)
        cit = cx.ci[ci % 2]
        CP(S, "dve", cit[:, 0:3], cx.halo[:, ci, 0:3], [cx.halo.b], [cit.b])
        CP(S, "act", cit[:, 3:515], ps[:], [psb], [cit.b])
        CP(S, "dve", cx.halo[:, ci, 0:3], cit[:, 512:515], [cit.b], [cx.halo.b])
        cy = cx.cy[ci % 2]
        TS(S, "dve", cy[:], cit[:, 0:512], cx.cw[:, ci, 0:1], ALU.mult, [cit.b, cx.cw.b], [cy.b])
        for i in range(1, 4):
            STT(S, "dve", cy[:], cit[:, i:i + 512], cx.cw[:, ci, i:i + 1], cy[:], ALU.mult, ALU.add,
                [cit.b, cx.cw.b, cy.b], [cy.b])
        cs = cx.cs[ci % 2]
        ACT(S, cs[:], cy[:], AF.Silu, [cy.b], [cs.b])
        c = ci % 4
        if ci < 8:
            sq = cx.sq[ci % 2]
            TT(S, "dve", sq[:], cs[:], cs[:], ALU.mult, [cs.b], [sq.b])
            ps2, ps2b = next_ps(cx)
            MM(S, ps2[:], CB(cx, "bm2"), sq[:], True, True, [sq.b, cx.cstb], [ps2b])
            ACT(S, cx.rn[:], ps2[:], AF.Ln, [ps2b], [cx.rn.b], bias=1e-6)
            ACT(S, cx.rn[:], cx.rn[:], AF.Exp, [cx.rn.b], [cx.rn.b], scale=-0.5)
            if ci < 4:
                STT(S, "dve", cx.qT[:, c, :], cs[:], 0.125, cx.rn[:], ALU.mult, ALU.mult,
                    [cs.b, cx.rn.b], [cx.qTb[c]])
            else:
                TT(S, "dve", cx.kT[:, c, :], cs[:], cx.rn[:], ALU.mult, [cs.b, cx.rn.b], [cx.kTb[c]])
        else:
            CP(S, "act", cx.vT[:, c, :], cs[:], [cs.b], [cx.vTb[c]])
    if "Cfm" not in STG:
        return
    ps, psb = proj_fm(S, cx, winv, C_CQ, 128, cx.wf[0])
    CP(S, "act", cx.cqT[:], ps[:], [psb], [cx.cqT.b])
    ps, psb = proj_fm(S, cx, winv, C_CK, 128, cx.wf[1])
    CP(S, "act", cx.ckT[:], ps[:], [psb], [cx.ckT.b])
    ps, psb = proj_fm(S, cx, winv, C_CG, 16, cx.wf[2])
    CP(S, "act", cx.cgTb[:], ps[0:16, :], [psb], [cx.cgTb.b])
    wA, wZ, wC = cx.wt
    S.op("dve", lambda e: e.memset(cx.stat[:, 0:1], 0.0), writes=[cx.stat.b, wA.b, wZ.b] + cx.gTb[0:16])
    for key_, wt_, c0_ in (("wA", wA, C_AU), ("wZ", wZ, C_BZ), ("wC", wC, C_CV)):
        WLOAD(S, cx, key_, wt_, wt_[:].rearrange("p k c -> p (k c)"),
              lambda wt_=wt_, c0_=c0_: DMA(S, "pool", wt_[:], winv[:, :, c0_:c0_ + 512], [], [wt_.b], wt_.b))
    wS = cx.wf[0]
    DMA(S, "pool", wS[:, :, 0:16], winv[:, :, C_BS:C_BS + 16], [], [wS.b], wS.b)
    for j in range(4):
        ts = slice(j * 128, (j + 1) * 128)
        extra = []
        if "A" in STG:
            extra.append(emit_mixer_A(S, cx, j, ts, wA))
        if "C" in STG:
            extra.append(emit_mixer_C(S, cx, j, ts, wC))
        if "B" in STG:
            emit_mixer_B(S, cx, j, ts, wZ, wS, extra)
        else:
            for g_ in extra:
                for _ in g_:
                    pass


def proj_tm(S, cx, ts, wt, ncols):
    ps, psb = next_ps(cx)
    for k in range(8):
        MM(S, ps[:, 0:ncols], cx.XT[:, k, ts], wt[:, k, 0:ncols], k == 0, k == 7, [wt.b] + cx.XTb, [psb])
    return ps, psb


def emit_mixer_A(S, cx, j, ts, wA):
    ps, psb = proj_tm(S, cx, ts, wA, 512)
    ga, t1 = cx.ga, cx.gt1
    ACT(S, ga[:], ps[:], AF.Copy, [psb], [ga.b], scale=0.5)
    yield
    TT(S, "dve", t1[:], ga[:], ga[:], ALU.mult, [ga.b], [t1.b])
    yield
    TS(S, "dve", t1[:], t1[:], 4 * 0.044715, ALU.mult, [t1.b], [t1.b], s2=1.0, op1=ALU.add)
    yield
    TT(S, "dve", t1[:], t1[:], ga[:], ALU.mult, [t1.b, ga.b], [t1.b])
    yield
    ACT(S, t1[:], t1[:], AF.Tanh, [t1.b], [t1.b], scale=GELU_C)
    yield
    STT(S, "dve", ga[:], t1[:], 1.0, ga[:], ALU.add, ALU.mult, [ga.b, t1.b], [ga.b])
    yield
    emit_ln(S, cx, ga[:, 256:512], ga.b, cx.vln[:], cx.vln.b, 256, cx.sgg[:], cx.sgg.b, cx.sgb[:], cx.sgb.b)
    yield
    pz, pzb = next_ps(cx)
    for h in range(4):
        MM(S, pz[:, h * 64:(h + 1) * 64], cx.WmT[:, h, :], cx.vln[:, h * 64:(h + 1) * 64], True, True,
           [cx.WmT.b, cx.vln.b], [pzb])
    for h in range(4):
        STT(S, "dve", cx.MIX[:, j, h * 64:(h + 1) * 64], pz[:, h * 64:(h + 1) * 64], cx.sbT[:, h:h + 1],
            ga[:, h * 64:(h + 1) * 64], ALU.add, ALU.mult, [pzb, cx.sbT.b, ga.b], [cx.MIXb[j]])


def emit_mixer_C(S, cx, j, ts, wC):
    psC, psCb = proj_tm(S, cx, ts, wC, 512)
    CP(S, "act", cx.cv[:], psC[:, 0:256], [psCb], [cx.cv.b])
    yield
    ACT(S, cx.rs[:], psC[:, 256:512], AF.Silu, [psCb], [cx.rs.b])
    yield
    pl, plb = next_ps(cx)
    MM(S, pl[:, 0:128], cx.cgTb[0:16, ts], cx.gupb[0:16, :], True, True, [cx.cgTb.b, cx.gupb.b], [plb])
    TT(S, "dve", cx.la[:], pl[:, 0:128], cx.gbb[:], ALU.add, [plb, cx.gbb.b], [cx.la.b])
    yield
    ACT(S, cx.la[:], cx.la[:], AF.Exp, [cx.la.b], [cx.la.b], scale=-1.0)
    yield
    ACT(S, cx.la[:], cx.la[:], AF.Ln, [cx.la.b], [cx.la.b], bias=1.0)
    yield
    pb, pbb = next_ps(cx)
    CP(S, "dve", cx.lah[:], cx.la[:], [cx.la.b], [cx.lah.b])
    yield
    TT(S, "dve", cx.lal[:], cx.la[:], cx.lah[:], ALU.subtract, [cx.la.b, cx.lah.b], [cx.lal.b])
    yield
    MM(S, pb[:, 0:128], cx.lah[:], CB(cx, "Un16"), True, False, [cx.lah.b, cx.cstb], [pbb])
    MM(S, pb[:, 0:128], cx.lal[:], CB(cx, "Un16"), False, True, [cx.lal.b, cx.cstb], [pbb])
    ACT(S, cx.eb[:], pb[:, 0:128], AF.Exp, [pbb], [cx.eb.b])
    yield
    ACT(S, cx.enb[:], pb[:, 0:128], AF.Exp, [pbb], [cx.enb.b], scale=-1.0)
    yield
    STT(S, "dve", cx.cqd[:], cx.cqT[:, ts], 32.0 ** -0.5, cx.eb[:], ALU.mult, ALU.mult,
        [cx.cqT.b, cx.eb.b], [cx.cqd.b])
    yield
    TT(S, "dve", cx.ckd[:], cx.ckT[:, ts], cx.enb[:], ALU.mult, [cx.ckT.b, cx.enb.b], [cx.ckd.b])
    yield
    TS(S, "dve", cx.ckdecT[:], cx.ckd[:], cx.eb[:, 127:128], ALU.mult, [cx.ckd.b, cx.eb.b], [cx.ckdecT.b])
    yield
    pt, ptb = next_ps(cx)
    ptv = pt[:].bitcast(BF16)
    TR(S, ptv[:, 0:128], cx.ckdecT[:], CB(cx, "ident"), [cx.ckdecT.b, cx.cstb], [ptb])
    CP(S, "act", cx.ckdec[:], ptv[:, 0:128], [ptb], [cx.ckdec.b])
    yield
    for h in range(4):
        TS(S, "dve", cx.ckm[:, h, :], cx.ckd[:], CF(cx, "hm")[:, h:h + 1], ALU.mult,
           [cx.ckd.b, cx.cstb], [cx.ckm.b])
    pp, ppb = next_ps(cx)
    for h in range(4):
        MM(S, pp[:, h * 128:(h + 1) * 128], cx.ckm[:, h, :], cx.cqd[:], True, True,
           [cx.ckm.b, cx.cqd.b], [ppb])
    TT(S, "dve", cx.cpT[:], pp[:].rearrange("p (h c) -> p h c", h=4), bc4(CF(cx, "U")), ALU.mult,
       [ppb, cx.cstb], [cx.cpT.b])
    yield
    po, pob = next_ps(cx)
    for h in range(4):
        hs = slice(h * 64, (h + 1) * 64)
        MM(S, po[:, hs], cx.cqd[:], cx.SCb[:, hs], True, False, [cx.cqd.b, cx.SCb.b], [pob])
        MM(S, po[:, hs], cx.cpT[:, h, :], cx.cv[:, hs], False, True, [cx.cpT.b, cx.cv.b], [pob])
    CP(S, "act", cx.oc[:], po[:, 0:256], [pob], [cx.oc.b])
    yield
    pS, pSb = next_ps(cx)
    MM(S, pS[:, 0:256], cx.ckdec[:], cx.cv[:], True, True, [cx.ckdec.b, cx.cv.b], [pSb])
    st = cx.stmp[0]
    TT(S, "dve", st[:], pS[:, 0:256], CF(cx, "bmC"), ALU.mult, [pSb, cx.cstb], [st.b])
    yield
    STT(S, "dve", cx.SC[:], cx.SC[:], cx.eb[:, 127:128], st[:], ALU.mult, ALU.add,
        [cx.SC.b, cx.eb.b, st.b], [cx.SC.b])
    yield
    CP(S, "act", cx.SCb[:], cx.SC[:], [cx.SC.b], [cx.SCb.b])
    yield
    emit_rms_gate(S, cx, cx.oc, cx.oc.b, 4, cx.cng, cx.rs[:], cx.rs.b, cx.MIX[:, j, 768:1024], cx.MIXb[j],
                  cx.osqc, cx.sm2)
    yield


def emit_mixer_B(S, cx, j, ts, wZ, wS, extra=()):
    sm = cx.sm
    cst = cx.cstb
    pZ, pZb = proj_tm(S, cx, ts, wZ, 512)
    ACT(S, cx.zs[:], pZ[:], AF.Silu, [pZb], [cx.zs.b])
    p16, p16b = proj_tm(S, cx, ts, wS, 16)
    CP(S, "act", sm[:, 0:16], p16[:, 0:16], [p16b], [sm.b])
    ACT(S, sm[:, 0:8], sm[:, 0:8], AF.Tanh, [sm.b], [sm.b], scale=0.5)
    TS(S, "dve", sm[:, 0:8], sm[:, 0:8], 0.5, ALU.mult, [sm.b], [sm.b], s2=0.5, op1=ALU.add)
    TT(S, "dve", sm[:, 8:16], sm[:, 8:16], cx.dtb[:], ALU.add, [sm.b, cx.dtb.b], [sm.b])
    ACT(S, sm[:, 8:16], sm[:, 8:16], AF.Exp, [sm.b], [sm.b])
    ACT(S, sm[:, 8:16], sm[:, 8:16], AF.Ln, [sm.b], [sm.b], bias=1.0)
    TT(S, "dve", sm[:, 8:16], sm[:, 8:16], cx.nega[:], ALU.mult, [sm.b, cx.nega.b], [sm.b])
    pg, pgb = next_ps(cx)
    smb = cx.smb
    CP(S, "dve", smb[:, 0:8], sm[:, 8:16], [sm.b], [smb.b])
    TT(S, "dve", smb[:, 8:16], sm[:, 8:16], smb[:, 0:8], ALU.subtract, [sm.b, smb.b], [smb.b])
    MM(S, pg[:, 0:8], CB(cx, "U"), smb[:, 0:8], True, False, [smb.b, cst], [pgb])
    MM(S, pg[:, 0:8], CB(cx, "U"), smb[:, 8:16], False, True, [smb.b, cst], [pgb])
    MM(S, pg[:, 8:16], CB(cx, "ones"), smb[:, 0:8], True, False, [smb.b, cst], [pgb])
    MM(S, pg[:, 8:16], CB(cx, "ones"), smb[:, 8:16], False, True, [smb.b, cst], [pgb])
    CP(S, "dve", sm[:, 16:32], pg[:, 0:16], [pgb], [sm.b])
    ACT(S, sm[:, 32:40], sm[:, 16:24], AF.Exp, [sm.b], [sm.b])
    TT(S, "dve", sm[:, 40:48], sm[:, 24:32], sm[:, 16:24], ALU.subtract, [sm.b], [sm.b])
    ACT(S, sm[:, 40:48], sm[:, 40:48], AF.Exp, [sm.b], [sm.b])
    TT(S, "dve", sm[:, 48:56], sm[:, 0:8], sm[:, 32:40], ALU.mult, [sm.b], [sm.b])
    TT(S, "dve", cx.lgb[:], bcl(smb[:, 0:8], 128), bc4(CB(cx, "ones"), 8), ALU.mult, [smb.b, cst], [cx.lgb.b])
    TT(S, "dve", cx.lgl[:], bcl(smb[:, 8:16], 128), bc4(CB(cx, "ones"), 8), ALU.mult, [smb.b, cst], [cx.lgl.b])
    import os as _os
    CUT = int(_os.environ.get("MIXB_CUT", "99"))
    if CUT <= 1:
        return
    pk, pkb = next_ps(cx)
    pkv = pk[:].bitcast(BF16)
    for c in range(4):
        TR(S, pkv[:, c * 128:(c + 1) * 128], cx.kT[:, c, ts], CB(cx, "ident"), [cx.kTb[c], cst], [pkb])
    CP(S, "act", cx.ktok[:], pkv[:, 0:512], [pkb], [cx.ktok.b])
    pv, pvb = next_ps(cx)
    pvv = pv[:].bitcast(BF16)
    for c in range(4):
        TR(S, pvv[:, c * 128:(c + 1) * 128], cx.vT[:, c, ts], CB(cx, "ident"), [cx.vTb[c], cst], [pvb])
    CP(S, "act", cx.vtok[:], pvv[:, 0:512], [pvb], [cx.vtok.b])

    def hv(t):
        return t[:].rearrange("p (h d) -> p h d", d=64)

    TT(S, "dve", hv(cx.bv), hv(cx.vtok), bcl(sm[:, 0:8], 64), ALU.mult, [cx.vtok.b, sm.b], [cx.bv.b])
    TT(S, "dve", hv(cx.bek), hv(cx.ktok), bcl(sm[:, 48:56], 64), ALU.mult, [cx.ktok.b, sm.b], [cx.bek.b])
    TT(S, "dve", hv(cx.kdec), hv(cx.ktok), bcl(sm[:, 40:48], 64), ALU.mult, [cx.ktok.b, sm.b], [cx.kdec.b])

    for par in range(2):
        TS(S, "dve", cx.kTm[:, :, par, :], cx.kT[:, :, ts], CF(cx, "pm")[:, par:par + 1], ALU.mult,
           cx.kTb + [cst], [cx.kTm.b])
        TT(S, "dve", cx.bekm[:, :, par, :], cx.bek[:].rearrange("p (c x) -> p c x", c=4),
           bc4(CB(cx, "cm")[:, par * 128:(par + 1) * 128]), ALU.mult, [cx.bek.b, cst], [cx.bekm.b])
    TT(S, "dve", cx.lgp[:], bcl(smb[:, 0:8], 64), bc4(CB(cx, "ones")[:, 0:64], 8), ALU.mult, [smb.b, cst], [cx.lgp.b])
    TT(S, "dve", cx.lgpl[:], bcl(smb[:, 8:16], 64), bc4(CB(cx, "ones")[:, 0:64], 8), ALU.mult, [smb.b, cst], [cx.lgpl.b])
    pGp, pGpb = next_ps(cx)
    lgpv = cx.lgp[:].rearrange("p (c q) d -> p c (q d)", q=2)
    lgplv = cx.lgpl[:].rearrange("p (c q) d -> p c (q d)", q=2)
    for c in range(4):
        MM(S, pGp[:, c * 128:(c + 1) * 128], lgpv[:, c, :], CB(cx, "U"), True, False, [cx.lgp.b, cst], [pGpb])
        MM(S, pGp[:, c * 128:(c + 1) * 128], lgplv[:, c, :], CB(cx, "U"), False, True, [cx.lgpl.b, cst], [pGpb])
    ACT(S, cx.EG[:], pGp[:].rearrange("p (c x) -> p c x", c=4), AF.Exp, [pGpb], [cx.EG.b])
    TT(S, "dve", cx.qdT[:], cx.qT[:, :, ts], cx.EG[:], ALU.mult, cx.qTb + [cx.EG.b], [cx.qdT.b])
    CP(S, "dve", cx.gcol[:], cx.EG[:, :, 127], [cx.EG.b], [cx.gcol.b])

    if CUT <= 2:
        return
    def hg_chain(hg, B):
        h0 = 4 * hg

        def kTh(hh):
            h = h0 + hh
            pb = 64 * (h % 2)
            return cx.kT[pb:pb + 64, h // 2, ts], cx.kTb[h // 2]

        def qTh(hh):
            h = h0 + hh
            pb = 64 * (h % 2)
            return cx.qT[pb:pb + 64, h // 2, ts], cx.qTb[h // 2]

        pG, pGb = next_ps(cx)
        pGv = pG[:].rearrange("p (h c) -> p h c", h=4)
        for hh in range(4):
            MM(S, pG[:, hh * 128:(hh + 1) * 128], cx.lgb[:, h0 + hh, :], CB(cx, "U"), True, False,
               [cx.lgb.b, cst], [pGb])
            MM(S, pG[:, hh * 128:(hh + 1) * 128], cx.lgl[:, h0 + hh, :], CB(cx, "U"), False, True,
               [cx.lgl.b, cst], [pGb])
        TT(S, "dve", B.tmpD[:], pGv, bcl(sm[:, 16 + h0:20 + h0], 128), ALU.subtract, [pGb, sm.b], [B.tmpD.b])
        yield
        TS(S, "dve", B.tmpE[:], B.tmpD[:], 0.0, ALU.max, [B.tmpD.b], [B.tmpE.b])
        yield
        ACT(S, B.E[:], B.tmpE[:], AF.Exp, [B.tmpE.b], [B.E.b], scale=-1.0)
        yield
        TS(S, "dve", B.tmpE[:], B.tmpD[:], 0.0, ALU.min, [B.tmpD.b, B.E.b], [B.tmpE.b])
        yield
        ACT(S, B.ET[:], B.tmpE[:], AF.Exp, [B.tmpE.b], [B.ET.b])
        yield
        TT(S, "dve", B.E[:], B.E[:], bc4(CF(cx, "Lstr")), ALU.mult, [B.E.b, cst], [B.E.b])
        yield
        TT(S, "dve", B.ET[:], B.ET[:], bc4(CF(cx, "U")), ALU.mult, [B.ET.b, cst], [B.ET.b])
        yield
        if CUT <= 3:
            return
        pK, pKb = next_ps(cx)
        for hh in range(4):
            h = h0 + hh
            MM(S, pK[:, hh * 128:(hh + 1) * 128], cx.kTm[:, h // 2, h % 2, :], cx.kT[:, h // 2, ts], True, True,
               [cx.kTm.b, cx.kTb[h // 2]], [pKb])
        TT(S, "dve", B.tmpD[:], pK[:].rearrange("p (h c) -> p h c", h=4), B.E[:], ALU.mult,
           [pKb, B.E.b], [B.tmpD.b])
        yield
        TT(S, "dve", B.L[:], B.tmpD[:], bcl(sm[:, h0:h0 + 4], 128), ALU.mult, [B.tmpD.b, sm.b], [B.L.b])
        yield
        pN, pNb = next_ps(cx)
        pNv = pN[:].bitcast(BF16)
        for hh in range(4):
            TR(S, pNv[:, hh * 128:(hh + 1) * 128], B.L[:, hh, :], CB(cx, "ident"), [B.L.b, cst], [pNb])
        CP(S, "act", B.N[:], pNv[:, 0:512].rearrange("p (h c) -> p h c", h=4), [pNb], [B.N.b])
        yield
        if CUT <= 4:
            return
        I4 = bc4(CB(cx, "ident"))
        TT(S, "dve", B.Xa[:], B.L[:], bc4(CB(cx, "m1")), ALU.mult, [B.L.b, cst], [B.Xa.b])
        yield
        STT(S, "dve", B.P[:], B.Xa[:], -1.0, I4, ALU.mult, ALU.add, [B.Xa.b, cst], [B.P.b])
        yield
        TT(S, "dve", B.Xb2[:], B.N[:], bc4(CB(cx, "mT1")), ALU.mult, [B.N.b, cst], [B.Xb2.b])
        yield
        STT(S, "dve", B.Q[:], B.Xb2[:], -1.0, I4, ALU.mult, ALU.add, [B.Xb2.b, cst], [B.Q.b])
        yield
        for b in MB_LIST[1:]:
            last = b == MB_LIST[-1]
            p1, p1b = next_ps(cx)
            for hh in range(4):
                MM(S, p1[:, hh * 128:(hh + 1) * 128], B.N[:, hh, :], B.P[:, hh, :], True, True,
                   [B.N.b, B.P.b], [p1b])
            TT(S, "dve", B.Xa[:], p1[:].rearrange("p (h c) -> p h c", h=4), bc4(CB(cx, "m%d" % b)), ALU.mult,
               [p1b, cst], [B.Xa.b])
            yield
            if not last:
                p2, p2b = next_ps(cx)
                for hh in range(4):
                    MM(S, p2[:, hh * 128:(hh + 1) * 128], B.L[:, hh, :], B.Q[:, hh, :], True, True,
                       [B.L.b, B.Q.b], [p2b])
                TT(S, "dve", B.Xb2[:], p2[:].rearrange("p (h c) -> p h c", h=4), bc4(CB(cx, "mT%d" % b)),
                   ALU.mult, [p2b, cst], [B.Xb2.b])
                yield
            p3, p3b = next_ps(cx)
            for hh in range(4):
                MM(S, p3[:, hh * 128:(hh + 1) * 128], B.Xa[:, hh, :], B.Q[:, hh, :], True, True,
                   [B.Xa.b, B.Q.b], [p3b])
            if not last:
                p4, p4b = next_ps(cx)
                for hh in range(4):
                    MM(S, p4[:, hh * 128:(hh + 1) * 128], B.Xb2[:, hh, :], B.P[:, hh, :], True, True,
                       [B.Xb2.b, B.P.b], [p4b])
            TT(S, "dve", B.Q[:], B.Q[:], p3[:].rearrange("p (h c) -> p h c", h=4), ALU.subtract,
               [B.Q.b, p3b], [B.Q.b])
            yield
            if not last:
                TT(S, "dve", B.P[:], B.P[:], p4[:].rearrange("p (h c) -> p h c", h=4), ALU.subtract,
                   [B.P.b, p4b], [B.P.b])
                yield
        if CUT <= 5:
            return
        pu, pub = next_ps(cx)
        for hh in range(4):
            h = h0 + hh
            MM(S, pu[:, hh * 64:(hh + 1) * 64], B.Q[:, hh, :], cx.bv[:, h * 64:(h + 1) * 64], True, True,
               [B.Q.b, cx.bv.b], [pub])
        CP(S, "act", cx.ub[:, hg * 256:(hg + 1) * 256], pu[:, 0:256], [pub], [cx.ub.b])
        yield
        pw, pwb = next_ps(cx)
        for cc in range(2):
            c = 2 * hg + cc
            for par in range(2):
                MM(S, pw[:, cc * 128:(cc + 1) * 128], cx.bekm[:, c, par, :], B.Q[:, 2 * cc + par, :],
                   par == 0, par == 1, [cx.bekm.b, B.Q.b], [pwb])
        CP(S, "act", cx.wT[:, 2 * hg:2 * hg + 2, :], pw[:, 0:256].rearrange("p (c x) -> p c x", c=2),
           [pwb], [cx.wT.b])
        yield
        pq, pqb = next_ps(cx)
        for hh in range(4):
            h = h0 + hh
            MM(S, pq[:, hh * 128:(hh + 1) * 128], cx.kTm[:, h // 2, h % 2, :], cx.qT[:, h // 2, ts], True, True,
               [cx.kTm.b, cx.qTb[h // 2]], [pqb])
        TT(S, "dve", cx.pT[:, h0:h0 + 4, :], pq[:].rearrange("p (h c) -> p h c", h=4), B.ET[:], ALU.mult,
           [pqb, B.ET.b], [cx.pT.b])
        yield

    gens = [hg_chain(0, cx.BS[0]), hg_chain(1, cx.BS[1])] + list(extra)
    while gens:
        for g_ in list(gens):
            try:
                next(g_)
            except StopIteration:
                gens.remove(g_)
    if CUT <= 6:
        return
    for c in range(4):
        SBc, SBb = cx.SB[c], cx.SBb[c]
        ubf = cx.ubf[c % 2]
        pu2, pu2b = next_ps(cx)
        MM(S, pu2[:, 0:128], cx.wT[:, c, :], SBb[:], True, True, [cx.wT.b, SBb.b], [pu2b])
        TT(S, "dve", ubf[:], cx.ub[:, c * 128:(c + 1) * 128], pu2[:, 0:128], ALU.subtract,
           [cx.ub.b, pu2b], [ubf.b])
        po, pob = next_ps(cx)
        for par in range(2):
            h = 2 * c + par
            hs = slice(par * 64, par * 64 + 64)
            MM(S, po[:, hs], cx.qdT[:, c, :], SBb[:, hs], True, False, [cx.qdT.b, SBb.b], [pob])
            MM(S, po[:, hs], cx.pT[:, h, :], ubf[:, hs], False, True, [cx.pT.b, ubf.b], [pob])
        CP(S, "act", cx.ob[:, c * 128:(c + 1) * 128], po[:, 0:128], [pob], [cx.ob.b])
        pS, pSb = next_ps(cx)
        MM(S, pS[:, 0:128], cx.kdec[:, c * 128:(c + 1) * 128], ubf[:], True, True, [cx.kdec.b, ubf.b], [pSb])
        st = cx.stmp[c % 2]
        TT(S, "dve", st[:, 0:128], pS[:, 0:128], CF(cx, "bm2"), ALU.mult, [pSb, cst], [st.b])
        STT(S, "dve", SBc[:], SBc[:], cx.gcol[:, c:c + 1], st[:, 0:128], ALU.mult, ALU.add,
            [SBc.b, cx.gcol.b, st.b], [SBc.b])
        CP(S, "act", SBb[:], SBc[:], [SBc.b], [SBb.b])
    if CUT <= 7:
        return
    emit_rms_gate(S, cx, cx.ob, cx.ob.b, 8, cx.gng, cx.zs[:], cx.zs.b, cx.MIX[:, j, 256:768], cx.MIXb[j],
                  cx.osq)


def emit_wout_ln(S, cx, P):
    for j in range(4):
        emit_make_T(S, cx, cx.MIX[:, j, :], cx.MIXb[j], cx.XT, cx.XTb[j], j)
    emit_load_ln(S, cx, P["ln2_g"], P["ln2_b"])
    wov = P["w_out"].rearrange("(kc kp) d -> kp kc d", kp=128)
    cx.psi = 0
    for k in range(8):
        wo = cx.wo[k % 3]
        WLOAD(S, cx, "wo_%d" % k, wo, wo[:],
              lambda k=k, wo=wo: DMA(S, "pool", wo[:], wov[:, k, :], [], [wo.b], wo.b))
        for j in range(4):
            for h in range(2):
                bi = j * 2 + h
                MM(S, cx.ps[bi][:], cx.XT[:, k, j * 128:(j + 1) * 128], wo[:, h * 512:(h + 1) * 512],
                   k == 0, k == 7, [wo.b, cx.XTb[j]], [cx.psb[bi]])
    for j in range(4):
        emit_res_ln(S, cx, j, (j * 2, j * 2 + 1))
    cx.psi = 0
    for j in range(4):
        emit_xt(S, cx, j)


PNAMES = ["ffn1_w1", "ffn1_w3", "ffn1_w2", "ln1_g", "ln1_b", "w_in", "sgu_ln_g", "sgu_ln_b", "sgu_wT", "sgu_bT",
          "conv_wT", "a_log", "dt_bias", "gdn_norm_g", "gate_up", "gate_b", "gla_norm_g", "w_out", "ln2_g",
          "ln2_b", "ffn2_w1", "ffn2_w3", "ffn2_w2", "ln3_g", "ln3_b"]
PSHAPES = {"ffn1_w1": [D, DFF], "ffn1_w3": [D, DFF], "ffn1_w2": [DFF, D], "ln1_g": [D], "ln1_b": [D],
           "w_in": [D, DIN], "sgu_ln_g": [256], "sgu_ln_b": [256], "sgu_wT": [4, 128, 128], "sgu_bT": [128, 4],
           "conv_wT": [1536, 4], "a_log": [8], "dt_bias": [8], "gdn_norm_g": [64], "gate_up": [16, 128],
           "gate_b": [128], "gla_norm_g": [64], "w_out": [D, D], "ln2_g": [D], "ln2_b": [D],
           "ffn2_w1": [D, DFF], "ffn2_w3": [D, DFF], "ffn2_w2": [DFF, D], "ln3_g": [D], "ln3_b": [D]}


def build_program(T_tok, depth, stop_after=None, dbg=False):
    nc = bass.Bass("TRN2", target_bir_lowering=False)
    x = nc.dram_tensor("x", [T_tok, D], F32, kind="ExternalInput").ap()
    y = nc.dram_tensor("y", [T_tok, D], F32, kind="ExternalOutput").ap()
    cst = nc.dram_tensor("cstf", [128, NCSTF], F32, kind="ExternalInput").ap()
    cstb = nc.dram_tensor("cstb", [128, NCSTB], F32, kind="ExternalInput").ap()
    prm = {}
    for n in PNAMES:
        prm[n] = nc.dram_tensor(n, [depth] + PSHAPES[n], F32, kind="ExternalInput").ap()
    xs = [nc.dram_tensor("xs%d" % i, [T_tok, D], F32, kind="Internal").ap() for i in range(2)]
    NG = T_tok // 512
    with ExitStack() as stack:
        S = Sched(nc, stack)
        cx = Ctx()
        alloc_all(S, nc, cx)
        cx.nc = nc
        cx.scr = {}
        cx.scrb = {}
        cx.first = True
        DMA(S, "sp", cx.CST[:], cst, [], [cx.cstb], cx.cstb)
        cstb2 = S.buf("cstb2")
        DMA(S, "pool", cx.CSTB[:], cstb, [], [cstb2], cstb2)
        S.op("dve", lambda e: e.memset(cx.stat[:, 0:1], 0.0), reads=[cstb2], writes=[cx.cstb, cx.stat.b])
        xsb = [[S.buf("xs%d_%d" % (i, g)) for g in range(NG)] for i in range(2)]
        yb = S.buf("y")
        for l in range(depth):
            P = {n: prm[n][l] for n in PNAMES}
            emit_layer_setup(S, cx, P)
            src = x if l == 0 else xs[(l - 1) % 2]
            dst = y if l == depth - 1 else xs[l % 2]
            srcv = src.rearrange("(g t p) d -> g p t d", p=128, t=4)
            dstv = dst.rearrange("(g t p) d -> g p t d", p=128, t=4)
            for g in range(NG):
                cx.first = (g == 0)
                for t in range(4):
                    rd = [] if l == 0 else [xsb[(l - 1) % 2][g]]
                    DMA(S, "sp", cx.X[:, t, :], srcv[g, :, t, :], rd, [cx.Xb[t]], cx.Xb[t])
                for t in range(4):
                    emit_xt(S, cx, t)
                emit_ffn_ln(S, cx, P["ffn1_w1"], P["ffn1_w3"], P["ffn1_w2"], P["ln1_g"], P["ln1_b"], "f1")
                if stop_after != "ffn1":
                    emit_mixer_group(S, cx, P)
                    if dbg and g == 0 and l == 0:
                        S.dump("mix", cx.MIX[:], cx.MIXb[3], [128, 4, D], BF16)
                    emit_wout_ln(S, cx, P)
                    if stop_after != "mix":
                        emit_ffn_ln(S, cx, P["ffn2_w1"], P["ffn2_w3"], P["ffn2_w2"], P["ln3_g"], P["ln3_b"], "f2")
                for t in range(4):
                    wr = [yb] if l == depth - 1 else [xsb[l % 2][g]]
                    DMA(S, "sp", dstv[g, :, t, :], cx.X[:, t, :], [cx.Xb[t]], wr, cx.Xb[t])
        S.finish([yb] + cx.Xb)
        S.emit()
    return nc


def host_params(inputs, depth=DEPTH):
    f = lambda a: np.ascontiguousarray(np.asarray(a, dtype=np.float32))
    p = {}
    for n in ["ffn1_w1", "ffn1_w3", "ffn1_w2", "ln1_g", "ln1_b", "w_in", "sgu_ln_g", "sgu_ln_b", "w_out",
              "ln2_g", "ln2_b", "ffn2_w1", "ffn2_w3", "ffn2_w2", "ln3_g", "ln3_b"]:
        p[n] = f(inputs[n])[:depth]
    p["sgu_wT"] = f(np.transpose(np.asarray(inputs["sgu_w"]), (0, 1, 3, 2)))[:depth]
    p["sgu_bT"] = f(np.transpose(np.asarray(inputs["sgu_b"]), (0, 2, 1)))[:depth]
    p["conv_wT"] = f(np.transpose(np.asarray(inputs["gdn_conv_w"]), (0, 2, 1)))[:depth]
    p["a_log"] = f(inputs["gdn_a_log"])[:depth]
    p["dt_bias"] = f(inputs["gdn_dt_bias"])[:depth]
    p["gdn_norm_g"] = f(inputs["gdn_norm_g"])[:depth]
    p["gate_up"] = f(inputs["gla_gate_up"])[:depth]
    p["gate_b"] = f(inputs["gla_gate_b"])[:depth]
    p["gla_norm_g"] = f(inputs["gla_norm_g"])[:depth]
    p["cstf"] = CSTF_NP
    p["cstb"] = CSTB_NP
    return p


_PROG = {}


def kernel(**inputs):
    x = np.asarray(inputs["x"], dtype=np.float32)
    B, T_tok, _ = x.shape
    p = host_params(inputs)
    key = (T_tok, DEPTH)
    if key not in _PROG:
        _PROG[key] = build_program(T_tok, DEPTH)
    nc = _PROG[key]
    in_maps = []
    for b in range(B):
        m = dict(p)
        m["x"] = np.ascontiguousarray(x[b])
        in_maps.append(m)
    res = run_bass_kernel_spmd(nc, in_maps, core_ids=list(range(B)))
    return np.stack([res.results[b]["y"] for b in range(B)], axis=0).astype(np.float32)
```

```python
import numpy as np
from contextlib import ExitStack
import concourse.bass as bass
import concourse.mybir as mybir
from concourse.bass_utils import run_bass_kernel_spmd

F32 = mybir.dt.float32
BF16 = mybir.dt.bfloat16
AF = mybir.ActivationFunctionType
ALU = mybir.AluOpType
AX = mybir.AxisListType

D = 1024
DFF = 2816
NF = DFF // 128
DEPTH = 4
TOK = 2048
NT = TOK // 128
ALPHA = (2.0 * DEPTH) ** 0.25
EPS = 1e-5
DIN = 3360


import os as _os0
SKIP_SAME = tuple(_os0.environ.get("SKIP_SAME", "pe").split(","))


class Buf:
    __slots__ = ("name", "wev", "revs", "dsem")

    def __init__(self, name):
        self.name = name
        self.wev = None
        self.revs = {}
        self.dsem = None


class Sched:
    ENG = ("pe", "act", "dve", "pool", "sp")

    def __init__(self, nc, stack):
        self.nc = nc
        self.stack = stack
        self.prog = {e: [] for e in self.ENG}
        self.esem = {e: stack.enter_context(nc.semaphore("s_" + e)) for e in self.ENG}
        self.cnt = {}
        self.seen = {e: {} for e in self.ENG}
        self.nsem = len(self.ENG)
        self.nbuf = 0
        self.dbg = []

    def buf(self, name=None):
        self.nbuf += 1
        return Buf(name or "b%d" % self.nbuf)

    def bufs(self, n, name="b"):
        return [self.buf("%s%d" % (name, i)) for i in range(n)]

    def sb(self, name, shape, dtype):
        return self.stack.enter_context(self.nc.sbuf_tensor(name, list(shape), dtype))

    def ps(self, name, shape, dtype):
        return self.stack.enter_context(self.nc.psum_tensor(name, list(shape), dtype))

    def op(self, eng, fn, reads=(), writes=(), dma=None, ndma=1):
        waits = {}

        def need(ev):
            if ev is None:
                return
            s, v = ev
            if v > waits.get(s, 0):
                waits[s] = v

        for b in reads:
            need(b.wev)
        for b in writes:
            need(b.wev)
            for ev in b.revs.values():
                need(ev)
        if dma is not None:
            if dma.dsem is None:
                self.nsem += 1
                dma.dsem = self.stack.enter_context(self.nc.semaphore("d%d" % self.nsem))
            sem = dma.dsem
            amt = 16
            total = 16 * ndma
        else:
            sem = self.esem[eng]
            amt = 1
            total = 1
        own = self.esem[eng]
        wl = []
        for s, v in waits.items():
            if s is own and eng in SKIP_SAME:
                continue
            if self.seen[eng].get(s, 0) >= v:
                continue
            self.seen[eng][s] = v
            wl.append((s, v))
        self.cnt[sem] = self.cnt.get(sem, 0) + total
        ev = (sem, self.cnt[sem])
        self.prog[eng].append((wl, fn, sem, amt))
        for b in reads:
            b.revs[sem] = ev
        for b in writes:
            b.wev = ev
            b.revs = {}
        return ev

    def dump(self, name, ap, buf, shape, dtype=F32):
        d = self.nc.dram_tensor("dbg_" + name, list(shape), dtype, kind="ExternalOutput").ap()
        db = self.buf("dbg_" + name)
        self.dbg.append(db)
        self.op("sp", lambda e: e.dma_start(out=d, in_=ap), reads=[buf], writes=[db], dma=buf)

    def finish(self, bufs):
        bufs = list(bufs) + self.dbg
        waits = {}
        for b in bufs:
            for ev in [b.wev] + list(b.revs.values()):
                if ev is not None and ev[1] > waits.get(ev[0], 0):
                    waits[ev[0]] = ev[1]
        self.prog["sp"].append((list(waits.items()), None, None, 0))

    def emit(self):
        prog = self.prog

        def mk(name):
            def f(e):
                for wl, fn, sem, amt in prog[name]:
                    for s, v in wl:
                        e.wait_ge(s, v)
                    if fn is None:
                        continue
                    r = fn(e)
                    if not isinstance(r, (list, tuple)):
                        r = [r]
                    for ins in r:
                        ins.then_inc(sem, amt)
            return f

        with self.nc.Block() as block:
            block.tensor(mk("pe"))
            block.scalar(mk("act"))
            block.vector(mk("dve"))
            block.gpsimd(mk("pool"))
            block.sync(mk("sp"))


def MM(S, out, lhsT, rhs, start, stop, rd, wr):
    S.op("pe", lambda e: e.matmul(out, lhsT=lhsT, rhs=rhs, start=start, stop=stop), reads=rd, writes=wr)


def TR(S, out, in_, ident, rd, wr):
    S.op("pe", lambda e: e.transpose(out=out, in_=in_, identity=ident), reads=rd, writes=wr)


def ACT(S, out, in_, func, rd, wr, scale=None, bias=None, accum=None):
    kw = {}
    if scale is not None:
        kw["scale"] = scale
    if bias is not None:
        kw["bias"] = bias
    if accum is not None:
        kw["accum_out"] = accum
    S.op("act", lambda e: e.activation(out=out, in_=in_, func=func, **kw), reads=rd, writes=wr)


def TT(S, eng, out, in0, in1, op, rd, wr):
    S.op(eng, lambda e: e.tensor_tensor(out=out, in0=in0, in1=in1, op=op), reads=rd, writes=wr)


def TS(S, eng, out, in0, s1, op0, rd, wr, s2=None, op1=None):
    if op1 is None:
        S.op(eng, lambda e: e.tensor_scalar(out=out, in0=in0, scalar1=s1, scalar2=None, op0=op0),
             reads=rd, writes=wr)
    else:
        S.op(eng, lambda e: e.tensor_scalar(out=out, in0=in0, scalar1=s1, scalar2=s2, op0=op0, op1=op1),
             reads=rd, writes=wr)


def STT(S, eng, out, in0, scalar, in1, op0, op1, rd, wr):
    S.op(eng, lambda e: e.scalar_tensor_tensor(out=out, in0=in0, scalar=scalar, in1=in1, op0=op0, op1=op1),
         reads=rd, writes=wr)


def CP(S, eng, out, in_, rd, wr):
    if eng == "act":
        S.op("act", lambda e: e.activation(out=out, in_=in_, func=AF.Copy), reads=rd, writes=wr)
    else:
        S.op(eng, lambda e: e.tensor_copy(out=out, in_=in_), reads=rd, writes=wr)


def RECIP(S, out, in_, rd, wr):
    S.op("dve", lambda e: e.reciprocal(out=out, in_=in_), reads=rd, writes=wr)


def DMA(S, eng, out, in_, rd, wr, dma):
    S.op(eng, lambda e: e.dma_start(out=out, in_=in_), reads=rd, writes=wr, dma=dma)


def WLOAD(S, cx, key, tile, flat_ap, cast_fn):
    if key not in cx.scr:
        n = flat_ap.shape[1]
        cx.scr[key] = cx.nc.dram_tensor("scr_" + key, [128, n], BF16).ap()
        cx.scrb[key] = S.buf("scr_" + key)
    scr, scrb = cx.scr[key], cx.scrb[key]
    if not hasattr(tile, "hw"):
        tile.hw = S.buf("hw")
    if cx.first:
        cast_fn()
        DMA(S, "sp", scr, flat_ap, [tile.b], [scrb], tile.hw)
    else:
        DMA(S, "sp", flat_ap, scr, [scrb], [tile.b], tile.hw)


def MEMSET(S, eng, ap, val, wr):
    S.op(eng, lambda e: e.memset(ap, val), writes=wr)


class Ctx:
    pass


class T:
    def __init__(self, S, name, shape, dtype):
        self.h = S.sb(name, shape, dtype)
        self.b = S.buf(name)

    def __getitem__(self, k):
        return self.h[k]

    @classmethod
    def view(cls, S, name, ap):
        o = cls.__new__(cls)
        o.h = ap
        o.b = S.buf(name)
        return o


MB_LIST = [1, 2, 4, 8, 16, 32, 64]


def make_consts():
    i = np.arange(128)
    cf = {}
    cb = {}
    cb["ident"] = np.eye(128)
    cf["U"] = (i[:, None] <= i[None, :]) * 1.0
    cf["Un16"] = (i[:, None] <= i[None, :]) * (-1.0 / 16.0)
    cf["ones"] = np.ones((128, 128))
    cf["Lstr"] = (i[:, None] > i[None, :]) * 1.0
    for b in MB_LIST:
        bi = i // b
        m = ((bi[:, None] % 2 == 1) & (bi[None, :] == bi[:, None] - 1)) * 1.0
        cb["m%d" % b] = m
        cb["mT%d" % b] = m.T.copy()
    cf["bm2"] = ((i[:, None] // 64) == (i[None, :] // 64)) * 1.0
    cb["bm2"] = cf["bm2"]
    cb["U"] = cf["U"]
    cb["Un16"] = cf["Un16"]
    cb["ones"] = cf["ones"]
    j = np.arange(256)
    cf["bmC"] = ((i[:, None] // 32) == (j[None, :] // 64)) * 1.0
    cf["hm"] = ((i[:, None] // 32) == np.arange(4)[None, :]) * 1.0
    cf["pm"] = ((i[:, None] // 64) == np.arange(2)[None, :]) * 1.0
    cb["cm"] = np.concatenate([np.tile((i[None, :] // 64 == q) * 1.0, (128, 1)) for q in range(2)], axis=1)

    def pack(cols):
        off = {}
        arrs = []
        o = 0
        for k, v in cols.items():
            off[k] = (o, v.shape[1])
            o += v.shape[1]
            arrs.append(v.astype(np.float32))
        return np.concatenate(arrs, axis=1), off

    return pack(cf), pack(cb)


(CSTF_NP, CSTF_OFF), (CSTB_NP, CSTB_OFF) = make_consts()
NCSTF = CSTF_NP.shape[1]
NCSTB = CSTB_NP.shape[1]


def next_ps(cx):
    i = cx.psi
    cx.psi = (cx.psi + 1) % 8
    return cx.ps[i], cx.psb[i]


def CF(cx, name):
    o, n = CSTF_OFF[name]
    return cx.CST[:, o:o + n]


def CB(cx, name):
    o, n = CSTB_OFF[name]
    return cx.CSTB[:, o:o + n]


def bc4(ap, n=4):
    return ap.unsqueeze(1).to_broadcast([128, n, ap.shape[1]])


def bcl(ap, m):
    return ap.unsqueeze(2).to_broadcast([128, ap.shape[1], m])


def alloc_all(S, nc, cx):
    cx.ps = [S.ps("ps%d" % i, [128, 512], F32) for i in range(8)]
    cx.psb = S.bufs(8, "ps")
    cx.psi = 0
    cx.CST = S.sb("CST", [128, NCSTF], F32)
    cx.CSTB = S.sb("CSTB", [128, NCSTB], BF16)
    cx.cstb = S.buf("cst")
    cx.X = S.sb("X", [128, 4, D], F32)
    cx.Xb = S.bufs(4, "X")
    cx.XT = S.sb("XT", [128, 8, 512], BF16)
    cx.XTb = S.bufs(4, "XT")
    cx.MIX = S.sb("MIX", [128, 4, D], BF16)
    cx.MIXb = S.bufs(4, "MIX")
    cx.lng = T(S, "lng", [128, D], F32)
    cx.lnb = T(S, "lnb", [128, D], F32)
    cx.w13 = [T(S, "w13_%d" % i, [128, 2, 8, 128], BF16) for i in range(3)]
    cx.w2t = [T(S, "w2_%d" % i, [128, D], BF16) for i in range(3)]
    cx.sa = [T(S, "sa%d" % i, [128, 512], F32) for i in range(1)] * 2
    cx.gTflat = S.sb("gT", [128, NF * 512], BF16)
    cx.gT = cx.gTflat[:, :].rearrange("p (f t) -> p f t", f=NF)
    cx.gTb = S.bufs(NF, "gT")
    cx.stat4 = [T(S, "stat4_%d" % i, [128, 8], F32) for i in range(4)]
    cx.stat = T(S, "stat", [128, 8], F32)
    cx.xb16 = T(S, "xb16", [128, D], BF16)
    cx.junk = cx.xb16
    cx.wf = [T(S, "wf%d" % i, [128, 8, 128], BF16) for i in range(3)]
    cx.wt = [T.view(S, "wt%d" % i, cx.gTflat[:, i * 4096:(i + 1) * 4096].rearrange("p (k c) -> p k c", k=8))
             for i in range(2)] + [T(S, "wt2", [128, 8, 512], BF16)]
    cx.wo = cx.w2t
    cx.sgg = T(S, "sgg", [128, 256], F32)
    cx.sgb = T(S, "sgb", [128, 256], F32)
    cx.sbT = T(S, "sbT", [128, 4], F32)
    cx.WmT = T(S, "WmT", [128, 4, 128], BF16)
    cx.ga = T(S, "ga", [128, 512], F32)
    cx.gt1 = T(S, "gt1", [128, 512], F32)
    cx.wstg = T.view(S, "wstg", cx.ga[:].rearrange("p (h c) -> p h c", h=4))
    cx.wstg.b = cx.ga.b
    cx.rn = T.view(S, "rn", cx.gt1[:])
    cx.rn.b = cx.gt1.b
    cx.vln = T(S, "vln", [128, 256], BF16)
    cx.cw = T(S, "cw", [128, 12, 4], F32)
    cx.halo = T(S, "halo", [128, 12, 4], F32)
    cx.ci = [T(S, "ci%d" % i, [128, 516], F32) for i in range(2)]
    cx.cy = [T(S, "cy%d" % i, [128, 512], F32) for i in range(2)]
    cx.cs = [T(S, "cs%d" % i, [128, 512], F32) for i in range(2)]
    cx.sq = [T(S, "sq%d" % i, [128, 512], BF16) for i in range(1)] * 2
    cx.qT = S.sb("qT", [128, 4, 512], BF16)
    cx.qTb = S.bufs(4, "qT")
    cx.kT = S.sb("kT", [128, 4, 512], BF16)
    cx.kTb = S.bufs(4, "kT")
    cx.vT = S.sb("vT", [128, 4, 512], BF16)
    cx.vTb = S.bufs(4, "vT")
    cx.dtb = T(S, "dtb", [128, 8], F32)
    cx.nega = T(S, "nega", [128, 8], F32)
    cx.gng = T(S, "gng", [128, 64], F32)
    cx.sm = T(S, "sm", [128, 96], F32)
    cx.sm2 = T(S, "sm2", [128, 96], F32)
    cx.osqc = T(S, "osqc", [128, 256], F32)
    cx.lgb = T(S, "lgb", [128, 8, 128], BF16)
    cx.lgl = T(S, "lgl", [128, 8, 128], BF16)
    cx.smb = T(S, "smb", [128, 32], BF16)
    cx.kTm = T(S, "kTm", [128, 4, 2, 128], BF16)
    cx.bekm = T(S, "bekm", [128, 4, 2, 128], BF16)
    cx.lgp = T(S, "lgp", [128, 8, 64], BF16)
    cx.lgpl = T(S, "lgpl", [128, 8, 64], BF16)
    cx.lah = T(S, "lah", [128, 128], BF16)
    cx.lal = T(S, "lal", [128, 128], BF16)
    cx.gbb = T(S, "gbb", [128, 128], F32)
    cx.gupb = T(S, "gupb", [16, 128], BF16)
    cx.cgTb = T(S, "cgTb", [16, 512], BF16)
    cx.EG = T(S, "EG", [128, 4, 128], F32)
    cx.BS = []
    for q in range(2):
        B = Ctx()
        B.tmpD = T(S, "tmpD%d" % q, [128, 4, 128], F32)
        B.tmpE = T(S, "tmpE%d" % q, [128, 4, 128], F32)
        B.E = T(S, "E%d" % q, [128, 4, 128], BF16)
        B.ET = T(S, "ET%d" % q, [128, 4, 128], BF16)
        if q == 0:
            for nm in ("L", "N", "P", "Q", "Xa", "Xb2"):
                setattr(B, nm, T(S, nm + "0", [128, 4, 128], BF16))
        else:
            for i_, nm in enumerate(("L", "N", "P", "Q", "Xa", "Xb2")):
                o = 8192 + 512 * i_
                setattr(B, nm, T.view(S, nm + "1", cx.gTflat[:, o:o + 512].rearrange("p (h c) -> p h c", h=4)))
        cx.BS.append(B)
    cx.gt_alias = [getattr(cx.BS[1], nm).b for nm in ("L", "N", "P", "Q", "Xa", "Xb2")]
    cx.pT = T(S, "pT", [128, 8, 128], BF16)
    cx.ktok = T(S, "ktok", [128, 512], BF16)
    cx.vtok = T(S, "vtok", [128, 512], BF16)
    cx.bv = T(S, "bv", [128, 512], BF16)
    cx.bek = T(S, "bek", [128, 512], BF16)
    cx.kdec = T(S, "kdec", [128, 512], BF16)
    cx.ub = T(S, "ub", [128, 512], F32)
    cx.wT = T(S, "wT", [128, 4, 128], BF16)
    cx.qdT = T(S, "qdT", [128, 4, 128], BF16)
    cx.gcol = T(S, "gcol", [128, 4], F32)
    cx.SB = [T(S, "SB%d" % i, [128, 128], F32) for i in range(4)]
    cx.SBb = [T(S, "SBb%d" % i, [128, 128], BF16) for i in range(4)]
    cx.ubf = [T(S, "ubf%d" % i, [128, 128], BF16) for i in range(2)]
    cx.stmp = [T(S, "stmp%d" % i, [128, 256], F32) for i in range(1)] * 2
    cx.ob = T(S, "ob", [128, 512], F32)
    cx.osq = cx.gt1
    cx.zs = T(S, "zs", [128, 512], F32)
    cx.gup = T(S, "gup", [16, 128], F32)
    cx.cng = T(S, "cng", [128, 64], F32)
    cx.cqT = T(S, "cqT", [128, 512], BF16)
    cx.ckT = T(S, "ckT", [128, 512], BF16)
    cx.la = T(S, "la", [128, 128], F32)
    cx.eb = T(S, "eb", [128, 128], F32)
    cx.enb = T(S, "enb", [128, 128], F32)
    cx.cqd = T(S, "cqd", [128, 128], BF16)
    cx.ckd = T(S, "ckd", [128, 128], BF16)
    cx.ckm = T(S, "ckm", [128, 4, 128], BF16)
    cx.ckdecT = T(S, "ckdecT", [128, 128], BF16)
    cx.ckdec = T(S, "ckdec", [128, 128], BF16)
    cx.cpT = T(S, "cpT", [128, 4, 128], BF16)
    cx.cv = T(S, "cv", [128, 256], BF16)
    cx.SC = T(S, "SC", [128, 256], F32)
    cx.SCb = T(S, "SCb", [128, 256], BF16)
    cx.oc = T(S, "oc", [128, 256], F32)
    cx.rs = T(S, "rs", [128, 256], F32)


def emit_make_T(S, cx, src_bf, src_b, dst, dst_b, t):
    ps, psb = next_ps(cx)
    psv = ps[:].bitcast(BF16)
    for k in range(8):
        TR(S, psv[:, k * 128:(k + 1) * 128], src_bf[:, k * 128:(k + 1) * 128], CB(cx, "ident"),
           [src_b, cx.cstb], [psb])
    CP(S, "dve", dst[:, :, t * 128:(t + 1) * 128],
       psv[:, 0:1024].rearrange("p (k c) -> p k c", k=8), [psb], [dst_b])


def emit_xt(S, cx, t):
    CP(S, "act", cx.xb16[:], cx.X[:, t, :], [cx.Xb[t]], [cx.xb16.b])
    emit_make_T(S, cx, cx.xb16, cx.xb16.b, cx.XT, cx.XTb[t], t)


def emit_ln(S, cx, src, srcb, dst, dstb, n, g, gb, b, bb, st=None):
    st = cx.stat if st is None else st
    stb = st.b
    junk, junkb = cx.junk, cx.junk.b
    MEMSET(S, "dve", st[:, 0:2], 0.0, [stb])
    ACT(S, junk[:, 0:n], src[:, 0:n], AF.Identity, [srcb], [junkb, stb], accum=st[:, 0:1])
    yield
    ACT(S, junk[:, 0:n], src[:, 0:n], AF.Square, [srcb], [junkb, stb], accum=st[:, 1:2])
    yield
    TS(S, "dve", st[:, 2:4], st[:, 0:2], 1.0 / n, ALU.mult, [stb], [stb])
    yield
    TT(S, "dve", st[:, 4:5], st[:, 2:3], st[:, 2:3], ALU.mult, [stb], [stb])
    yield
    TT(S, "dve", st[:, 5:6], st[:, 3:4], st[:, 4:5], ALU.subtract, [stb], [stb])
    yield
    ACT(S, st[:, 7:8], st[:, 5:6], AF.Ln, [stb], [stb], bias=EPS)
    yield
    ACT(S, st[:, 6:7], st[:, 7:8], AF.Exp, [stb], [stb], scale=-0.5)
    yield
    TS(S, "dve", src[:, 0:n], src[:, 0:n], st[:, 2:3], ALU.subtract, [srcb, stb], [srcb],
       s2=st[:, 6:7], op1=ALU.mult)
    yield
    TT(S, "pool", src[:, 0:n], src[:, 0:n], g, ALU.mult, [srcb, gb], [srcb])
    yield
    TT(S, "pool", dst, src[:, 0:n], b, ALU.add, [srcb, bb], [dstb])
    yield


def emit_res_ln_xt(S, cx, t, ybanks):
    Xt = cx.X[:, t, :]
    for h in range(2):
        bi = ybanks[h]
        STT(S, "dve", cx.X[:, t, h * 512:(h + 1) * 512], cx.X[:, t, h * 512:(h + 1) * 512], ALPHA,
            cx.ps[bi][:], ALU.mult, ALU.add, [cx.Xb[t], cx.psb[bi]], [cx.Xb[t]])
        yield
    yield from emit_ln(S, cx, Xt, cx.Xb[t], Xt, cx.Xb[t], D, cx.lng[:], cx.lng.b, cx.lnb[:], cx.lnb.b,
                       st=cx.stat4[t])
    emit_xt(S, cx, t)
    yield


def run_rr(gens):
    gens = list(gens)
    while gens:
        for g_ in list(gens):
            try:
                next(g_)
            except StopIteration:
                gens.remove(g_)


def emit_load_ln(S, cx, g_ap, b_ap):
    DMA(S, "sp", cx.lng[:], g_ap.partition_broadcast(128), [], [cx.lng.b], cx.lng.b)
    DMA(S, "sp", cx.lnb[:], b_ap.partition_broadcast(128), [], [cx.lnb.b], cx.lnb.b)


def emit_ffn_ln(S, cx, w1, w3, w2, g_ap, b_ap, tag):
    emit_load_ln(S, cx, g_ap, b_ap)
    w1v = w1.rearrange("(kc kp) f -> kp kc f", kp=128)
    w3v = w3.rearrange("(kc kp) f -> kp kc f", kp=128)
    w2v = w2.rearrange("(fc fp) d -> fp fc d", fp=128)
    for f in range(NF):
        wb = cx.w13[f % 3]

        def _cast13(f=f, wb=wb):
            S.op("pool", lambda e: [
                e.dma_start(out=wb[:, 0], in_=w1v[:, :, f * 128:(f + 1) * 128]),
                e.dma_start(out=wb[:, 1], in_=w3v[:, :, f * 128:(f + 1) * 128])],
                writes=[wb.b], dma=wb.b, ndma=2)
        WLOAD(S, cx, "%s_w13_%d" % (tag, f), wb, wb[:].rearrange("p a k c -> p (a k c)"), _cast13)
        pa, pab = next_ps(cx)
        pb, pbb = next_ps(cx)
        rd = [wb.b] + cx.XTb
        for k in range(8):
            MM(S, pa[:], wb[:, 0, k, :], cx.XT[:, k, :], k == 0, k == 7, rd, [pab])
        for k in range(8):
            MM(S, pb[:], wb[:, 1, k, :], cx.XT[:, k, :], k == 0, k == 7, rd, [pbb])
        sa = cx.sa[f % 2]
        ACT(S, sa[:], pa[:], AF.Silu, [pab], [sa.b])
        STT(S, "dve", cx.gT[:, f, :], sa[:], 0.5, pb[:], ALU.mult, ALU.mult, [sa.b, pbb],
            [cx.gTb[f], cx.wt[0].b, cx.wt[1].b] + (cx.gt_alias if f >= 16 else []))
    for f in range(NF):
        wb = cx.w2t[f % 3]
        WLOAD(S, cx, "%s_w2_%d" % (tag, f), wb, wb[:],
              lambda f=f, wb=wb: DMA(S, "pool", wb[:], w2v[:, f, :], [], [wb.b], wb.b))
        for j in range(4):
            for h in range(2):
                bi = j * 2 + h
                MM(S, cx.ps[bi][:], cx.gT[:, f, j * 128:(j + 1) * 128], wb[:, h * 512:(h + 1) * 512],
                   f == 0, f == NF - 1, [wb.b, cx.gTb[f]], [cx.psb[bi]])
    cx.psi = 0
    run_rr([emit_res_ln_xt(S, cx, j, (j * 2, j * 2 + 1)) for j in range(4)])


C_AU, C_AV, C_BQ, C_BK, C_BV, C_BZ, C_BS, C_CQ, C_CK, C_CV, C_CR, C_CG = (
    0, 256, 512, 1024, 1536, 2048, 2560, 2576, 2704, 2832, 3088, 3344)
GELU_C = 1.5957691216057308


def emit_layer_setup(S, cx, P):
    DMA(S, "sp", cx.sgg[:], P["sgu_ln_g"].partition_broadcast(128), [], [cx.sgg.b], cx.sgg.b)
    DMA(S, "sp", cx.sgb[:], P["sgu_ln_b"].partition_broadcast(128), [], [cx.sgb.b], cx.sgb.b)
    DMA(S, "sp", cx.sbT[:], P["sgu_bT"], [], [cx.sbT.b], cx.sbT.b)
    DMA(S, "sp", cx.wstg[:], P["sgu_wT"].rearrange("h j i -> j h i"), [], [cx.wstg.b], cx.wstg.b)
    TT(S, "dve", cx.WmT[:], cx.wstg[:], bc4(CF(cx, "U")), ALU.mult, [cx.wstg.b, cx.cstb], [cx.WmT.b])
    DMA(S, "sp", cx.cw[:], P["conv_wT"].rearrange("(c p) i -> p c i", p=128), [], [cx.cw.b], cx.cw.b)
    DMA(S, "sp", cx.dtb[:], P["dt_bias"].partition_broadcast(128), [], [cx.dtb.b], cx.dtb.b)
    DMA(S, "sp", cx.nega[:], P["a_log"].partition_broadcast(128), [], [cx.nega.b], cx.nega.b)
    ACT(S, cx.nega[:], cx.nega[:], AF.Exp, [cx.nega.b], [cx.nega.b])
    TS(S, "dve", cx.nega[:], cx.nega[:], -1.0, ALU.mult, [cx.nega.b], [cx.nega.b])
    DMA(S, "sp", cx.gng[:], P["gdn_norm_g"].partition_broadcast(128), [], [cx.gng.b], cx.gng.b)
    DMA(S, "sp", cx.cng[:], P["gla_norm_g"].partition_broadcast(128), [], [cx.cng.b], cx.cng.b)
    DMA(S, "sp", cx.gup[:], P["gate_up"], [], [cx.gup.b], cx.gup.b)
    DMA(S, "sp", cx.gbb[:], P["gate_b"].partition_broadcast(128), [], [cx.gbb.b], cx.gbb.b)
    CP(S, "dve", cx.gupb[:], cx.gup[:], [cx.gup.b], [cx.gupb.b])
    MEMSET(S, "dve", cx.halo[:], 0.0, [cx.halo.b])
    for c in range(4):
        MEMSET(S, "dve", cx.SB[c][:], 0.0, [cx.SB[c].b])
        MEMSET(S, "dve", cx.SBb[c][:], 0.0, [cx.SBb[c].b])
    MEMSET(S, "dve", cx.SC[:], 0.0, [cx.SC.b])
    MEMSET(S, "dve", cx.SCb[:], 0.0, [cx.SCb.b])


def proj_fm(S, cx, winv, col0, ncols, wf):
    if ncols == 128:
        WLOAD(S, cx, "wf_%d" % col0, wf, wf[:].rearrange("p k c -> p (k c)"),
              lambda: DMA(S, "pool", wf[:, :, 0:ncols], winv[:, :, col0:col0 + ncols], [], [wf.b], wf.b))
    else:
        DMA(S, "pool", wf[:, :, 0:ncols], winv[:, :, col0:col0 + ncols], [], [wf.b], wf.b)
    ps, psb = next_ps(cx)
    for k in range(8):
        MM(S, ps[0:ncols, :], wf[:, k, 0:ncols], cx.XT[:, k, :], k == 0, k == 7, [wf.b] + cx.XTb, [psb])
    return ps, psb


def emit_rms_gate(S, cx, o, ob, nh, gtile, gate_ap, gate_b, dst, dstb, sq, sm=None):
    n = nh * 64
    sm = cx.sm if sm is None else sm
    TT(S, "dve", sq[:, 0:n], o[:, 0:n], o[:, 0:n], ALU.mult, [ob], [sq.b])
    S.op("dve", lambda e: e.tensor_reduce(out=sm[:, 64:64 + nh],
                                          in_=sq[:, 0:n].rearrange("p (h d) -> p h d", d=64),
                                          axis=AX.X, op=ALU.add), reads=[sq.b], writes=[sm.b])
    ACT(S, sm[:, 72:72 + nh], sm[:, 64:64 + nh], AF.Ln, [sm.b], [sm.b], scale=1.0 / 64, bias=EPS)
    ACT(S, sm[:, 80:80 + nh], sm[:, 72:72 + nh], AF.Exp, [sm.b], [sm.b], scale=-0.5)
    ov = o[:, 0:n].rearrange("p (h d) -> p h d", d=64)
    TT(S, "dve", ov, ov, bcl(sm[:, 80:80 + nh], 64), ALU.mult, [ob, sm.b], [ob])
    TT(S, "dve", ov, ov, gtile[:].unsqueeze(1).to_broadcast([128, nh, 64]), ALU.mult, [ob, gtile.b], [ob])
    TT(S, "dve", dst, o[:, 0:n], gate_ap, ALU.mult, [ob, gate_b], [dstb])


def emit_mixer_group(S, cx, P):
    winv = P["w_in"].rearrange("(kc kp) f -> kp kc f", kp=128)
    U = CF(cx, "U")
    import os as _os
    STG = _os.environ.get("MIX_STAGES", "Bfm,Cfm,A,C,B").split(",")
    S.op("dve", lambda e: e.memset(cx.stat[:, 0:1], 0.0), writes=[cx.stat.b] + cx.gTb[16:] + cx.gt_alias)
    for ci in range(12 if "Bfm" in STG else 0):
        wf = cx.wf[ci % 3]
        ps, psb = proj_fm(S, cx, winv, C_BQ + 128 * ci, 128, wf)
        cit = cx.ci[ci % 2]
        CP(S, "pool", cit[:, 0:3], cx.halo[:, ci, 0:3], [cx.halo.b], [cit.b])
        CP(S, "act", cit[:, 3:515], ps[:], [psb], [cit.b])
        CP(S, "pool", cx.halo[:, ci, 0:3], cit[:, 512:515], [cit.b], [cx.halo.b])
        cy = cx.cy[ci % 2]
        ACT(S, cy[:], ps[:], AF.Copy, [psb, cx.cw.b], [cy.b], scale=cx.cw[:, ci, 3:4])
        for i in range(0, 3):
            STT(S, "dve", cy[:], cit[:, i:i + 512], cx.cw[:, ci, i:i + 1], cy[:], ALU.mult, ALU.add,
                [cit.b, cx.cw.b, cy.b], [cy.b])
        cs = cx.cs[ci % 2]
        ACT(S, cs[:], cy[:], AF.Silu, [cy.b], [cs.b])
        c = ci % 4
        if ci < 8:
            sq = cx.sq[ci % 2]
            ACT(S, sq[:], cs[:], AF.Square, [cs.b], [sq.b])
            ps2, ps2b = next_ps(cx)
            MM(S, ps2[:], CB(cx, "bm2"), sq[:], True, True, [sq.b, cx.cstb], [ps2b])
            ACT(S, cx.rn[:], ps2[:], AF.Ln, [ps2b], [cx.rn.b], bias=1e-6)
            ACT(S, cx.rn[:], cx.rn[:], AF.Exp, [cx.rn.b], [cx.rn.b], scale=-0.5)
            if ci < 4:
                STT(S, "dve", cx.qT[:, c, :], cs[:], 0.125, cx.rn[:], ALU.mult, ALU.mult,
                    [cs.b, cx.rn.b], [cx.qTb[c]])
            else:
                TT(S, "dve", cx.kT[:, c, :], cs[:], cx.rn[:], ALU.mult, [cs.b, cx.rn.b], [cx.kTb[c]])
        else:
            CP(S, "act", cx.vT[:, c, :], cs[:], [cs.b], [cx.vTb[c]])
    if "Cfm" not in STG:
        return
    ps, psb = proj_fm(S, cx, winv, C_CQ, 128, cx.wf[0])
    CP(S, "act", cx.cqT[:], ps[:], [psb], [cx.cqT.b])
    ps, psb = proj_fm(S, cx, winv, C_CK, 128, cx.wf[1])
    CP(S, "act", cx.ckT[:], ps[:], [psb], [cx.ckT.b])
    ps, psb = proj_fm(S, cx, winv, C_CG, 16, cx.wf[2])
    CP(S, "act", cx.cgTb[:], ps[0:16, :], [psb], [cx.cgTb.b])
    wA, wZ, wC = cx.wt
    S.op("dve", lambda e: e.memset(cx.stat[:, 0:1], 0.0), writes=[cx.stat.b, wA.b, wZ.b] + cx.gTb[0:16])
    for key_, wt_, c0_ in (("wA", wA, C_AU), ("wZ", wZ, C_BZ), ("wC", wC, C_CV)):
        WLOAD(S, cx, key_, wt_, wt_[:].rearrange("p k c -> p (k c)"),
              lambda wt_=wt_, c0_=c0_: DMA(S, "pool", wt_[:], winv[:, :, c0_:c0_ + 512], [], [wt_.b], wt_.b))
    wS = cx.wf[0]
    DMA(S, "pool", wS[:, :, 0:16], winv[:, :, C_BS:C_BS + 16], [], [wS.b], wS.b)
    for j in range(4):
        ts = slice(j * 128, (j + 1) * 128)
        extra = []
        if "A" in STG:
            extra.append(emit_mixer_A(S, cx, j, ts, wA))
        if "C" in STG:
            extra.append(emit_mixer_C(S, cx, j, ts, wC))
        if "B" in STG:
            emit_mixer_B(S, cx, j, ts, wZ, wS, extra)
        else:
            for g_ in extra:
                for _ in g_:
                    pass


def proj_tm(S, cx, ts, wt, ncols):
    ps, psb = next_ps(cx)
    for k in range(8):
        MM(S, ps[:, 0:ncols], cx.XT[:, k, ts], wt[:, k, 0:ncols], k == 0, k == 7, [wt.b] + cx.XTb, [psb])
    return ps, psb


def emit_mixer_A(S, cx, j, ts, wA):
    ps, psb = proj_tm(S, cx, ts, wA, 512)
    ga, t1 = cx.ga, cx.gt1
    ACT(S, ga[:], ps[:], AF.Copy, [psb], [ga.b], scale=0.5)
    yield
    TT(S, "dve", t1[:], ga[:], ga[:], ALU.mult, [ga.b], [t1.b])
    yield
    TS(S, "dve", t1[:], t1[:], 4 * 0.044715, ALU.mult, [t1.b], [t1.b], s2=1.0, op1=ALU.add)
    yield
    TT(S, "dve", t1[:], t1[:], ga[:], ALU.mult, [t1.b, ga.b], [t1.b])
    yield
    ACT(S, t1[:], t1[:], AF.Tanh, [t1.b], [t1.b], scale=GELU_C)
    yield
    STT(S, "dve", ga[:], t1[:], 1.0, ga[:], ALU.add, ALU.mult, [ga.b, t1.b], [ga.b])
    yield
    yield from emit_ln(S, cx, ga[:, 256:512], ga.b, cx.vln[:], cx.vln.b, 256, cx.sgg[:], cx.sgg.b,
                       cx.sgb[:], cx.sgb.b)
    pz, pzb = next_ps(cx)
    for h in range(4):
        MM(S, pz[:, h * 64:(h + 1) * 64], cx.WmT[:, h, :], cx.vln[:, h * 64:(h + 1) * 64], True, True,
           [cx.WmT.b, cx.vln.b], [pzb])
    for h in range(4):
        STT(S, "dve", cx.MIX[:, j, h * 64:(h + 1) * 64], pz[:, h * 64:(h + 1) * 64], cx.sbT[:, h:h + 1],
            ga[:, h * 64:(h + 1) * 64], ALU.add, ALU.mult, [pzb, cx.sbT.b, ga.b], [cx.MIXb[j]])


def emit_mixer_C(S, cx, j, ts, wC):
    psC, psCb = proj_tm(S, cx, ts, wC, 512)
    CP(S, "act", cx.cv[:], psC[:, 0:256], [psCb], [cx.cv.b])
    yield
    ACT(S, cx.rs[:], psC[:, 256:512], AF.Silu, [psCb], [cx.rs.b])
    yield
    pl, plb = next_ps(cx)
    MM(S, pl[:, 0:128], cx.cgTb[0:16, ts], cx.gupb[0:16, :], True, True, [cx.cgTb.b, cx.gupb.b], [plb])
    TT(S, "dve", cx.la[:], pl[:, 0:128], cx.gbb[:], ALU.add, [plb, cx.gbb.b], [cx.la.b])
    yield
    ACT(S, cx.la[:], cx.la[:], AF.Exp, [cx.la.b], [cx.la.b], scale=-1.0)
    yield
    ACT(S, cx.la[:], cx.la[:], AF.Ln, [cx.la.b], [cx.la.b], bias=1.0)
    yield
    pb, pbb = next_ps(cx)
    CP(S, "dve", cx.lah[:], cx.la[:], [cx.la.b], [cx.lah.b])
    yield
    TT(S, "dve", cx.lal[:], cx.la[:], cx.lah[:], ALU.subtract, [cx.la.b, cx.lah.b], [cx.lal.b])
    yield
    MM(S, pb[:, 0:128], cx.lah[:], CB(cx, "Un16"), True, False, [cx.lah.b, cx.cstb], [pbb])
    MM(S, pb[:, 0:128], cx.lal[:], CB(cx, "Un16"), False, True, [cx.lal.b, cx.cstb], [pbb])
    ACT(S, cx.eb[:], pb[:, 0:128], AF.Exp, [pbb], [cx.eb.b])
    yield
    ACT(S, cx.enb[:], pb[:, 0:128], AF.Exp, [pbb], [cx.enb.b], scale=-1.0)
    yield
    STT(S, "dve", cx.cqd[:], cx.cqT[:, ts], 32.0 ** -0.5, cx.eb[:], ALU.mult, ALU.mult,
        [cx.cqT.b, cx.eb.b], [cx.cqd.b])
    yield
    TT(S, "dve", cx.ckd[:], cx.ckT[:, ts], cx.enb[:], ALU.mult, [cx.ckT.b, cx.enb.b], [cx.ckd.b])
    yield
    TS(S, "dve", cx.ckdecT[:], cx.ckd[:], cx.eb[:, 127:128], ALU.mult, [cx.ckd.b, cx.eb.b], [cx.ckdecT.b])
    yield
    pt, ptb = next_ps(cx)
    ptv = pt[:].bitcast(BF16)
    TR(S, ptv[:, 0:128], cx.ckdecT[:], CB(cx, "ident"), [cx.ckdecT.b, cx.cstb], [ptb])
    CP(S, "act", cx.ckdec[:], ptv[:, 0:128], [ptb], [cx.ckdec.b])
    yield
    for h in range(4):
        TS(S, "dve", cx.ckm[:, h, :], cx.ckd[:], CF(cx, "hm")[:, h:h + 1], ALU.mult,
           [cx.ckd.b, cx.cstb], [cx.ckm.b])
    pp, ppb = next_ps(cx)
    for h in range(4):
        MM(S, pp[:, h * 128:(h + 1) * 128], cx.ckm[:, h, :], cx.cqd[:], True, True,
           [cx.ckm.b, cx.cqd.b], [ppb])
    TT(S, "dve", cx.cpT[:], pp[:].rearrange("p (h c) -> p h c", h=4), bc4(CF(cx, "U")), ALU.mult,
       [ppb, cx.cstb], [cx.cpT.b])
    yield
    po, pob = next_ps(cx)
    for h in range(4):
        hs = slice(h * 64, (h + 1) * 64)
        MM(S, po[:, hs], cx.cqd[:], cx.SCb[:, hs], True, False, [cx.cqd.b, cx.SCb.b], [pob])
        MM(S, po[:, hs], cx.cpT[:, h, :], cx.cv[:, hs], False, True, [cx.cpT.b, cx.cv.b], [pob])
    CP(S, "act", cx.oc[:], po[:, 0:256], [pob], [cx.oc.b])
    yield
    pS, pSb = next_ps(cx)
    MM(S, pS[:, 0:256], cx.ckdec[:], cx.cv[:], True, True, [cx.ckdec.b, cx.cv.b], [pSb])
    st = cx.stmp[0]
    TT(S, "dve", st[:], pS[:, 0:256], CF(cx, "bmC"), ALU.mult, [pSb, cx.cstb], [st.b])
    yield
    STT(S, "dve", cx.SC[:], cx.SC[:], cx.eb[:, 127:128], st[:], ALU.mult, ALU.add,
        [cx.SC.b, cx.eb.b, st.b], [cx.SC.b])
    yield
    CP(S, "act", cx.SCb[:], cx.SC[:], [cx.SC.b], [cx.SCb.b])
    yield
    emit_rms_gate(S, cx, cx.oc, cx.oc.b, 4, cx.cng, cx.rs[:], cx.rs.b, cx.MIX[:, j, 768:1024], cx.MIXb[j],
                  cx.osqc, cx.sm2)
    yield


def emit_mixer_B(S, cx, j, ts, wZ, wS, extra=()):
    sm = cx.sm
    cst = cx.cstb
    pZ, pZb = proj_tm(S, cx, ts, wZ, 512)
    ACT(S, cx.zs[:], pZ[:], AF.Silu, [pZb], [cx.zs.b])
    p16, p16b = proj_tm(S, cx, ts, wS, 16)
    CP(S, "act", sm[:, 0:16], p16[:, 0:16], [p16b], [sm.b])
    ACT(S, sm[:, 0:8], sm[:, 0:8], AF.Tanh, [sm.b], [sm.b], scale=0.5)
    TS(S, "dve", sm[:, 0:8], sm[:, 0:8], 0.5, ALU.mult, [sm.b], [sm.b], s2=0.5, op1=ALU.add)
    TT(S, "dve", sm[:, 8:16], sm[:, 8:16], cx.dtb[:], ALU.add, [sm.b, cx.dtb.b], [sm.b])
    ACT(S, sm[:, 8:16], sm[:, 8:16], AF.Exp, [sm.b], [sm.b])
    ACT(S, sm[:, 8:16], sm[:, 8:16], AF.Ln, [sm.b], [sm.b], bias=1.0)
    TT(S, "dve", sm[:, 8:16], sm[:, 8:16], cx.nega[:], ALU.mult, [sm.b, cx.nega.b], [sm.b])
    pg, pgb = next_ps(cx)
    smb = cx.smb
    CP(S, "dve", smb[:, 0:8], sm[:, 8:16], [sm.b], [smb.b])
    TT(S, "dve", smb[:, 8:16], sm[:, 8:16], smb[:, 0:8], ALU.subtract, [sm.b, smb.b], [smb.b])
    MM(S, pg[:, 0:8], CB(cx, "U"), smb[:, 0:8], True, False, [smb.b, cst], [pgb])
    MM(S, pg[:, 0:8], CB(cx, "U"), smb[:, 8:16], False, True, [smb.b, cst], [pgb])
    MM(S, pg[:, 8:16], CB(cx, "ones"), smb[:, 0:8], True, False, [smb.b, cst], [pgb])
    MM(S, pg[:, 8:16], CB(cx, "ones"), smb[:, 8:16], False, True, [smb.b, cst], [pgb])
    CP(S, "dve", sm[:, 16:32], pg[:, 0:16], [pgb], [sm.b])
    ACT(S, sm[:, 32:40], sm[:, 16:24], AF.Exp, [sm.b], [sm.b])
    TT(S, "dve", sm[:, 40:48], sm[:, 24:32], sm[:, 16:24], ALU.subtract, [sm.b], [sm.b])
    ACT(S, sm[:, 40:48], sm[:, 40:48], AF.Exp, [sm.b], [sm.b])
    TT(S, "dve", sm[:, 48:56], sm[:, 0:8], sm[:, 32:40], ALU.mult, [sm.b], [sm.b])
    TT(S, "dve", cx.lgb[:], bcl(smb[:, 0:8], 128), bc4(CB(cx, "ones"), 8), ALU.mult, [smb.b, cst], [cx.lgb.b])
    TT(S, "dve", cx.lgl[:], bcl(smb[:, 8:16], 128), bc4(CB(cx, "ones"), 8), ALU.mult, [smb.b, cst], [cx.lgl.b])
    import os as _os
    CUT = int(_os.environ.get("MIXB_CUT", "99"))
    if CUT <= 1:
        return
    pk, pkb = next_ps(cx)
    pkv = pk[:].bitcast(BF16)
    for c in range(4):
        TR(S, pkv[:, c * 128:(c + 1) * 128], cx.kT[:, c, ts], CB(cx, "ident"), [cx.kTb[c], cst], [pkb])
    CP(S, "act", cx.ktok[:], pkv[:, 0:512], [pkb], [cx.ktok.b])
    pv, pvb = next_ps(cx)
    pvv = pv[:].bitcast(BF16)
    for c in range(4):
        TR(S, pvv[:, c * 128:(c + 1) * 128], cx.vT[:, c, ts], CB(cx, "ident"), [cx.vTb[c], cst], [pvb])
    CP(S, "act", cx.vtok[:], pvv[:, 0:512], [pvb], [cx.vtok.b])

    def hv(t):
        return t[:].rearrange("p (h d) -> p h d", d=64)

    TT(S, "dve", hv(cx.bv), hv(cx.vtok), bcl(sm[:, 0:8], 64), ALU.mult, [cx.vtok.b, sm.b], [cx.bv.b])
    TT(S, "dve", hv(cx.bek), hv(cx.ktok), bcl(sm[:, 48:56], 64), ALU.mult, [cx.ktok.b, sm.b], [cx.bek.b])
    TT(S, "dve", hv(cx.kdec), hv(cx.ktok), bcl(sm[:, 40:48], 64), ALU.mult, [cx.ktok.b, sm.b], [cx.kdec.b])

    for par in range(2):
        TS(S, "dve", cx.kTm[:, :, par, :], cx.kT[:, :, ts], CF(cx, "pm")[:, par:par + 1], ALU.mult,
           cx.kTb + [cst], [cx.kTm.b])
        TT(S, "dve", cx.bekm[:, :, par, :], cx.bek[:].rearrange("p (c x) -> p c x", c=4),
           bc4(CB(cx, "cm")[:, par * 128:(par + 1) * 128]), ALU.mult, [cx.bek.b, cst], [cx.bekm.b])
    TT(S, "dve", cx.lgp[:], bcl(smb[:, 0:8], 64), bc4(CB(cx, "ones")[:, 0:64], 8), ALU.mult, [smb.b, cst], [cx.lgp.b])
    TT(S, "dve", cx.lgpl[:], bcl(smb[:, 8:16], 64), bc4(CB(cx, "ones")[:, 0:64], 8), ALU.mult, [smb.b, cst], [cx.lgpl.b])
    pGp, pGpb = next_ps(cx)
    lgpv = cx.lgp[:].rearrange("p (c q) d -> p c (q d)", q=2)
    lgplv = cx.lgpl[:].rearrange("p (c q) d -> p c (q d)", q=2)
    for c in range(4):
        MM(S, pGp[:, c * 128:(c + 1) * 128], lgpv[:, c, :], CB(cx, "U"), True, False, [cx.lgp.b, cst], [pGpb])
        MM(S, pGp[:, c * 128:(c + 1) * 128], lgplv[:, c, :], CB(cx, "U"), False, True, [cx.lgpl.b, cst], [pGpb])
    ACT(S, cx.EG[:], pGp[:].rearrange("p (c x) -> p c x", c=4), AF.Exp, [pGpb], [cx.EG.b])
    TT(S, "dve", cx.qdT[:], cx.qT[:, :, ts], cx.EG[:], ALU.mult, cx.qTb + [cx.EG.b], [cx.qdT.b])
    CP(S, "dve", cx.gcol[:], cx.EG[:, :, 127], [cx.EG.b], [cx.gcol.b])

    if CUT <= 2:
        return
    def hg_chain(hg, B):
        h0 = 4 * hg

        def kTh(hh):
            h = h0 + hh
            pb = 64 * (h % 2)
            return cx.kT[pb:pb + 64, h // 2, ts], cx.kTb[h // 2]

        def qTh(hh):
            h = h0 + hh
            pb = 64 * (h % 2)
            return cx.qT[pb:pb + 64, h // 2, ts], cx.qTb[h // 2]

        pG, pGb = next_ps(cx)
        pGv = pG[:].rearrange("p (h c) -> p h c", h=4)
        for hh in range(4):
            MM(S, pG[:, hh * 128:(hh + 1) * 128], cx.lgb[:, h0 + hh, :], CB(cx, "U"), True, False,
               [cx.lgb.b, cst], [pGb])
            MM(S, pG[:, hh * 128:(hh + 1) * 128], cx.lgl[:, h0 + hh, :], CB(cx, "U"), False, True,
               [cx.lgl.b, cst], [pGb])
        TT(S, "dve", B.tmpD[:], pGv, bcl(sm[:, 16 + h0:20 + h0], 128), ALU.subtract, [pGb, sm.b], [B.tmpD.b])
        yield
        TS(S, "dve", B.tmpE[:], B.tmpD[:], 0.0, ALU.max, [B.tmpD.b], [B.tmpE.b])
        yield
        ACT(S, B.E[:], B.tmpE[:], AF.Exp, [B.tmpE.b], [B.E.b], scale=-1.0)
        yield
        TS(S, "dve", B.tmpE[:], B.tmpD[:], 0.0, ALU.min, [B.tmpD.b, B.E.b], [B.tmpE.b])
        yield
        ACT(S, B.ET[:], B.tmpE[:], AF.Exp, [B.tmpE.b], [B.ET.b])
        yield
        TT(S, "dve", B.E[:], B.E[:], bc4(CF(cx, "Lstr")), ALU.mult, [B.E.b, cst], [B.E.b])
        yield
        TT(S, "dve", B.ET[:], B.ET[:], bc4(CF(cx, "U")), ALU.mult, [B.ET.b, cst], [B.ET.b])
        yield
        if CUT <= 3:
            return
        pK, pKb = next_ps(cx)
        for hh in range(4):
            h = h0 + hh
            MM(S, pK[:, hh * 128:(hh + 1) * 128], cx.kTm[:, h // 2, h % 2, :], cx.kT[:, h // 2, ts], True, True,
               [cx.kTm.b, cx.kTb[h // 2]], [pKb])
        TT(S, "dve", B.tmpD[:], pK[:].rearrange("p (h c) -> p h c", h=4), B.E[:], ALU.mult,
           [pKb, B.E.b], [B.tmpD.b])
        yield
        TT(S, "dve", B.L[:], B.tmpD[:], bcl(sm[:, h0:h0 + 4], 128), ALU.mult, [B.tmpD.b, sm.b], [B.L.b])
        yield
        pN, pNb = next_ps(cx)
        pNv = pN[:].bitcast(BF16)
        for hh in range(4):
            TR(S, pNv[:, hh * 128:(hh + 1) * 128], B.L[:, hh, :], CB(cx, "ident"), [B.L.b, cst], [pNb])
        CP(S, "act", B.N[:], pNv[:, 0:512].rearrange("p (h c) -> p h c", h=4), [pNb], [B.N.b])
        yield
        if CUT <= 4:
            return
        I4 = bc4(CB(cx, "ident"))
        TT(S, "dve", B.Xa[:], B.L[:], bc4(CB(cx, "m1")), ALU.mult, [B.L.b, cst], [B.Xa.b])
        yield
        STT(S, "dve", B.P[:], B.Xa[:], -1.0, I4, ALU.mult, ALU.add, [B.Xa.b, cst], [B.P.b])
        yield
        TT(S, "dve", B.Xb2[:], B.N[:], bc4(CB(cx, "mT1")), ALU.mult, [B.N.b, cst], [B.Xb2.b])
        yield
        STT(S, "dve", B.Q[:], B.Xb2[:], -1.0, I4, ALU.mult, ALU.add, [B.Xb2.b, cst], [B.Q.b])
        yield
        for b in MB_LIST[1:]:
            last = b == MB_LIST[-1]
            p1, p1b = next_ps(cx)
            for hh in range(4):
                MM(S, p1[:, hh * 128:(hh + 1) * 128], B.N[:, hh, :], B.P[:, hh, :], True, True,
                   [B.N.b, B.P.b], [p1b])
            TT(S, "dve", B.Xa[:], p1[:].rearrange("p (h c) -> p h c", h=4), bc4(CB(cx, "m%d" % b)), ALU.mult,
               [p1b, cst], [B.Xa.b])
            yield
            if not last:
                p2, p2b = next_ps(cx)
                for hh in range(4):
                    MM(S, p2[:, hh * 128:(hh + 1) * 128], B.L[:, hh, :], B.Q[:, hh, :], True, True,
                       [B.L.b, B.Q.b], [p2b])
                TT(S, "dve", B.Xb2[:], p2[:].rearrange("p (h c) -> p h c", h=4), bc4(CB(cx, "mT%d" % b)),
                   ALU.mult, [p2b, cst], [B.Xb2.b])
                yield
            p3, p3b = next_ps(cx)
            for hh in range(4):
                MM(S, p3[:, hh * 128:(hh + 1) * 128], B.Xa[:, hh, :], B.Q[:, hh, :], True, True,
                   [B.Xa.b, B.Q.b], [p3b])
            if not last:
                p4, p4b = next_ps(cx)
                for hh in range(4):
                    MM(S, p4[:, hh * 128:(hh + 1) * 128], B.Xb2[:, hh, :], B.P[:, hh, :], True, True,
                       [B.Xb2.b, B.P.b], [p4b])
            TT(S, "dve", B.Q[:], B.Q[:], p3[:].rearrange("p (h c) -> p h c", h=4), ALU.subtract,
               [B.Q.b, p3b], [B.Q.b])
            yield
            if not last:
                TT(S, "dve", B.P[:], B.P[:], p4[:].rearrange("p (h c) -> p h c", h=4), ALU.subtract,
                   [B.P.b, p4b], [B.P.b])
                yield
        if CUT <= 5:
            return
        pu, pub = next_ps(cx)
        for hh in range(4):
            h = h0 + hh
            MM(S, pu[:, hh * 64:(hh + 1) * 64], B.Q[:, hh, :], cx.bv[:, h * 64:(h + 1) * 64], True, True,
               [B.Q.b, cx.bv.b], [pub])
        CP(S, "act", cx.ub[:, hg * 256:(hg + 1) * 256], pu[:, 0:256], [pub], [cx.ub.b])
        yield
        pw, pwb = next_ps(cx)
        for cc in range(2):
            c = 2 * hg + cc
            for par in range(2):
                MM(S, pw[:, cc * 128:(cc + 1) * 128], cx.bekm[:, c, par, :], B.Q[:, 2 * cc + par, :],
                   par == 0, par == 1, [cx.bekm.b, B.Q.b], [pwb])
        CP(S, "act", cx.wT[:, 2 * hg:2 * hg + 2, :], pw[:, 0:256].rearrange("p (c x) -> p c x", c=2),
           [pwb], [cx.wT.b])
        yield
        pq, pqb = next_ps(cx)
        for hh in range(4):
            h = h0 + hh
            MM(S, pq[:, hh * 128:(hh + 1) * 128], cx.kTm[:, h // 2, h % 2, :], cx.qT[:, h // 2, ts], True, True,
               [cx.kTm.b, cx.qTb[h // 2]], [pqb])
        TT(S, "dve", cx.pT[:, h0:h0 + 4, :], pq[:].rearrange("p (h c) -> p h c", h=4), B.ET[:], ALU.mult,
           [pqb, B.ET.b], [cx.pT.b])
        yield

    gens = [hg_chain(0, cx.BS[0]), hg_chain(1, cx.BS[1])] + list(extra)
    while gens:
        for g_ in list(gens):
            try:
                next(g_)
            except StopIteration:
                gens.remove(g_)
    if CUT <= 6:
        return
    for c in range(4):
        SBc, SBb = cx.SB[c], cx.SBb[c]
        ubf = cx.ubf[c % 2]
        pu2, pu2b = next_ps(cx)
        MM(S, pu2[:, 0:128], cx.wT[:, c, :], SBb[:], True, True, [cx.wT.b, SBb.b], [pu2b])
        TT(S, "dve", ubf[:], cx.ub[:, c * 128:(c + 1) * 128], pu2[:, 0:128], ALU.subtract,
           [cx.ub.b, pu2b], [ubf.b])
        po, pob = next_ps(cx)
        for par in range(2):
            h = 2 * c + par
            hs = slice(par * 64, par * 64 + 64)
            MM(S, po[:, hs], cx.qdT[:, c, :], SBb[:, hs], True, False, [cx.qdT.b, SBb.b], [pob])
            MM(S, po[:, hs], cx.pT[:, h, :], ubf[:, hs], False, True, [cx.pT.b, ubf.b], [pob])
        CP(S, "act", cx.ob[:, c * 128:(c + 1) * 128], po[:, 0:128], [pob], [cx.ob.b])
        pS, pSb = next_ps(cx)
        MM(S, pS[:, 0:128], cx.kdec[:, c * 128:(c + 1) * 128], ubf[:], True, True, [cx.kdec.b, ubf.b], [pSb])
        st = cx.stmp[c % 2]
        TT(S, "dve", st[:, 0:128], pS[:, 0:128], CF(cx, "bm2"), ALU.mult, [pSb, cst], [st.b])
        STT(S, "dve", SBc[:], SBc[:], cx.gcol[:, c:c + 1], st[:, 0:128], ALU.mult, ALU.add,
            [SBc.b, cx.gcol.b, st.b], [SBc.b])
        CP(S, "act", SBb[:], SBc[:], [SBc.b], [SBb.b])
    if CUT <= 7:
        return
    emit_rms_gate(S, cx, cx.ob, cx.ob.b, 8, cx.gng, cx.zs[:], cx.zs.b, cx.MIX[:, j, 256:768], cx.MIXb[j],
                  cx.osq)


def emit_wout_ln(S, cx, P):
    for j in range(4):
        emit_make_T(S, cx, cx.MIX[:, j, :], cx.MIXb[j], cx.XT, cx.XTb[j], j)
    emit_load_ln(S, cx, P["ln2_g"], P["ln2_b"])
    wov = P["w_out"].rearrange("(kc kp) d -> kp kc d", kp=128)
    cx.psi = 0
    for k in range(8):
        wo = cx.wo[k % 3]
        WLOAD(S, cx, "wo_%d" % k, wo, wo[:],
              lambda k=k, wo=wo: DMA(S, "pool", wo[:], wov[:, k, :], [], [wo.b], wo.b))
        for j in range(4):
            for h in range(2):
                bi = j * 2 + h
                MM(S, cx.ps[bi][:], cx.XT[:, k, j * 128:(j + 1) * 128], wo[:, h * 512:(h + 1) * 512],
                   k == 0, k == 7, [wo.b, cx.XTb[j]], [cx.psb[bi]])
    cx.psi = 0
    run_rr([emit_res_ln_xt(S, cx, j, (j * 2, j * 2 + 1)) for j in range(4)])


PNAMES = ["ffn1_w1", "ffn1_w3", "ffn1_w2", "ln1_g", "ln1_b", "w_in", "sgu_ln_g", "sgu_ln_b", "sgu_wT", "sgu_bT",
          "conv_wT", "a_log", "dt_bias", "gdn_norm_g", "gate_up", "gate_b", "gla_norm_g", "w_out", "ln2_g",
          "ln2_b", "ffn2_w1", "ffn2_w3", "ffn2_w2", "ln3_g", "ln3_b"]
PSHAPES = {"ffn1_w1": [D, DFF], "ffn1_w3": [D, DFF], "ffn1_w2": [DFF, D], "ln1_g": [D], "ln1_b": [D],
           "w_in": [D, DIN], "sgu_ln_g": [256], "sgu_ln_b": [256], "sgu_wT": [4, 128, 128], "sgu_bT": [128, 4],
           "conv_wT": [1536, 4], "a_log": [8], "dt_bias": [8], "gdn_norm_g": [64], "gate_up": [16, 128],
           "gate_b": [128], "gla_norm_g": [64], "w_out": [D, D], "ln2_g": [D], "ln2_b": [D],
           "ffn2_w1": [D, DFF], "ffn2_w3": [D, DFF], "ffn2_w2": [DFF, D], "ln3_g": [D], "ln3_b": [D]}


def build_program(T_tok, depth, stop_after=None, dbg=False):
    nc = bass.Bass("TRN2", target_bir_lowering=False)
    x = nc.dram_tensor("x", [T_tok, D], F32, kind="ExternalInput").ap()
    y = nc.dram_tensor("y", [T_tok, D], F32, kind="ExternalOutput").ap()
    cst = nc.dram_tensor("cstf", [128, NCSTF], F32, kind="ExternalInput").ap()
    cstb = nc.dram_tensor("cstb", [128, NCSTB], F32, kind="ExternalInput").ap()
    prm = {}
    for n in PNAMES:
        prm[n] = nc.dram_tensor(n, [depth] + PSHAPES[n], F32, kind="ExternalInput").ap()
    xs = [nc.dram_tensor("xs%d" % i, [T_tok, D], F32, kind="Internal").ap() for i in range(2)]
    NG = T_tok // 512
    with ExitStack() as stack:
        S = Sched(nc, stack)
        cx = Ctx()
        alloc_all(S, nc, cx)
        cx.nc = nc
        cx.scr = {}
        cx.scrb = {}
        cx.first = True
        DMA(S, "sp", cx.CST[:], cst, [], [cx.cstb], cx.cstb)
        cstb2 = S.buf("cstb2")
        DMA(S, "pool", cx.CSTB[:], cstb, [], [cstb2], cstb2)
        S.op("dve", lambda e: e.memset(cx.stat[:, 0:1], 0.0), reads=[cstb2], writes=[cx.cstb, cx.stat.b])
        xsb = [[S.buf("xs%d_%d" % (i, g)) for g in range(NG)] for i in range(2)]
        yb = S.buf("y")
        for l in range(depth):
            P = {n: prm[n][l] for n in PNAMES}
            emit_layer_setup(S, cx, P)
            src = x if l == 0 else xs[(l - 1) % 2]
            dst = y if l == depth - 1 else xs[l % 2]
            srcv = src.rearrange("(g t p) d -> g p t d", p=128, t=4)
            dstv = dst.rearrange("(g t p) d -> g p t d", p=128, t=4)
            for g in range(NG):
                cx.first = (g == 0)
                for t in range(4):
                    rd = [] if l == 0 else [xsb[(l - 1) % 2][g]]
                    DMA(S, "sp", cx.X[:, t, :], srcv[g, :, t, :], rd, [cx.Xb[t]], cx.Xb[t])
                for t in range(4):
                    emit_xt(S, cx, t)
                emit_ffn_ln(S, cx, P["ffn1_w1"], P["ffn1_w3"], P["ffn1_w2"], P["ln1_g"], P["ln1_b"], "f1")
                if stop_after != "ffn1":
                    emit_mixer_group(S, cx, P)
                    if dbg and g == 0 and l == 0:
                        S.dump("mix", cx.MIX[:], cx.MIXb[3], [128, 4, D], BF16)
                    emit_wout_ln(S, cx, P)
                    if stop_after != "mix":
                        emit_ffn_ln(S, cx, P["ffn2_w1"], P["ffn2_w3"], P["ffn2_w2"], P["ln3_g"], P["ln3_b"], "f2")
                for t in range(4):
                    wr = [yb] if l == depth - 1 else [xsb[l % 2][g]]
                    DMA(S, "sp", dstv[g, :, t, :], cx.X[:, t, :], [cx.Xb[t]], wr, cx.Xb[t])
        S.finish([yb] + cx.Xb)
        S.emit()
    return nc


def host_params(inputs, depth=DEPTH):
    f = lambda a: np.ascontiguousarray(np.asarray(a, dtype=np.float32))
    p = {}
    for n in ["ffn1_w1", "ffn1_w3", "ffn1_w2", "ln1_g", "ln1_b", "w_in", "sgu_ln_g", "sgu_ln_b", "w_out",
              "ln2_g", "ln2_b", "ffn2_w1", "ffn2_w3", "ffn2_w2", "ln3_g", "ln3_b"]:
        p[n] = f(inputs[n])[:depth]
    p["sgu_wT"] = f(np.transpose(np.asarray(inputs["sgu_w"]), (0, 1, 3, 2)))[:depth]
    p["sgu_bT"] = f(np.transpose(np.asarray(inputs["sgu_b"]), (0, 2, 1)))[:depth]
    p["conv_wT"] = f(np.transpose(np.asarray(inputs["gdn_conv_w"]), (0, 2, 1)))[:depth]
    p["a_log"] = f(inputs["gdn_a_log"])[:depth]
    p["dt_bias"] = f(inputs["gdn_dt_bias"])[:depth]
    p["gdn_norm_g"] = f(inputs["gdn_norm_g"])[:depth]
    p["gate_up"] = f(inputs["gla_gate_up"])[:depth]
    p["gate_b"] = f(inputs["gla_gate_b"])[:depth]
    p["gla_norm_g"] = f(inputs["gla_norm_g"])[:depth]
    p["cstf"] = CSTF_NP
    p["cstb"] = CSTB_NP
    return p


_PROG = {}


def kernel(**inputs):
    x = np.asarray(inputs["x"], dtype=np.float32)
    B, T_tok, _ = x.shape
    p = host_params(inputs)
    key = (T_tok, DEPTH)
    if key not in _PROG:
        _PROG[key] = build_program(T_tok, DEPTH)
    nc = _PROG[key]
    in_maps = []
    for b in range(B):
        m = dict(p)
        m["x"] = np.ascontiguousarray(x[b])
        in_maps.append(m)
    res = run_bass_kernel_spmd(nc, in_maps, core_ids=list(range(B)))
    return np.stack([res.results[b]["y"] for b in range(B)], axis=0).astype(np.float32)
```

```python
import numpy as np
from contextlib import ExitStack
import concourse.bass as bass
import concourse.mybir as mybir
from concourse.bass_utils import run_bass_kernel_spmd

F32 = mybir.dt.float32
BF16 = mybir.dt.bfloat16
AF = mybir.ActivationFunctionType
ALU = mybir.AluOpType
AX = mybir.AxisListType

D = 1024
DFF = 2816
NF = DFF // 128
DEPTH = 4
TOK = 2048
NT = TOK // 128
ALPHA = (2.0 * DEPTH) ** 0.25
EPS = 1e-5
DIN = 3360


import os as _os0
SKIP_SAME = tuple(_os0.environ.get("SKIP_SAME", "pe").split(","))


class Buf:
    __slots__ = ("name", "wev", "revs", "dsem")

    def __init__(self, name):
        self.name = name
        self.wev = None
        self.revs = {}
        self.dsem = None


class Sched:
    ENG = ("pe", "act", "dve", "pool", "sp")

    def __init__(self, nc, stack):
        self.nc = nc
        self.stack = stack
        self.prog = {e: [] for e in self.ENG}
        self.esem = {e: stack.enter_context(nc.semaphore("s_" + e)) for e in self.ENG}
        self.cnt = {}
        self.seen = {e: {} for e in self.ENG}
        self.nsem = len(self.ENG)
        self.nbuf = 0
        self.dbg = []

    def buf(self, name=None):
        self.nbuf += 1
        return Buf(name or "b%d" % self.nbuf)

    def bufs(self, n, name="b"):
        return [self.buf("%s%d" % (name, i)) for i in range(n)]

    def sb(self, name, shape, dtype):
        return self.stack.enter_context(self.nc.sbuf_tensor(name, list(shape), dtype))

    def ps(self, name, shape, dtype):
        return self.stack.enter_context(self.nc.psum_tensor(name, list(shape), dtype))

    def op(self, eng, fn, reads=(), writes=(), dma=None, ndma=1):
        waits = {}

        def need(ev):
            if ev is None:
                return
            s, v = ev
            if v > waits.get(s, 0):
                waits[s] = v

        for b in reads:
            need(b.wev)
        for b in writes:
            need(b.wev)
            for ev in b.revs.values():
                need(ev)
        if dma is not None:
            if dma.dsem is None:
                self.nsem += 1
                dma.dsem = self.stack.enter_context(self.nc.semaphore("d%d" % self.nsem))
            sem = dma.dsem
            amt = 16
            total = 16 * ndma
        else:
            sem = self.esem[eng]
            amt = 1
            total = 1
        own = self.esem[eng]
        wl = []
        for s, v in waits.items():
            if s is own and eng in SKIP_SAME:
                continue
            if self.seen[eng].get(s, 0) >= v:
                continue
            self.seen[eng][s] = v
            wl.append((s, v))
        self.cnt[sem] = self.cnt.get(sem, 0) + total
        ev = (sem, self.cnt[sem])
        self.prog[eng].append((wl, fn, sem, amt))
        for b in reads:
            b.revs[sem] = ev
        for b in writes:
            b.wev = ev
            b.revs = {}
        return ev

    def dump(self, name, ap, buf, shape, dtype=F32):
        d = self.nc.dram_tensor("dbg_" + name, list(shape), dtype, kind="ExternalOutput").ap()
        db = self.buf("dbg_" + name)
        self.dbg.append(db)
        self.op("sp", lambda e: e.dma_start(out=d, in_=ap), reads=[buf], writes=[db], dma=buf)

    def finish(self, bufs):
        bufs = list(bufs) + self.dbg
        waits = {}
        for b in bufs:
            for ev in [b.wev] + list(b.revs.values()):
                if ev is not None and ev[1] > waits.get(ev[0], 0):
                    waits[ev[0]] = ev[1]
        self.prog["sp"].append((list(waits.items()), None, None, 0))

    def emit(self):
        prog = self.prog

        def mk(name):
            def f(e):
                for wl, fn, sem, amt in prog[name]:
                    for s, v in wl:
                        e.wait_ge(s, v)
                    if fn is None:
                        continue
                    r = fn(e)
                    if not isinstance(r, (list, tuple)):
                        r = [r]
                    for ins in r:
                        ins.then_inc(sem, amt)
            return f

        with self.nc.Block() as block:
            block.tensor(mk("pe"))
            block.scalar(mk("act"))
            block.vector(mk("dve"))
            block.gpsimd(mk("pool"))
            block.sync(mk("sp"))


def MM(S, out, lhsT, rhs, start, stop, rd, wr):
    S.op("pe", lambda e: e.matmul(out, lhsT=lhsT, rhs=rhs, start=start, stop=stop), reads=rd, writes=wr)


def TR(S, out, in_, ident, rd, wr):
    S.op("pe", lambda e: e.transpose(out=out, in_=in_, identity=ident), reads=rd, writes=wr)


def ACT(S, out, in_, func, rd, wr, scale=None, bias=None, accum=None):
    kw = {}
    if scale is not None:
        kw["scale"] = scale
    if bias is not None:
        kw["bias"] = bias
    if accum is not None:
        kw["accum_out"] = accum
    S.op("act", lambda e: e.activation(out=out, in_=in_, func=func, **kw), reads=rd, writes=wr)


def TT(S, eng, out, in0, in1, op, rd, wr):
    S.op(eng, lambda e: e.tensor_tensor(out=out, in0=in0, in1=in1, op=op), reads=rd, writes=wr)


def TS(S, eng, out, in0, s1, op0, rd, wr, s2=None, op1=None):
    if op1 is None:
        S.op(eng, lambda e: e.tensor_scalar(out=out, in0=in0, scalar1=s1, scalar2=None, op0=op0),
             reads=rd, writes=wr)
    else:
        S.op(eng, lambda e: e.tensor_scalar(out=out, in0=in0, scalar1=s1, scalar2=s2, op0=op0, op1=op1),
             reads=rd, writes=wr)


def STT(S, eng, out, in0, scalar, in1, op0, op1, rd, wr):
    S.op(eng, lambda e: e.scalar_tensor_tensor(out=out, in0=in0, scalar=scalar, in1=in1, op0=op0, op1=op1),
         reads=rd, writes=wr)


def CP(S, eng, out, in_, rd, wr):
    if eng == "act":
        S.op("act", lambda e: e.activation(out=out, in_=in_, func=AF.Copy), reads=rd, writes=wr)
    else:
        S.op(eng, lambda e: e.tensor_copy(out=out, in_=in_), reads=rd, writes=wr)


def RECIP(S, out, in_, rd, wr):
    S.op("dve", lambda e: e.reciprocal(out=out, in_=in_), reads=rd, writes=wr)


def DMA(S, eng, out, in_, rd, wr, dma):
    S.op(eng, lambda e: e.dma_start(out=out, in_=in_), reads=rd, writes=wr, dma=dma)


def WLOAD(S, cx, key, tile, flat_ap, cast_fn):
    if key not in cx.scr:
        n = flat_ap.shape[1]
        cx.scr[key] = cx.nc.dram_tensor("scr_" + key, [128, n], BF16).ap()
        cx.scrb[key] = S.buf("scr_" + key)
    scr, scrb = cx.scr[key], cx.scrb[key]
    if not hasattr(tile, "hw"):
        tile.hw = S.buf("hw")
    if cx.first:
        cast_fn()
        DMA(S, "sp", scr, flat_ap, [tile.b], [scrb], tile.hw)
    else:
        DMA(S, "sp", flat_ap, scr, [scrb], [tile.b], tile.hw)


def MEMSET(S, eng, ap, val, wr):
    S.op(eng, lambda e: e.memset(ap, val), writes=wr)


class Ctx:
    pass


class T:
    def __init__(self, S, name, shape, dtype):
        self.h = S.sb(name, shape, dtype)
        self.b = S.buf(name)

    def __getitem__(self, k):
        return self.h[k]

    @classmethod
    def view(cls, S, name, ap):
        o = cls.__new__(cls)
        o.h = ap
        o.b = S.buf(name)
        return o


MB_LIST = [1, 2, 4, 8, 16, 32, 64]


def make_consts():
    i = np.arange(128)
    cf = {}
    cb = {}
    cb["ident"] = np.eye(128)
    cf["U"] = (i[:, None] <= i[None, :]) * 1.0
    cf["Un16"] = (i[:, None] <= i[None, :]) * (-1.0 / 16.0)
    cf["ones"] = np.ones((128, 128))
    cf["Lstr"] = (i[:, None] > i[None, :]) * 1.0
    for b in MB_LIST:
        bi = i // b
        m = ((bi[:, None] % 2 == 1) & (bi[None, :] == bi[:, None] - 1)) * 1.0
        cb["m%d" % b] = m
        cb["mT%d" % b] = m.T.copy()
    cf["bm2"] = ((i[:, None] // 64) == (i[None, :] // 64)) * 1.0
    cb["bm2"] = cf["bm2"]
    cb["U"] = cf["U"]
    cb["Un16"] = cf["Un16"]
    cb["ones"] = cf["ones"]
    j = np.arange(256)
    cf["bmC"] = ((i[:, None] // 32) == (j[None, :] // 64)) * 1.0
    cf["hm"] = ((i[:, None] // 32) == np.arange(4)[None, :]) * 1.0
    cf["pm"] = ((i[:, None] // 64) == np.arange(2)[None, :]) * 1.0
    cb["cm"] = np.concatenate([np.tile((i[None, :] // 64 == q) * 1.0, (128, 1)) for q in range(2)], axis=1)

    def pack(cols):
        off = {}
        arrs = []
        o = 0
        for k, v in cols.items():
            off[k] = (o, v.shape[1])
            o += v.shape[1]
            arrs.append(v.astype(np.float32))
        return np.concatenate(arrs, axis=1), off

    return pack(cf), pack(cb)


(CSTF_NP, CSTF_OFF), (CSTB_NP, CSTB_OFF) = make_consts()
NCSTF = CSTF_NP.shape[1]
NCSTB = CSTB_NP.shape[1]


def next_ps(cx):
    i = cx.psi
    cx.psi = (cx.psi + 1) % 8
    return cx.ps[i], cx.psb[i]


def CF(cx, name):
    o, n = CSTF_OFF[name]
    return cx.CST[:, o:o + n]


def CB(cx, name):
    o, n = CSTB_OFF[name]
    return cx.CSTB[:, o:o + n]


def bc4(ap, n=4):
    return ap.unsqueeze(1).to_broadcast([128, n, ap.shape[1]])


def bcl(ap, m):
    return ap.unsqueeze(2).to_broadcast([128, ap.shape[1], m])


def alloc_all(S, nc, cx):
    cx.ps = [S.ps("ps%d" % i, [128, 512], F32) for i in range(8)]
    cx.psb = S.bufs(8, "ps")
    cx.psi = 0
    cx.CST = S.sb("CST", [128, NCSTF], F32)
    cx.CSTB = S.sb("CSTB", [128, NCSTB], BF16)
    cx.cstb = S.buf("cst")
    cx.X = S.sb("X", [128, 4, D], F32)
    cx.Xb = S.bufs(4, "X")
    cx.XT = S.sb("XT", [128, 8, 512], BF16)
    cx.XTb = S.bufs(4, "XT")
    cx.MIX = S.sb("MIX", [128, 4, D], BF16)
    cx.MIXb = S.bufs(4, "MIX")
    cx.lng = T(S, "lng", [128, D], F32)
    cx.lnb = T(S, "lnb", [128, D], F32)
    cx.w13 = [T(S, "w13_%d" % i, [128, 2, 8, 128], BF16) for i in range(3)]
    cx.w2t = [T(S, "w2_%d" % i, [128, D], BF16) for i in range(3)]
    cx.sa = [T(S, "sa%d" % i, [128, 512], F32) for i in range(1)] * 2
    cx.gTflat = S.sb("gT", [128, NF * 512], BF16)
    cx.gT = cx.gTflat[:, :].rearrange("p (f t) -> p f t", f=NF)
    cx.gTb = S.bufs(NF, "gT")
    cx.stat4 = [T(S, "stat4_%d" % i, [128, 8], F32) for i in range(4)]
    cx.stat = T(S, "stat", [128, 8], F32)
    cx.xb16 = T(S, "xb16", [128, D], BF16)
    cx.junk = cx.xb16
    cx.wf = [T(S, "wf%d" % i, [128, 8, 128], BF16) for i in range(3)]
    cx.wt = [T.view(S, "wt%d" % i, cx.gTflat[:, i * 4096:(i + 1) * 4096].rearrange("p (k c) -> p k c", k=8))
             for i in range(2)] + [T(S, "wt2", [128, 8, 512], BF16)]
    cx.wo = cx.w2t
    cx.sgg = T(S, "sgg", [128, 256], F32)
    cx.sgb = T(S, "sgb", [128, 256], F32)
    cx.sbT = T(S, "sbT", [128, 4], F32)
    cx.WmT = T(S, "WmT", [128, 4, 128], BF16)
    cx.ga = T(S, "ga", [128, 512], F32)
    cx.gt1 = T(S, "gt1", [128, 512], F32)
    cx.wstg = T.view(S, "wstg", cx.ga[:].rearrange("p (h c) -> p h c", h=4))
    cx.wstg.b = cx.ga.b
    cx.rn = T.view(S, "rn", cx.gt1[:])
    cx.rn.b = cx.gt1.b
    cx.vln = T(S, "vln", [128, 256], BF16)
    cx.cw = T(S, "cw", [128, 12, 4], F32)
    cx.halo = T(S, "halo", [128, 12, 4], F32)
    cx.ci = [T(S, "ci%d" % i, [128, 516], F32) for i in range(2)]
    cx.cy = [T(S, "cy%d" % i, [128, 512], F32) for i in range(2)]
    cx.cs = [T(S, "cs%d" % i, [128, 512], F32) for i in range(2)]
    cx.sq = [T(S, "sq%d" % i, [128, 512], BF16) for i in range(1)] * 2
    cx.qT = S.sb("qT", [128, 4, 512], BF16)
    cx.qTb = S.bufs(4, "qT")
    cx.kT = S.sb("kT", [128, 4, 512], BF16)
    cx.kTb = S.bufs(4, "kT")
    cx.vT = S.sb("vT", [128, 4, 512], BF16)
    cx.vTb = S.bufs(4, "vT")
    cx.dtb = T(S, "dtb", [128, 8], F32)
    cx.nega = T(S, "nega", [128, 8], F32)
    cx.gng = T(S, "gng", [128, 64], F32)
    cx.sm = T(S, "sm", [128, 96], F32)
    cx.sm2 = T(S, "sm2", [128, 96], F32)
    cx.osqc = T(S, "osqc", [128, 256], F32)
    cx.lgb = T(S, "lgb", [128, 8, 128], BF16)
    cx.lgl = T(S, "lgl", [128, 8, 128], BF16)
    cx.smb = T(S, "smb", [128, 32], BF16)
    cx.kTm = T(S, "kTm", [128, 4, 2, 128], BF16)
    cx.bekm = T(S, "bekm", [128, 4, 2, 128], BF16)
    cx.lgp = T(S, "lgp", [128, 8, 64], BF16)
    cx.lgpl = T(S, "lgpl", [128, 8, 64], BF16)
    cx.lah = T(S, "lah", [128, 128], BF16)
    cx.lal = T(S, "lal", [128, 128], BF16)
    cx.gbb = T(S, "gbb", [128, 128], F32)
    cx.gupb = T(S, "gupb", [16, 128], BF16)
    cx.cgTb = T(S, "cgTb", [16, 512], BF16)
    cx.EG = T(S, "EG", [128, 4, 128], F32)
    cx.BS = []
    for q in range(2):
        B = Ctx()
        B.tmpD = T(S, "tmpD%d" % q, [128, 4, 128], F32)
        B.tmpE = T(S, "tmpE%d" % q, [128, 4, 128], F32)
        B.E = T(S, "E%d" % q, [128, 4, 128], BF16)
        B.ET = T(S, "ET%d" % q, [128, 4, 128], BF16)
        if q == 0:
            for nm in ("L", "N", "P", "Q", "Xa", "Xb2"):
                setattr(B, nm, T(S, nm + "0", [128, 4, 128], BF16))
        else:
            for i_, nm in enumerate(("L", "N", "P", "Q", "Xa", "Xb2")):
                o = 8192 + 512 * i_
                setattr(B, nm, T.view(S, nm + "1", cx.gTflat[:, o:o + 512].rearrange("p (h c) -> p h c", h=4)))
        cx.BS.append(B)
    cx.gt_alias = [getattr(cx.BS[1], nm).b for nm in ("L", "N", "P", "Q", "Xa", "Xb2")]
    cx.pT = T(S, "pT", [128, 8, 128], BF16)
    cx.ktok = T(S, "ktok", [128, 512], BF16)
    cx.vtok = T(S, "vtok", [128, 512], BF16)
    cx.bv = T(S, "bv", [128, 512], BF16)
    cx.bek = T(S, "bek", [128, 512], BF16)
    cx.kdec = T(S, "kdec", [128, 512], BF16)
    cx.ub = T(S, "ub", [128, 512], F32)
    cx.wT = T(S, "wT", [128, 4, 128], BF16)
    cx.qdT = T(S, "qdT", [128, 4, 128], BF16)
    cx.gcol = T(S, "gcol", [128, 4], F32)
    cx.SB = [T(S, "SB%d" % i, [128, 128], F32) for i in range(4)]
    cx.SBb = [T(S, "SBb%d" % i, [128, 128], BF16) for i in range(4)]
    cx.ubf = [T(S, "ubf%d" % i, [128, 128], BF16) for i in range(2)]
    cx.stmp = [T(S, "stmp%d" % i, [128, 256], F32) for i in range(1)] * 2
    cx.ob = T(S, "ob", [128, 512], F32)
    cx.osq = cx.gt1
    cx.zs = T(S, "zs", [128, 512], F32)
    cx.gup = T(S, "gup", [16, 128], F32)
    cx.cng = T(S, "cng", [128, 64], F32)
    cx.cqT = T(S, "cqT", [128, 512], BF16)
    cx.ckT = T(S, "ckT", [128, 512], BF16)
    cx.la = T(S, "la", [128, 128], F32)
    cx.eb = T(S, "eb", [128, 128], F32)
    cx.enb = T(S, "enb", [128, 128], F32)
    cx.cqd = T(S, "cqd", [128, 128], BF16)
    cx.ckd = T(S, "ckd", [128, 128], BF16)
    cx.ckm = T(S, "ckm", [128, 4, 128], BF16)
    cx.ckdecT = T(S, "ckdecT", [128, 128], BF16)
    cx.ckdec = T(S, "ckdec", [128, 128], BF16)
    cx.cpT = T(S, "cpT", [128, 4, 128], BF16)
    cx.cv = T(S, "cv", [128, 256], BF16)
    cx.SC = T(S, "SC", [128, 256], F32)
    cx.SCb = T(S, "SCb", [128, 256], BF16)
    cx.oc = T(S, "oc", [128, 256], F32)
    cx.rs = T(S, "rs", [128, 256], F32)


def emit_make_T(S, cx, src_bf, src_b, dst, dst_b, t):
    ps, psb = next_ps(cx)
    psv = ps[:].bitcast(BF16)
    for k in range(8):
        TR(S, psv[:, k * 128:(k + 1) * 128], src_bf[:, k * 128:(k + 1) * 128], CB(cx, "ident"),
           [src_b, cx.cstb], [psb])
    CP(S, "dve", dst[:, :, t * 128:(t + 1) * 128],
       psv[:, 0:1024].rearrange("p (k c) -> p k c", k=8), [psb], [dst_b])


def emit_xt(S, cx, t):
    CP(S, "act", cx.xb16[:], cx.X[:, t, :], [cx.Xb[t]], [cx.xb16.b])
    emit_make_T(S, cx, cx.xb16, cx.xb16.b, cx.XT, cx.XTb[t], t)


def emit_ln(S, cx, src, srcb, dst, dstb, n, g, gb, b, bb, st=None):
    st = cx.stat if st is None else st
    stb = st.b
    junk, junkb = cx.junk, cx.junk.b
    MEMSET(S, "dve", st[:, 0:2], 0.0, [stb])
    ACT(S, junk[:, 0:n], src[:, 0:n], AF.Identity, [srcb], [junkb, stb], accum=st[:, 0:1])
    yield
    ACT(S, junk[:, 0:n], src[:, 0:n], AF.Square, [srcb], [junkb, stb], accum=st[:, 1:2])
    yield
    TS(S, "dve", st[:, 2:4], st[:, 0:2], 1.0 / n, ALU.mult, [stb], [stb])
    yield
    TT(S, "dve", st[:, 4:5], st[:, 2:3], st[:, 2:3], ALU.mult, [stb], [stb])
    yield
    TT(S, "dve", st[:, 5:6], st[:, 3:4], st[:, 4:5], ALU.subtract, [stb], [stb])
    yield
    ACT(S, st[:, 7:8], st[:, 5:6], AF.Ln, [stb], [stb], bias=EPS)
    yield
    ACT(S, st[:, 6:7], st[:, 7:8], AF.Exp, [stb], [stb], scale=-0.5)
    yield
    TS(S, "dve", src[:, 0:n], src[:, 0:n], st[:, 2:3], ALU.subtract, [srcb, stb], [srcb],
       s2=st[:, 6:7], op1=ALU.mult)
    yield
    TT(S, "pool", src[:, 0:n], src[:, 0:n], g, ALU.mult, [srcb, gb], [srcb])
    yield
    TT(S, "pool", dst, src[:, 0:n], b, ALU.add, [srcb, bb], [dstb])
    yield


def emit_res_ln_xt(S, cx, t, ybanks):
    Xt = cx.X[:, t, :]
    for h in range(2):
        bi = ybanks[h]
        STT(S, "dve", cx.X[:, t, h * 512:(h + 1) * 512], cx.X[:, t, h * 512:(h + 1) * 512], ALPHA,
            cx.ps[bi][:], ALU.mult, ALU.add, [cx.Xb[t], cx.psb[bi]], [cx.Xb[t]])
        yield
    yield from emit_ln(S, cx, Xt, cx.Xb[t], Xt, cx.Xb[t], D, cx.lng[:], cx.lng.b, cx.lnb[:], cx.lnb.b,
                       st=cx.stat4[t])
    emit_xt(S, cx, t)
    yield


def run_rr(gens):
    gens = list(gens)
    while gens:
        for g_ in list(gens):
            try:
                next(g_)
            except StopIteration:
                gens.remove(g_)


def emit_load_ln(S, cx, g_ap, b_ap):
    DMA(S, "sp", cx.lng[:], g_ap.partition_broadcast(128), [], [cx.lng.b], cx.lng.b)
    DMA(S, "sp", cx.lnb[:], b_ap.partition_broadcast(128), [], [cx.lnb.b], cx.lnb.b)


def emit_ffn_ln(S, cx, w1, w3, w2, g_ap, b_ap, tag):
    emit_load_ln(S, cx, g_ap, b_ap)
    w1v = w1.rearrange("(kc kp) f -> kp kc f", kp=128)
    w3v = w3.rearrange("(kc kp) f -> kp kc f", kp=128)
    w2v = w2.rearrange("(fc fp) d -> fp fc d", fp=128)
    for f in range(NF):
        wb = cx.w13[f % 3]

        def _cast13(f=f, wb=wb):
            S.op("pool", lambda e: [
                e.dma_start(out=wb[:, 0], in_=w1v[:, :, f * 128:(f + 1) * 128]),
                e.dma_start(out=wb[:, 1], in_=w3v[:, :, f * 128:(f + 1) * 128])],
                writes=[wb.b], dma=wb.b, ndma=2)
        WLOAD(S, cx, "%s_w13_%d" % (tag, f), wb, wb[:].rearrange("p a k c -> p (a k c)"), _cast13)
        pa, pab = next_ps(cx)
        pb, pbb = next_ps(cx)
        rd = [wb.b] + cx.XTb
        for k in range(8):
            MM(S, pa[:], wb[:, 0, k, :], cx.XT[:, k, :], k == 0, k == 7, rd, [pab])
        for k in range(8):
            MM(S, pb[:], wb[:, 1, k, :], cx.XT[:, k, :], k == 0, k == 7, rd, [pbb])
        sa = cx.sa[f % 2]
        ACT(S, sa[:], pa[:], AF.Silu, [pab], [sa.b])
        STT(S, "dve", cx.gT[:, f, :], sa[:], 0.5, pb[:], ALU.mult, ALU.mult, [sa.b, pbb],
            [cx.gTb[f], cx.wt[0].b, cx.wt[1].b] + (cx.gt_alias if f >= 16 else []))
    for f in range(NF):
        wb = cx.w2t[f % 3]
        WLOAD(S, cx, "%s_w2_%d" % (tag, f), wb, wb[:],
              lambda f=f, wb=wb: DMA(S, "pool", wb[:], w2v[:, f, :], [], [wb.b], wb.b))
        for j in range(4):
            for h in range(2):
                bi = j * 2 + h
                MM(S, cx.ps[bi][:], cx.gT[:, f, j * 128:(j + 1) * 128], wb[:, h * 512:(h + 1) * 512],
                   f == 0, f == NF - 1, [wb.b, cx.gTb[f]], [cx.psb[bi]])
    cx.psi = 0
    run_rr([emit_res_ln_xt(S, cx, j, (j * 2, j * 2 + 1)) for j in range(4)])


C_AU, C_AV, C_BQ, C_BK, C_BV, C_BZ, C_BS, C_CQ, C_CK, C_CV, C_CR, C_CG = (
    0, 256, 512, 1024, 1536, 2048, 2560, 2576, 2704, 2832, 3088, 3344)
GELU_C = 1.5957691216057308


def emit_layer_setup(S, cx, P):
    DMA(S, "sp", cx.sgg[:], P["sgu_ln_g"].partition_broadcast(128), [], [cx.sgg.b], cx.sgg.b)
    DMA(S, "sp", cx.sgb[:], P["sgu_ln_b"].partition_broadcast(128), [], [cx.sgb.b], cx.sgb.b)
    DMA(S, "sp", cx.sbT[:], P["sgu_bT"], [], [cx.sbT.b], cx.sbT.b)
    DMA(S, "sp", cx.wstg[:], P["sgu_wT"].rearrange("h j i -> j h i"), [], [cx.wstg.b], cx.wstg.b)
    TT(S, "dve", cx.WmT[:], cx.wstg[:], bc4(CF(cx, "U")), ALU.mult, [cx.wstg.b, cx.cstb], [cx.WmT.b])
    DMA(S, "sp", cx.cw[:], P["conv_wT"].rearrange("(c p) i -> p c i", p=128), [], [cx.cw.b], cx.cw.b)
    DMA(S, "sp", cx.dtb[:], P["dt_bias"].partition_broadcast(128), [], [cx.dtb.b], cx.dtb.b)
    DMA(S, "sp", cx.nega[:], P["a_log"].partition_broadcast(128), [], [cx.nega.b], cx.nega.b)
    ACT(S, cx.nega[:], cx.nega[:], AF.Exp, [cx.nega.b], [cx.nega.b])
    TS(S, "dve", cx.nega[:], cx.nega[:], -1.0, ALU.mult, [cx.nega.b], [cx.nega.b])
    DMA(S, "sp", cx.gng[:], P["gdn_norm_g"].partition_broadcast(128), [], [cx.gng.b], cx.gng.b)
    DMA(S, "sp", cx.cng[:], P["gla_norm_g"].partition_broadcast(128), [], [cx.cng.b], cx.cng.b)
    DMA(S, "sp", cx.gup[:], P["gate_up"], [], [cx.gup.b], cx.gup.b)
    DMA(S, "sp", cx.gbb[:], P["gate_b"].partition_broadcast(128), [], [cx.gbb.b], cx.gbb.b)
    CP(S, "dve", cx.gupb[:], cx.gup[:], [cx.gup.b], [cx.gupb.b])
    MEMSET(S, "dve", cx.halo[:], 0.0, [cx.halo.b])
    for c in range(4):
        MEMSET(S, "dve", cx.SB[c][:], 0.0, [cx.SB[c].b])
        MEMSET(S, "dve", cx.SBb[c][:], 0.0, [cx.SBb[c].b])
    MEMSET(S, "dve", cx.SC[:], 0.0, [cx.SC.b])
    MEMSET(S, "dve", cx.SCb[:], 0.0, [cx.SCb.b])


def proj_fm(S, cx, winv, col0, ncols, wf):
    if ncols == 128:
        WLOAD(S, cx, "wf_%d" % col0, wf, wf[:].rearrange("p k c -> p (k c)"),
              lambda: DMA(S, "pool", wf[:, :, 0:ncols], winv[:, :, col0:col0 + ncols], [], [wf.b], wf.b))
    else:
        DMA(S, "pool", wf[:, :, 0:ncols], winv[:, :, col0:col0 + ncols], [], [wf.b], wf.b)
    ps, psb = next_ps(cx)
    for k in range(8):
        MM(S, ps[0:ncols, :], wf[:, k, 0:ncols], cx.XT[:, k, :], k == 0, k == 7, [wf.b] + cx.XTb, [psb])
    return ps, psb


def emit_rms_gate(S, cx, o, ob, nh, gtile, gate_ap, gate_b, dst, dstb, sq, sm=None):
    n = nh * 64
    sm = cx.sm if sm is None else sm
    TT(S, "dve", sq[:, 0:n], o[:, 0:n], o[:, 0:n], ALU.mult, [ob], [sq.b])
    S.op("dve", lambda e: e.tensor_reduce(out=sm[:, 64:64 + nh],
                                          in_=sq[:, 0:n].rearrange("p (h d) -> p h d", d=64),
                                          axis=AX.X, op=ALU.add), reads=[sq.b], writes=[sm.b])
    ACT(S, sm[:, 72:72 + nh], sm[:, 64:64 + nh], AF.Ln, [sm.b], [sm.b], scale=1.0 / 64, bias=EPS)
    ACT(S, sm[:, 80:80 + nh], sm[:, 72:72 + nh], AF.Exp, [sm.b], [sm.b], scale=-0.5)
    ov = o[:, 0:n].rearrange("p (h d) -> p h d", d=64)
    TT(S, "dve", ov, ov, bcl(sm[:, 80:80 + nh], 64), ALU.mult, [ob, sm.b], [ob])
    TT(S, "dve", ov, ov, gtile[:].unsqueeze(1).to_broadcast([128, nh, 64]), ALU.mult, [ob, gtile.b], [ob])
    TT(S, "dve", dst, o[:, 0:n], gate_ap, ALU.mult, [ob, gate_b], [dstb])


def emit_mixer_group(S, cx, P):
    winv = P["w_in"].rearrange("(kc kp) f -> kp kc f", kp=128)
    U = CF(cx, "U")
    import os as _os
    STG = _os.environ.get("MIX_STAGES", "Bfm,Cfm,A,C,B").split(",")
    S.op("dve", lambda e: e.memset(cx.stat[:, 0:1], 0.0), writes=[cx.stat.b] + cx.gTb[16:] + cx.gt_alias)
    for ci in range(12 if "Bfm" in STG else 0):
        wf = cx.wf[ci % 3]
        ps, psb = proj_fm(S, cx, winv, C_BQ + 128 * ci, 128, wf)
        cit = cx.ci[ci % 2]
        CP(S, "pool", cit[:, 0:3], cx.halo[:, ci, 0:3], [cx.halo.b], [cit.b])
        CP(S, "act", cit[:, 3:515], ps[:], [psb], [cit.b])
        CP(S, "pool", cx.halo[:, ci, 0:3], cit[:, 512:515], [cit.b], [cx.halo.b])
        cy = cx.cy[ci % 2]
        ACT(S, cy[:], ps[:], AF.Copy, [psb, cx.cw.b], [cy.b], scale=cx.cw[:, ci, 3:4])
        for i in range(0, 3):
            STT(S, "dve", cy[:], cit[:, i:i + 512], cx.cw[:, ci, i:i + 1], cy[:], ALU.mult, ALU.add,
                [cit.b, cx.cw.b, cy.b], [cy.b])
        cs = cx.cs[ci % 2]
        ACT(S, cs[:], cy[:], AF.Silu, [cy.b], [cs.b])
        c = ci % 4
        if ci < 8:
            sq = cx.sq[ci % 2]
            ACT(S, sq[:], cs[:], AF.Square, [cs.b], [sq.b])
            ps2, ps2b = next_ps(cx)
            MM(S, ps2[:], CB(cx, "bm2"), sq[:], True, True, [sq.b, cx.cstb], [ps2b])
            ACT(S, cx.rn[:], ps2[:], AF.Ln, [ps2b], [cx.rn.b], bias=1e-6)
            ACT(S, cx.rn[:], cx.rn[:], AF.Exp, [cx.rn.b], [cx.rn.b], scale=-0.5)
            if ci < 4:
                STT(S, "dve", cx.qT[:, c, :], cs[:], 0.125, cx.rn[:], ALU.mult, ALU.mult,
                    [cs.b, cx.rn.b], [cx.qTb[c]])
            else:
                TT(S, "dve", cx.kT[:, c, :], cs[:], cx.rn[:], ALU.mult, [cs.b, cx.rn.b], [cx.kTb[c]])
        else:
            CP(S, "act", cx.vT[:, c, :], cs[:], [cs.b], [cx.vTb[c]])
    if "Cfm" not in STG:
        return
    ps, psb = proj_fm(S, cx, winv, C_CQ, 128, cx.wf[0])
    CP(S, "act", cx.cqT[:], ps[:], [psb], [cx.cqT.b])
    ps, psb = proj_fm(S, cx, winv, C_CK, 128, cx.wf[1])
    CP(S, "act", cx.ckT[:], ps[:], [psb], [cx.ckT.b])
    ps, psb = proj_fm(S, cx, winv, C_CG, 16, cx.wf[2])
    CP(S, "act", cx.cgTb[:], ps[0:16, :], [psb], [cx.cgTb.b])
    wA, wZ, wC = cx.wt
    S.op("dve", lambda e: e.memset(cx.stat[:, 0:1], 0.0), writes=[cx.stat.b, wA.b, wZ.b] + cx.gTb[0:16])
    for key_, wt_, c0_ in (("wA", wA, C_AU), ("wZ", wZ, C_BZ), ("wC", wC, C_CV)):
        WLOAD(S, cx, key_, wt_, wt_[:].rearrange("p k c -> p (k c)"),
              lambda wt_=wt_, c0_=c0_: DMA(S, "pool", wt_[:], winv[:, :, c0_:c0_ + 512], [], [wt_.b], wt_.b))
    wS = cx.wf[0]
    DMA(S, "pool", wS[:, :, 0:16], winv[:, :, C_BS:C_BS + 16], [], [wS.b], wS.b)
    for j in range(4):
        ts = slice(j * 128, (j + 1) * 128)
        extra = []
        if "A" in STG:
            extra.append(emit_mixer_A(S, cx, j, ts, wA))
        if "C" in STG:
            extra.append(emit_mixer_C(S, cx, j, ts, wC))
        if "A" in STG and "C" in STG and "B" in STG:
            for _ in range(5):
                next(extra[0])
            for _ in range(2):
                next(extra[1])
        if "B" in STG:
            emit_mixer_B(S, cx, j, ts, wZ, wS, extra)
        else:
            for g_ in extra:
                for _ in g_:
                    pass


def proj_tm(S, cx, ts, wt, ncols):
    ps, psb = next_ps(cx)
    for k in range(8):
        MM(S, ps[:, 0:ncols], cx.XT[:, k, ts], wt[:, k, 0:ncols], k == 0, k == 7, [wt.b] + cx.XTb, [psb])
    return ps, psb


def emit_mixer_A(S, cx, j, ts, wA):
    ps, psb = proj_tm(S, cx, ts, wA, 512)
    ga, t1 = cx.ga, cx.gt1
    ACT(S, ga[:], ps[:], AF.Copy, [psb], [ga.b], scale=0.5)
    yield
    TT(S, "dve", t1[:], ga[:], ga[:], ALU.mult, [ga.b], [t1.b])
    yield
    TS(S, "dve", t1[:], t1[:], 4 * 0.044715, ALU.mult, [t1.b], [t1.b], s2=1.0, op1=ALU.add)
    yield
    TT(S, "dve", t1[:], t1[:], ga[:], ALU.mult, [t1.b, ga.b], [t1.b])
    yield
    ACT(S, t1[:], t1[:], AF.Tanh, [t1.b], [t1.b], scale=GELU_C)
    yield
    STT(S, "dve", ga[:], t1[:], 1.0, ga[:], ALU.add, ALU.mult, [ga.b, t1.b], [ga.b])
    yield
    yield from emit_ln(S, cx, ga[:, 256:512], ga.b, cx.vln[:], cx.vln.b, 256, cx.sgg[:], cx.sgg.b,
                       cx.sgb[:], cx.sgb.b)
    pz, pzb = next_ps(cx)
    for h in range(4):
        MM(S, pz[:, h * 64:(h + 1) * 64], cx.WmT[:, h, :], cx.vln[:, h * 64:(h + 1) * 64], True, True,
           [cx.WmT.b, cx.vln.b], [pzb])
    for h in range(4):
        STT(S, "dve", cx.MIX[:, j, h * 64:(h + 1) * 64], pz[:, h * 64:(h + 1) * 64], cx.sbT[:, h:h + 1],
            ga[:, h * 64:(h + 1) * 64], ALU.add, ALU.mult, [pzb, cx.sbT.b, ga.b], [cx.MIXb[j]])


def emit_mixer_C(S, cx, j, ts, wC):
    psC, psCb = proj_tm(S, cx, ts, wC, 512)
    CP(S, "act", cx.cv[:], psC[:, 0:256], [psCb], [cx.cv.b])
    yield
    ACT(S, cx.rs[:], psC[:, 256:512], AF.Silu, [psCb], [cx.rs.b])
    yield
    pl, plb = next_ps(cx)
    MM(S, pl[:, 0:128], cx.cgTb[0:16, ts], cx.gupb[0:16, :], True, True, [cx.cgTb.b, cx.gupb.b], [plb])
    TT(S, "dve", cx.la[:], pl[:, 0:128], cx.gbb[:], ALU.add, [plb, cx.gbb.b], [cx.la.b])
    yield
    ACT(S, cx.la[:], cx.la[:], AF.Exp, [cx.la.b], [cx.la.b], scale=-1.0)
    yield
    ACT(S, cx.la[:], cx.la[:], AF.Ln, [cx.la.b], [cx.la.b], bias=1.0)
    yield
    pb, pbb = next_ps(cx)
    CP(S, "dve", cx.lah[:], cx.la[:], [cx.la.b], [cx.lah.b])
    yield
    TT(S, "dve", cx.lal[:], cx.la[:], cx.lah[:], ALU.subtract, [cx.la.b, cx.lah.b], [cx.lal.b])
    yield
    MM(S, pb[:, 0:128], cx.lah[:], CB(cx, "Un16"), True, False, [cx.lah.b, cx.cstb], [pbb])
    MM(S, pb[:, 0:128], cx.lal[:], CB(cx, "Un16"), False, True, [cx.lal.b, cx.cstb], [pbb])
    ACT(S, cx.eb[:], pb[:, 0:128], AF.Exp, [pbb], [cx.eb.b])
    yield
    ACT(S, cx.enb[:], pb[:, 0:128], AF.Exp, [pbb], [cx.enb.b], scale=-1.0)
    yield
    STT(S, "dve", cx.cqd[:], cx.cqT[:, ts], 32.0 ** -0.5, cx.eb[:], ALU.mult, ALU.mult,
        [cx.cqT.b, cx.eb.b], [cx.cqd.b])
    yield
    TT(S, "dve", cx.ckd[:], cx.ckT[:, ts], cx.enb[:], ALU.mult, [cx.ckT.b, cx.enb.b], [cx.ckd.b])
    yield
    TS(S, "dve", cx.ckdecT[:], cx.ckd[:], cx.eb[:, 127:128], ALU.mult, [cx.ckd.b, cx.eb.b], [cx.ckdecT.b])
    yield
    pt, ptb = next_ps(cx)
    ptv = pt[:].bitcast(BF16)
    TR(S, ptv[:, 0:128], cx.ckdecT[:], CB(cx, "ident"), [cx.ckdecT.b, cx.cstb], [ptb])
    CP(S, "act", cx.ckdec[:], ptv[:, 0:128], [ptb], [cx.ckdec.b])
    yield
    for h in range(4):
        TS(S, "dve", cx.ckm[:, h, :], cx.ckd[:], CF(cx, "hm")[:, h:h + 1], ALU.mult,
           [cx.ckd.b, cx.cstb], [cx.ckm.b])
    pp, ppb = next_ps(cx)
    for h in range(4):
        MM(S, pp[:, h * 128:(h + 1) * 128], cx.ckm[:, h, :], cx.cqd[:], True, True,
           [cx.ckm.b, cx.cqd.b], [ppb])
    TT(S, "dve", cx.cpT[:], pp[:].rearrange("p (h c) -> p h c", h=4), bc4(CF(cx, "U")), ALU.mult,
       [ppb, cx.cstb], [cx.cpT.b])
    yield
    po, pob = next_ps(cx)
    for h in range(4):
        hs = slice(h * 64, (h + 1) * 64)
        MM(S, po[:, hs], cx.cqd[:], cx.SCb[:, hs], True, False, [cx.cqd.b, cx.SCb.b], [pob])
        MM(S, po[:, hs], cx.cpT[:, h, :], cx.cv[:, hs], False, True, [cx.cpT.b, cx.cv.b], [pob])
    CP(S, "act", cx.oc[:], po[:, 0:256], [pob], [cx.oc.b])
    yield
    pS, pSb = next_ps(cx)
    MM(S, pS[:, 0:256], cx.ckdec[:], cx.cv[:], True, True, [cx.ckdec.b, cx.cv.b], [pSb])
    st = cx.stmp[0]
    TT(S, "dve", st[:], pS[:, 0:256], CF(cx, "bmC"), ALU.mult, [pSb, cx.cstb], [st.b])
    yield
    STT(S, "dve", cx.SC[:], cx.SC[:], cx.eb[:, 127:128], st[:], ALU.mult, ALU.add,
        [cx.SC.b, cx.eb.b, st.b], [cx.SC.b])
    yield
    CP(S, "act", cx.SCb[:], cx.SC[:], [cx.SC.b], [cx.SCb.b])
    yield
    emit_rms_gate(S, cx, cx.oc, cx.oc.b, 4, cx.cng, cx.rs[:], cx.rs.b, cx.MIX[:, j, 768:1024], cx.MIXb[j],
                  cx.osqc, cx.sm2)
    yield


def emit_mixer_B(S, cx, j, ts, wZ, wS, extra=()):
    sm = cx.sm
    cst = cx.cstb
    pZ, pZb = proj_tm(S, cx, ts, wZ, 512)
    ACT(S, cx.zs[:], pZ[:], AF.Silu, [pZb], [cx.zs.b])
    p16, p16b = proj_tm(S, cx, ts, wS, 16)
    CP(S, "act", sm[:, 0:16], p16[:, 0:16], [p16b], [sm.b])
    ACT(S, sm[:, 0:8], sm[:, 0:8], AF.Tanh, [sm.b], [sm.b], scale=0.5)
    TS(S, "dve", sm[:, 0:8], sm[:, 0:8], 0.5, ALU.mult, [sm.b], [sm.b], s2=0.5, op1=ALU.add)
    TT(S, "dve", sm[:, 8:16], sm[:, 8:16], cx.dtb[:], ALU.add, [sm.b, cx.dtb.b], [sm.b])
    ACT(S, sm[:, 8:16], sm[:, 8:16], AF.Exp, [sm.b], [sm.b])
    ACT(S, sm[:, 8:16], sm[:, 8:16], AF.Ln, [sm.b], [sm.b], bias=1.0)
    TT(S, "dve", sm[:, 8:16], sm[:, 8:16], cx.nega[:], ALU.mult, [sm.b, cx.nega.b], [sm.b])
    pg, pgb = next_ps(cx)
    smb = cx.smb
    CP(S, "dve", smb[:, 0:8], sm[:, 8:16], [sm.b], [smb.b])
    TT(S, "dve", smb[:, 8:16], sm[:, 8:16], smb[:, 0:8], ALU.subtract, [sm.b, smb.b], [smb.b])
    MM(S, pg[:, 0:8], CB(cx, "U"), smb[:, 0:8], True, False, [smb.b, cst], [pgb])
    MM(S, pg[:, 0:8], CB(cx, "U"), smb[:, 8:16], False, True, [smb.b, cst], [pgb])
    MM(S, pg[:, 8:16], CB(cx, "ones"), smb[:, 0:8], True, False, [smb.b, cst], [pgb])
    MM(S, pg[:, 8:16], CB(cx, "ones"), smb[:, 8:16], False, True, [smb.b, cst], [pgb])
    CP(S, "dve", sm[:, 16:32], pg[:, 0:16], [pgb], [sm.b])
    ACT(S, sm[:, 32:40], sm[:, 16:24], AF.Exp, [sm.b], [sm.b])
    TT(S, "dve", sm[:, 40:48], sm[:, 24:32], sm[:, 16:24], ALU.subtract, [sm.b], [sm.b])
    ACT(S, sm[:, 40:48], sm[:, 40:48], AF.Exp, [sm.b], [sm.b])
    TT(S, "dve", sm[:, 48:56], sm[:, 0:8], sm[:, 32:40], ALU.mult, [sm.b], [sm.b])
    TT(S, "dve", cx.lgb[:], bcl(smb[:, 0:8], 128), bc4(CB(cx, "ones"), 8), ALU.mult, [smb.b, cst], [cx.lgb.b])
    TT(S, "dve", cx.lgl[:], bcl(smb[:, 8:16], 128), bc4(CB(cx, "ones"), 8), ALU.mult, [smb.b, cst], [cx.lgl.b])
    import os as _os
    CUT = int(_os.environ.get("MIXB_CUT", "99"))
    if CUT <= 1:
        return
    pk, pkb = next_ps(cx)
    pkv = pk[:].bitcast(BF16)
    for c in range(4):
        TR(S, pkv[:, c * 128:(c + 1) * 128], cx.kT[:, c, ts], CB(cx, "ident"), [cx.kTb[c], cst], [pkb])
    CP(S, "act", cx.ktok[:], pkv[:, 0:512], [pkb], [cx.ktok.b])
    pv, pvb = next_ps(cx)
    pvv = pv[:].bitcast(BF16)
    for c in range(4):
        TR(S, pvv[:, c * 128:(c + 1) * 128], cx.vT[:, c, ts], CB(cx, "ident"), [cx.vTb[c], cst], [pvb])
    CP(S, "act", cx.vtok[:], pvv[:, 0:512], [pvb], [cx.vtok.b])

    def hv(t):
        return t[:].rearrange("p (h d) -> p h d", d=64)

    TT(S, "dve", hv(cx.bv), hv(cx.vtok), bcl(sm[:, 0:8], 64), ALU.mult, [cx.vtok.b, sm.b], [cx.bv.b])
    TT(S, "dve", hv(cx.bek), hv(cx.ktok), bcl(sm[:, 48:56], 64), ALU.mult, [cx.ktok.b, sm.b], [cx.bek.b])
    TT(S, "dve", hv(cx.kdec), hv(cx.ktok), bcl(sm[:, 40:48], 64), ALU.mult, [cx.ktok.b, sm.b], [cx.kdec.b])

    for par in range(2):
        TS(S, "dve", cx.kTm[:, :, par, :], cx.kT[:, :, ts], CF(cx, "pm")[:, par:par + 1], ALU.mult,
           cx.kTb + [cst], [cx.kTm.b])
        TT(S, "dve", cx.bekm[:, :, par, :], cx.bek[:].rearrange("p (c x) -> p c x", c=4),
           bc4(CB(cx, "cm")[:, par * 128:(par + 1) * 128]), ALU.mult, [cx.bek.b, cst], [cx.bekm.b])
    TT(S, "dve", cx.lgp[:], bcl(smb[:, 0:8], 64), bc4(CB(cx, "ones")[:, 0:64], 8), ALU.mult, [smb.b, cst], [cx.lgp.b])
    TT(S, "dve", cx.lgpl[:], bcl(smb[:, 8:16], 64), bc4(CB(cx, "ones")[:, 0:64], 8), ALU.mult, [smb.b, cst], [cx.lgpl.b])
    pGp, pGpb = next_ps(cx)
    lgpv = cx.lgp[:].rearrange("p (c q) d -> p c (q d)", q=2)
    lgplv = cx.lgpl[:].rearrange("p (c q) d -> p c (q d)", q=2)
    for c in range(4):
        MM(S, pGp[:, c * 128:(c + 1) * 128], lgpv[:, c, :], CB(cx, "U"), True, False, [cx.lgp.b, cst], [pGpb])
        MM(S, pGp[:, c * 128:(c + 1) * 128], lgplv[:, c, :], CB(cx, "U"), False, True, [cx.lgpl.b, cst], [pGpb])
    ACT(S, cx.EG[:], pGp[:].rearrange("p (c x) -> p c x", c=4), AF.Exp, [pGpb], [cx.EG.b])
    TT(S, "dve", cx.qdT[:], cx.qT[:, :, ts], cx.EG[:], ALU.mult, cx.qTb + [cx.EG.b], [cx.qdT.b])
    CP(S, "dve", cx.gcol[:], cx.EG[:, :, 127], [cx.EG.b], [cx.gcol.b])

    if CUT <= 2:
        return
    def hg_chain(hg, B):
        h0 = 4 * hg

        def kTh(hh):
            h = h0 + hh
            pb = 64 * (h % 2)
            return cx.kT[pb:pb + 64, h // 2, ts], cx.kTb[h // 2]

        def qTh(hh):
            h = h0 + hh
            pb = 64 * (h % 2)
            return cx.qT[pb:pb + 64, h // 2, ts], cx.qTb[h // 2]

        pG, pGb = next_ps(cx)
        pGv = pG[:].rearrange("p (h c) -> p h c", h=4)
        for hh in range(4):
            MM(S, pG[:, hh * 128:(hh + 1) * 128], cx.lgb[:, h0 + hh, :], CB(cx, "U"), True, False,
               [cx.lgb.b, cst], [pGb])
            MM(S, pG[:, hh * 128:(hh + 1) * 128], cx.lgl[:, h0 + hh, :], CB(cx, "U"), False, True,
               [cx.lgl.b, cst], [pGb])
        TT(S, "dve", B.tmpD[:], pGv, bcl(sm[:, 16 + h0:20 + h0], 128), ALU.subtract, [pGb, sm.b], [B.tmpD.b])
        yield
        TS(S, "dve", B.tmpE[:], B.tmpD[:], 0.0, ALU.max, [B.tmpD.b], [B.tmpE.b])
        yield
        ACT(S, B.E[:], B.tmpE[:], AF.Exp, [B.tmpE.b], [B.E.b], scale=-1.0)
        yield
        TS(S, "dve", B.tmpE[:], B.tmpD[:], 0.0, ALU.min, [B.tmpD.b, B.E.b], [B.tmpE.b])
        yield
        ACT(S, B.ET[:], B.tmpE[:], AF.Exp, [B.tmpE.b], [B.ET.b])
        yield
        TT(S, "dve", B.E[:], B.E[:], bc4(CF(cx, "Lstr")), ALU.mult, [B.E.b, cst], [B.E.b])
        yield
        TT(S, "dve", B.ET[:], B.ET[:], bc4(CF(cx, "U")), ALU.mult, [B.ET.b, cst], [B.ET.b])
        yield
        if CUT <= 3:
            return
        pK, pKb = next_ps(cx)
        for hh in range(4):
            h = h0 + hh
            MM(S, pK[:, hh * 128:(hh + 1) * 128], cx.kTm[:, h // 2, h % 2, :], cx.kT[:, h // 2, ts], True, True,
               [cx.kTm.b, cx.kTb[h // 2]], [pKb])
        TT(S, "dve", B.tmpD[:], pK[:].rearrange("p (h c) -> p h c", h=4), B.E[:], ALU.mult,
           [pKb, B.E.b], [B.tmpD.b])
        yield
        TT(S, "dve", B.L[:], B.tmpD[:], bcl(sm[:, h0:h0 + 4], 128), ALU.mult, [B.tmpD.b, sm.b], [B.L.b])
        yield
        pN, pNb = next_ps(cx)
        pNv = pN[:].bitcast(BF16)
        for hh in range(4):
            TR(S, pNv[:, hh * 128:(hh + 1) * 128], B.L[:, hh, :], CB(cx, "ident"), [B.L.b, cst], [pNb])
        CP(S, "act", B.N[:], pNv[:, 0:512].rearrange("p (h c) -> p h c", h=4), [pNb], [B.N.b])
        yield
        if CUT <= 4:
            return
        I4 = bc4(CB(cx, "ident"))
        TT(S, "dve", B.Xa[:], B.L[:], bc4(CB(cx, "m1")), ALU.mult, [B.L.b, cst], [B.Xa.b])
        yield
        STT(S, "dve", B.P[:], B.Xa[:], -1.0, I4, ALU.mult, ALU.add, [B.Xa.b, cst], [B.P.b])
        yield
        TT(S, "dve", B.Xb2[:], B.N[:], bc4(CB(cx, "mT1")), ALU.mult, [B.N.b, cst], [B.Xb2.b])
        yield
        STT(S, "dve", B.Q[:], B.Xb2[:], -1.0, I4, ALU.mult, ALU.add, [B.Xb2.b, cst], [B.Q.b])
        yield
        for b in MB_LIST[1:]:
            last = b == MB_LIST[-1]
            p1, p1b = next_ps(cx)
            for hh in range(4):
                MM(S, p1[:, hh * 128:(hh + 1) * 128], B.N[:, hh, :], B.P[:, hh, :], True, True,
                   [B.N.b, B.P.b], [p1b])
            TT(S, "dve", B.Xa[:], p1[:].rearrange("p (h c) -> p h c", h=4), bc4(CB(cx, "m%d" % b)), ALU.mult,
               [p1b, cst], [B.Xa.b])
            yield
            if not last:
                p2, p2b = next_ps(cx)
                for hh in range(4):
                    MM(S, p2[:, hh * 128:(hh + 1) * 128], B.L[:, hh, :], B.Q[:, hh, :], True, True,
                       [B.L.b, B.Q.b], [p2b])
                TT(S, "dve", B.Xb2[:], p2[:].rearrange("p (h c) -> p h c", h=4), bc4(CB(cx, "mT%d" % b)),
                   ALU.mult, [p2b, cst], [B.Xb2.b])
                yield
            p3, p3b = next_ps(cx)
            for hh in range(4):
                MM(S, p3[:, hh * 128:(hh + 1) * 128], B.Xa[:, hh, :], B.Q[:, hh, :], True, True,
                   [B.Xa.b, B.Q.b], [p3b])
            if not last:
                p4, p4b = next_ps(cx)
                for hh in range(4):
                    MM(S, p4[:, hh * 128:(hh + 1) * 128], B.Xb2[:, hh, :], B.P[:, hh, :], True, True,
                       [B.Xb2.b, B.P.b], [p4b])
            TT(S, "dve", B.Q[:], B.Q[:], p3[:].rearrange("p (h c) -> p h c", h=4), ALU.subtract,
               [B.Q.b, p3b], [B.Q.b])
            yield
            if not last:
                TT(S, "dve", B.P[:], B.P[:], p4[:].rearrange("p (h c) -> p h c", h=4), ALU.subtract,
                   [B.P.b, p4b], [B.P.b])
                yield
        if CUT <= 5:
            return
        pu, pub = next_ps(cx)
        for hh in range(4):
            h = h0 + hh
            MM(S, pu[:, hh * 64:(hh + 1) * 64], B.Q[:, hh, :], cx.bv[:, h * 64:(h + 1) * 64], True, True,
               [B.Q.b, cx.bv.b], [pub])
        CP(S, "act", cx.ub[:, hg * 256:(hg + 1) * 256], pu[:, 0:256], [pub], [cx.ub.b])
        yield
        pw, pwb = next_ps(cx)
        for cc in range(2):
            c = 2 * hg + cc
            for par in range(2):
                MM(S, pw[:, cc * 128:(cc + 1) * 128], cx.bekm[:, c, par, :], B.Q[:, 2 * cc + par, :],
                   par == 0, par == 1, [cx.bekm.b, B.Q.b], [pwb])
        CP(S, "act", cx.wT[:, 2 * hg:2 * hg + 2, :], pw[:, 0:256].rearrange("p (c x) -> p c x", c=2),
           [pwb], [cx.wT.b])
        yield
        pq, pqb = next_ps(cx)
        for hh in range(4):
            h = h0 + hh
            MM(S, pq[:, hh * 128:(hh + 1) * 128], cx.kTm[:, h // 2, h % 2, :], cx.qT[:, h // 2, ts], True, True,
               [cx.kTm.b, cx.qTb[h // 2]], [pqb])
        TT(S, "dve", cx.pT[:, h0:h0 + 4, :], pq[:].rearrange("p (h c) -> p h c", h=4), B.ET[:], ALU.mult,
           [pqb, B.ET.b], [cx.pT.b])
        yield

    gens = [hg_chain(0, cx.BS[0]), hg_chain(1, cx.BS[1])] + list(extra)
    while gens:
        for g_ in list(gens):
            try:
                next(g_)
            except StopIteration:
                gens.remove(g_)
    if CUT <= 6:
        return
    for c in range(4):
        SBc, SBb = cx.SB[c], cx.SBb[c]
        ubf = cx.ubf[c % 2]
        pu2, pu2b = next_ps(cx)
        MM(S, pu2[:, 0:128], cx.wT[:, c, :], SBb[:], True, True, [cx.wT.b, SBb.b], [pu2b])
        TT(S, "dve", ubf[:], cx.ub[:, c * 128:(c + 1) * 128], pu2[:, 0:128], ALU.subtract,
           [cx.ub.b, pu2b], [ubf.b])
        po, pob = next_ps(cx)
        for par in range(2):
            h = 2 * c + par
            hs = slice(par * 64, par * 64 + 64)
            MM(S, po[:, hs], cx.qdT[:, c, :], SBb[:, hs], True, False, [cx.qdT.b, SBb.b], [pob])
            MM(S, po[:, hs], cx.pT[:, h, :], ubf[:, hs], False, True, [cx.pT.b, ubf.b], [pob])
        CP(S, "act", cx.ob[:, c * 128:(c + 1) * 128], po[:, 0:128], [pob], [cx.ob.b])
        pS, pSb = next_ps(cx)
        MM(S, pS[:, 0:128], cx.kdec[:, c * 128:(c + 1) * 128], ubf[:], True, True, [cx.kdec.b, ubf.b], [pSb])
        st = cx.stmp[c % 2]
        TT(S, "dve", st[:, 0:128], pS[:, 0:128], CF(cx, "bm2"), ALU.mult, [pSb, cst], [st.b])
        STT(S, "dve", SBc[:], SBc[:], cx.gcol[:, c:c + 1], st[:, 0:128], ALU.mult, ALU.add,
            [SBc.b, cx.gcol.b, st.b], [SBc.b])
        CP(S, "act", SBb[:], SBc[:], [SBc.b], [SBb.b])
    if CUT <= 7:
        return
    emit_rms_gate(S, cx, cx.ob, cx.ob.b, 8, cx.gng, cx.zs[:], cx.zs.b, cx.MIX[:, j, 256:768], cx.MIXb[j],
                  cx.osq)


def emit_wout_ln(S, cx, P):
    for j in range(4):
        emit_make_T(S, cx, cx.MIX[:, j, :], cx.MIXb[j], cx.XT, cx.XTb[j], j)
    emit_load_ln(S, cx, P["ln2_g"], P["ln2_b"])
    wov = P["w_out"].rearrange("(kc kp) d -> kp kc d", kp=128)
    cx.psi = 0
    for k in range(8):
        wo = cx.wo[k % 3]
        WLOAD(S, cx, "wo_%d" % k, wo, wo[:],
              lambda k=k, wo=wo: DMA(S, "pool", wo[:], wov[:, k, :], [], [wo.b], wo.b))
        for j in range(4):
            for h in range(2):
                bi = j * 2 + h
                MM(S, cx.ps[bi][:], cx.XT[:, k, j * 128:(j + 1) * 128], wo[:, h * 512:(h + 1) * 512],
                   k == 0, k == 7, [wo.b, cx.XTb[j]], [cx.psb[bi]])
    cx.psi = 0
    run_rr([emit_res_ln_xt(S, cx, j, (j * 2, j * 2 + 1)) for j in range(4)])


PNAMES = ["ffn1_w1", "ffn1_w3", "ffn1_w2", "ln1_g", "ln1_b", "w_in", "sgu_ln_g", "sgu_ln_b", "sgu_wT", "sgu_bT",
          "conv_wT", "a_log", "dt_bias", "gdn_norm_g", "gate_up", "gate_b", "gla_norm_g", "w_out", "ln2_g",
          "ln2_b", "ffn2_w1", "ffn2_w3", "ffn2_w2", "ln3_g", "ln3_b"]
PSHAPES = {"ffn1_w1": [D, DFF], "ffn1_w3": [D, DFF], "ffn1_w2": [DFF, D], "ln1_g": [D], "ln1_b": [D],
           "w_in": [D, DIN], "sgu_ln_g": [256], "sgu_ln_b": [256], "sgu_wT": [4, 128, 128], "sgu_bT": [128, 4],
           "conv_wT": [1536, 4], "a_log": [8], "dt_bias": [8], "gdn_norm_g": [64], "gate_up": [16, 128],
           "gate_b": [128], "gla_norm_g": [64], "w_out": [D, D], "ln2_g": [D], "ln2_b": [D],
           "ffn2_w1": [D, DFF], "ffn2_w3": [D, DFF], "ffn2_w2": [DFF, D], "ln3_g": [D], "ln3_b": [D]}


def build_program(T_tok, depth, stop_after=None, dbg=False):
    nc = bass.Bass("TRN2", target_bir_lowering=False)
    x = nc.dram_tensor("x", [T_tok, D], F32, kind="ExternalInput").ap()
    y = nc.dram_tensor("y", [T_tok, D], F32, kind="ExternalOutput").ap()
    cst = nc.dram_tensor("cstf", [128, NCSTF], F32, kind="ExternalInput").ap()
    cstb = nc.dram_tensor("cstb", [128, NCSTB], F32, kind="ExternalInput").ap()
    prm = {}
    for n in PNAMES:
        prm[n] = nc.dram_tensor(n, [depth] + PSHAPES[n], F32, kind="ExternalInput").ap()
    xs = [nc.dram_tensor("xs%d" % i, [T_tok, D], F32, kind="Internal").ap() for i in range(2)]
    NG = T_tok // 512
    with ExitStack() as stack:
        S = Sched(nc, stack)
        cx = Ctx()
        alloc_all(S, nc, cx)
        cx.nc = nc
        cx.scr = {}
        cx.scrb = {}
        cx.first = True
        DMA(S, "sp", cx.CST[:], cst, [], [cx.cstb], cx.cstb)
        cstb2 = S.buf("cstb2")
        DMA(S, "pool", cx.CSTB[:], cstb, [], [cstb2], cstb2)
        S.op("dve", lambda e: e.memset(cx.stat[:, 0:1], 0.0), reads=[cstb2], writes=[cx.cstb, cx.stat.b])
        xsb = [[S.buf("xs%d_%d" % (i, g)) for g in range(NG)] for i in range(2)]
        yb = S.buf("y")
        for l in range(depth):
            P = {n: prm[n][l] for n in PNAMES}
            emit_layer_setup(S, cx, P)
            src = x if l == 0 else xs[(l - 1) % 2]
            dst = y if l == depth - 1 else xs[l % 2]
            srcv = src.rearrange("(g t p) d -> g p t d", p=128, t=4)
            dstv = dst.rearrange("(g t p) d -> g p t d", p=128, t=4)
            for g in range(NG):
                cx.first = (g == 0)
                for t in range(4):
                    rd = [] if l == 0 else [xsb[(l - 1) % 2][g]]
                    DMA(S, "sp", cx.X[:, t, :], srcv[g, :, t, :], rd, [cx.Xb[t]], cx.Xb[t])
                for t in range(4):
                    emit_xt(S, cx, t)
                emit_ffn_ln(S, cx, P["ffn1_w1"], P["ffn1_w3"], P["ffn1_w2"], P["ln1_g"], P["ln1_b"], "f1")
                if stop_after != "ffn1":
                    emit_mixer_group(S, cx, P)
                    if dbg and g == 0 and l == 0:
                        S.dump("mix", cx.MIX[:], cx.MIXb[3], [128, 4, D], BF16)
                    emit_wout_ln(S, cx, P)
                    if stop_after != "mix":
                        emit_ffn_ln(S, cx, P["ffn2_w1"], P["ffn2_w3"], P["ffn2_w2"], P["ln3_g"], P["ln3_b"], "f2")
                for t in range(4):
                    wr = [yb] if l == depth - 1 else [xsb[l % 2][g]]
                    DMA(S, "sp", dstv[g, :, t, :], cx.X[:, t, :], [cx.Xb[t]], wr, cx.Xb[t])
        S.finish([yb] + cx.Xb)
        S.emit()
    return nc


def host_params(inputs, depth=DEPTH):
    f = lambda a: np.ascontiguousarray(np.asarray(a, dtype=np.float32))
    p = {}
    for n in ["ffn1_w1", "ffn1_w3", "ffn1_w2", "ln1_g", "ln1_b", "w_in", "sgu_ln_g", "sgu_ln_b", "w_out",
              "ln2_g", "ln2_b", "ffn2_w1", "ffn2_w3", "ffn2_w2", "ln3_g", "ln3_b"]:
        p[n] = f(inputs[n])[:depth]
    p["sgu_wT"] = f(np.transpose(np.asarray(inputs["sgu_w"]), (0, 1, 3, 2)))[:depth]
    p["sgu_bT"] = f(np.transpose(np.asarray(inputs["sgu_b"]), (0, 2, 1)))[:depth]
    p["conv_wT"] = f(np.transpose(np.asarray(inputs["gdn_conv_w"]), (0, 2, 1)))[:depth]
    p["a_log"] = f(inputs["gdn_a_log"])[:depth]
    p["dt_bias"] = f(inputs["gdn_dt_bias"])[:depth]
    p["gdn_norm_g"] = f(inputs["gdn_norm_g"])[:depth]
    p["gate_up"] = f(inputs["gla_gate_up"])[:depth]
    p["gate_b"] = f(inputs["gla_gate_b"])[:depth]
    p["gla_norm_g"] = f(inputs["gla_norm_g"])[:depth]
    p["cstf"] = CSTF_NP
    p["cstb"] = CSTB_NP
    return p


_PROG = {}


def kernel(**inputs):
    x = np.asarray(inputs["x"], dtype=np.float32)
    B, T_tok, _ = x.shape
    p = host_params(inputs)
    key = (T_tok, DEPTH)
    if key not in _PROG:
        _PROG[key] = build_program(T_tok, DEPTH)
    nc = _PROG[key]
    in_maps = []
    for b in range(B):
        m = dict(p)
        m["x"] = np.ascontiguousarray(x[b])
        in_maps.append(m)
    res = run_bass_kernel_spmd(nc, in_maps, core_ids=list(range(B)))
    return np.stack([res.results[b]["y"] for b in range(B)], axis=0).astype(np.float32)
```
